# Optimizing a Trainium2 kernel written in Bass

```python
import math
import jax, jax.numpy as jnp
from jax import lax
import numpy as np

D_MODEL = 1024
BATCH = 16
SEQ = 4096
DEPTH = 1
DEC_BATCH = 8
DEC_SEQ = 32
PAST_LEN = 4096

CHUNK = 64
N_HEADS = 8
HEAD_DIM = 64
ATTN_WIDTH = N_HEADS * HEAD_DIM
CONV_WIDTH = D_MODEL // 2
CONV_K = 31
D_FF = 2816
FFN_CONV_K = 3
Q_BLOCK = 128
N_BRANCH = 2
N_MOD = 6
DN_ALPHA = (2 * DEPTH) ** 0.25
DN_BETA = (8 * DEPTH) ** -0.25
LN_EPS = 1e-5
FORGET_BIAS_INIT = 3.0
SPLIT_Q = ATTN_WIDTH
SPLIT_K = 2 * ATTN_WIDTH
SPLIT_V = 3 * ATTN_WIDTH
SPLIT_F = SPLIT_V + N_HEADS
SPLIT_GLU = SPLIT_F + 2 * CONV_WIDTH
IN_COLS = SPLIT_GLU + N_BRANCH * D_MODEL

kernel_name = "fox_conformer_convffn_stream_step"


def layer_norm(x, g, b):
    xf = x.astype(jnp.float32)
    mu = jnp.mean(xf, axis=-1, keepdims=True)
    var = jnp.mean(jnp.square(xf - mu), axis=-1, keepdims=True)
    return ((xf - mu) * lax.rsqrt(var + LN_EPS) * g.astype(jnp.float32) + b.astype(jnp.float32)).astype(x.dtype)


def causal_dwconv(x, hist, w, b):
    K, C = w.shape
    xp = jnp.concatenate([hist.astype(x.dtype), x], axis=1)
    y = lax.conv_general_dilated(
        xp, w[:, None, :].astype(x.dtype), window_strides=(1,), padding="VALID",
        dimension_numbers=("NWC", "WIO", "NWC"), feature_group_count=C)
    return y + b.astype(x.dtype), xp[:, -(K - 1):]


def fox_attention(q, k, v, F_q, F_k, q_offset):
    B, T, H, dh = q.shape
    L = k.shape[1]
    qb = min(Q_BLOCK, T)
    nb = T // qb
    k_pos = jnp.arange(L)
    Fk_t = jnp.transpose(F_k, (0, 2, 1))[:, :, None, :]
    scale = HEAD_DIM ** -0.5

    def block(i):
        start = i * qb
        qi = lax.dynamic_slice_in_dim(q, start, qb, axis=1)
        fi = lax.dynamic_slice_in_dim(F_q, start, qb, axis=1)
        q_pos = q_offset + start + jnp.arange(qb)
        s = jnp.einsum('bqhd,bkhd->bhqk', qi, k, preferred_element_type=jnp.float32) * scale
        s = s + jnp.transpose(fi, (0, 2, 1))[..., None] - Fk_t
        mask = k_pos[None, :] <= q_pos[:, None]
        s = jnp.where(mask, s, -jnp.inf)
        p = jax.nn.softmax(s, axis=-1)
        return jnp.einsum('bhqk,bkhd->bqhd', p.astype(v.dtype), v)

    out = lax.map(block, jnp.arange(nb))
    return jnp.moveaxis(out, 0, 1).reshape(B, T, H, dh)


def trunk_layer(x, c, p, past_k, past_v, past_logf, conv_hist, ffn_hist):
    B, T, _ = x.shape
    P = 0 if past_k is None else past_k.shape[1]
    mod = (c @ p['w_ada'] + p['b_ada'])[:, None, :]
    sh1, sc1, g1, sh2, sc2, g2 = jnp.split(mod, N_MOD, axis=-1)

    u = x * (1 + sc1) + sh1
    z = u @ p['w_in']
    q, k, v, f_logit, glu_in, gate_logits = jnp.split(
        z, [SPLIT_Q, SPLIT_K, SPLIT_V, SPLIT_F, SPLIT_GLU], axis=-1)
    q = q.reshape(B, T, N_HEADS, HEAD_DIM)
    k = k.reshape(B, T, N_HEADS, HEAD_DIM)
    v = v.reshape(B, T, N_HEADS, HEAD_DIM)
    logf = jax.nn.log_sigmoid((f_logit + p['b_f']).astype(jnp.float32))
    if past_k is None:
        k_all, v_all, logf_all = k, v, logf
    else:
        k_all = jnp.concatenate([past_k.astype(k.dtype), k], axis=1)
        v_all = jnp.concatenate([past_v.astype(v.dtype), v], axis=1)
        logf_all = jnp.concatenate([past_logf.astype(jnp.float32), logf], axis=1)
    F_all = jnp.cumsum(logf_all, axis=1)
    attn = fox_attention(q, k_all, v_all, F_all[:, P:], F_all, P)
    y_a = attn.reshape(B, T, ATTN_WIDTH) @ p['w_attn_proj']

    glu_a, glu_b = jnp.split(glu_in, 2, axis=-1)
    glu = glu_a * jax.nn.sigmoid(glu_b)
    hc, new_conv = causal_dwconv(glu, conv_hist, p['conv_w'], p['conv_b'])
    hc = jax.nn.silu(layer_norm(hc, p['conv_ln_g'], p['conv_ln_b']))
    y_b = hc @ p['w_conv_proj']

    gate_a, gate_b = jnp.split(gate_logits, N_BRANCH, axis=-1)
    merged = jax.nn.sigmoid(gate_a) * y_a + jax.nn.sigmoid(gate_b) * y_b
    x1 = layer_norm(DN_ALPHA * x + g1 * (merged @ p['w_out']), p['ln1_g'], p['ln1_b'])

    u2 = x1 * (1 + sc2) + sh2
    a2, v2 = jnp.split(u2 @ p['w_up'], 2, axis=-1)
    a2c, new_ffn = causal_dwconv(a2, ffn_hist, p['ffn_conv_w'], p['ffn_conv_b'])
    h = jax.nn.silu(a2c) * v2
    y = layer_norm(DN_ALPHA * x1 + g2 * (h @ p['w_down']), p['ln2_g'], p['ln2_b'])
    return y, k, v, logf, new_conv, new_ffn


def setup_inputs(seed: int = 0) -> dict:
    key = jax.random.key(seed)
    ks = iter(jax.random.split(key, 40))
    f32 = jnp.float32

    def nrm(shape, scale=1.0):
        return jax.random.normal(next(ks), shape, f32) * scale

    L = DEPTH
    d = {}
    d['x_prompt'] = nrm((BATCH, SEQ, D_MODEL))
    d['x_sample'] = nrm((DEC_BATCH, DEC_SEQ, D_MODEL))
    d['c_prompt'] = nrm((BATCH, D_MODEL))
    d['c_sample'] = nrm((DEC_BATCH, D_MODEL))
    d['cache_k'] = nrm((L, DEC_BATCH, PAST_LEN, N_HEADS, HEAD_DIM))
    d['cache_v'] = nrm((L, DEC_BATCH, PAST_LEN, N_HEADS, HEAD_DIM))
    d['cache_logf'] = jax.nn.log_sigmoid(FORGET_BIAS_INIT + nrm((L, DEC_BATCH, PAST_LEN, N_HEADS)))
    d['state_conv'] = nrm((L, DEC_BATCH, CONV_K - 1, CONV_WIDTH), 0.5)
    d['state_ffn_conv'] = nrm((L, DEC_BATCH, FFN_CONV_K - 1, D_FF))
    d['w_ada'] = nrm((L, D_MODEL, N_MOD * D_MODEL), 0.5 * D_MODEL ** -0.5)
    d['b_ada'] = nrm((L, N_MOD * D_MODEL), 0.02)
    d['w_in'] = nrm((L, D_MODEL, IN_COLS), D_MODEL ** -0.5)
    d['b_f'] = FORGET_BIAS_INIT + nrm((L, N_HEADS), 0.5)
    d['conv_w'] = nrm((L, CONV_K, CONV_WIDTH), CONV_K ** -0.5)
    d['conv_b'] = nrm((L, CONV_WIDTH), 0.02)
    d['conv_ln_g'] = 1.0 + nrm((L, CONV_WIDTH), 0.05)
    d['conv_ln_b'] = nrm((L, CONV_WIDTH), 0.02)
    d['w_attn_proj'] = nrm((L, ATTN_WIDTH, D_MODEL), DN_BETA * ATTN_WIDTH ** -0.5)
    d['w_conv_proj'] = nrm((L, CONV_WIDTH, D_MODEL), DN_BETA * CONV_WIDTH ** -0.5)
    d['w_out'] = nrm((L, D_MODEL, D_MODEL), DN_BETA * D_MODEL ** -0.5)
    d['ln1_g'] = 1.0 + nrm((L, D_MODEL), 0.05)
    d['ln1_b'] = nrm((L, D_MODEL), 0.02)
    d['w_up'] = nrm((L, D_MODEL, 2 * D_FF), D_MODEL ** -0.5)
    d['ffn_conv_w'] = nrm((L, FFN_CONV_K, D_FF), FFN_CONV_K ** -0.5)
    d['ffn_conv_b'] = nrm((L, D_FF), 0.02)
    d['w_down'] = nrm((L, D_FF, D_MODEL), DN_BETA * D_FF ** -0.5)
    d['ln2_g'] = 1.0 + nrm((L, D_MODEL), 0.05)
    d['ln2_b'] = nrm((L, D_MODEL), 0.02)
    return d


def reference(x_prompt, x_sample, c_prompt, c_sample, cache_k, cache_v, cache_logf,
              state_conv, state_ffn_conv, w_ada, b_ada, w_in, b_f, conv_w, conv_b,
              conv_ln_g, conv_ln_b, w_attn_proj, w_conv_proj, w_out, ln1_g, ln1_b,
              w_up, ffn_conv_w, ffn_conv_b, w_down, ln2_g, ln2_b):
    y_p, y_s = x_prompt, x_sample
    Bp = x_prompt.shape[0]
    kp_l, vp_l, fp_l, cp_l, ffp_l = [], [], [], [], []
    ks_l, vs_l, fs_l, cs_l, ffs_l = [], [], [], [], []
    for l in range(DEPTH):
        p = dict(w_ada=w_ada[l], b_ada=b_ada[l], w_in=w_in[l], b_f=b_f[l],
                 conv_w=conv_w[l], conv_b=conv_b[l], conv_ln_g=conv_ln_g[l], conv_ln_b=conv_ln_b[l],
                 w_attn_proj=w_attn_proj[l], w_conv_proj=w_conv_proj[l], w_out=w_out[l],
                 ln1_g=ln1_g[l], ln1_b=ln1_b[l], w_up=w_up[l], ffn_conv_w=ffn_conv_w[l],
                 ffn_conv_b=ffn_conv_b[l], w_down=w_down[l], ln2_g=ln2_g[l], ln2_b=ln2_b[l])
        zc = jnp.zeros((Bp, CONV_K - 1, CONV_WIDTH), x_prompt.dtype)
        zf = jnp.zeros((Bp, FFN_CONV_K - 1, D_FF), x_prompt.dtype)
        y_p, kp, vp, fp, cp, ffp = trunk_layer(y_p, c_prompt, p, None, None, None, zc, zf)
        y_s, ks_, vs_, fs_, cs_, ffs_ = trunk_layer(
            y_s, c_sample, p, cache_k[l], cache_v[l], cache_logf[l], state_conv[l], state_ffn_conv[l])
        kp_l.append(kp); vp_l.append(vp); fp_l.append(fp); cp_l.append(cp); ffp_l.append(ffp)
        ks_l.append(ks_); vs_l.append(vs_); fs_l.append(fs_); cs_l.append(cs_); ffs_l.append(ffs_)
    k_prompt = jnp.stack(kp_l)
    v_prompt = jnp.stack(vp_l)
    logf_prompt = jnp.stack(fp_l)
    conv_prompt = jnp.stack(cp_l)
    ffn_conv_prompt = jnp.stack(ffp_l)
    k_sample = jnp.stack(ks_l)
    v_sample = jnp.stack(vs_l)
    logf_sample = jnp.stack(fs_l)
    conv_sample = jnp.stack(cs_l)
    ffn_conv_sample = jnp.stack(ffs_l)
    return (y_p, y_s, k_prompt, v_prompt, logf_prompt, conv_prompt, ffn_conv_prompt,
            k_sample, v_sample, logf_sample, conv_sample, ffn_conv_sample)
```

```python
import numpy as np
import concourse.bass as bass
import concourse.mybir as mybir
from concourse.bass_utils import run_bass_kernel_spmd

F32 = mybir.dt.float32
BF16 = mybir.dt.bfloat16
AF = mybir.ActivationFunctionType
ALU = mybir.AluOpType

D = 1024
H = 8
DH = 64
CK = 31
DFF = 2816
NFF = 22
ALPHA = float(2 ** 0.25)
EPS = 1e-5
N_CORES = 8


class Op:
    __slots__ = ("eng", "fn", "deps", "marked", "mark_idx", "chan", "target")


class Prog:
    ENGS = ("pe", "act", "dve", "pool", "sp")

    def __init__(self, nc, same_engine_sync=True):
        self.nc = nc
        self.ops = {e: [] for e in self.ENGS}
        self.last_w = {}
        self.readers = {}
        self.chan_cnt = {}
        self.same = same_engine_sync
        self.all_dma = []

    def _dep(self, o, reads, writes):
        deps = set()
        for r in reads:
            w = self.last_w.get(r)
            if w is not None:
                deps.add(w)
        for r in writes:
            w = self.last_w.get(r)
            if w is not None:
                deps.add(w)
            for rd in self.readers.get(r, ()):
                deps.add(rd)
        deps.discard(o)
        o.deps = list(deps)
        for r in reads:
            self.readers.setdefault(r, []).append(o)
        for r in writes:
            self.last_w[r] = o
            self.readers[r] = []

    def op(self, eng, fn, reads=(), writes=()):
        o = Op()
        o.eng = eng
        o.fn = fn
        o.marked = False
        o.mark_idx = 0
        o.chan = None
        o.target = 0
        self._dep(o, reads, writes)
        self.ops[eng].append(o)
        return o

    def dma(self, eng, out, in_, chan, reads=(), writes=()):
        return self.dma_multi(eng, [(out, in_)], chan, reads, writes)

    def dma_multi(self, eng, pairs, chan, reads=(), writes=()):
        o = Op()
        o.eng = eng
        o.fn = list(pairs)
        o.marked = False
        o.mark_idx = 0
        chan = (eng, chan)
        o.chan = chan
        self.chan_cnt[chan] = self.chan_cnt.get(chan, 0) + 16 * len(pairs)
        o.target = self.chan_cnt[chan]
        self._dep(o, reads, writes)
        self.ops[eng].append(o)
        self.all_dma.append(o)
        return o

    def emit(self, stack):
        nc = self.nc
        for e in self.ENGS:
            for o in self.ops[e]:
                for d in o.deps:
                    d.marked = True
        for e in self.ENGS:
            c = 0
            for o in self.ops[e]:
                if o.chan is None and o.marked:
                    c += 1
                    o.mark_idx = c
        esem = {e: stack.enter_context(nc.semaphore("s_" + e)) for e in self.ENGS}
        csem = {}
        for i, ch in enumerate(self.chan_cnt):
            csem[ch] = stack.enter_context(nc.semaphore("c%d" % i))
        finals = [(csem[ch], cnt) for ch, cnt in self.chan_cnt.items()]

        def run(e, eng):
            waited = {}
            for o in self.ops[e]:
                for d in o.deps:
                    if d.chan is not None:
                        key = ("c", d.chan)
                        val = d.target
                        sem = csem[d.chan]
                    else:
                        if d.eng == e and (e == "pe" or not self.same):
                            continue
                        key = d.eng
                        val = d.mark_idx
                        sem = esem[d.eng]
                    if waited.get(key, 0) >= val:
                        continue
                    eng.wait_ge(sem, val)
                    waited[key] = val
                if o.chan is not None:
                    for (do, di) in o.fn:
                        eng.dma_start(out=do, in_=di).then_inc(csem[o.chan], 16)
                    continue
                ins = o.fn(eng)
                if o.marked:
                    ins.then_inc(esem[e], 1)
            if e == "sp":
                for sem, cnt in finals:
                    eng.wait_ge(sem, cnt)

        block = stack.enter_context(nc.Block())

        @block.tensor
        def _(eng):
            run("pe", eng)

        @block.scalar
        def _(eng):
            run("act", eng)

        @block.vector
        def _(eng):
            run("dve", eng)

        @block.gpsimd
        def _(eng):
            run("pool", eng)

        @block.sync
        def _(eng):
            run("sp", eng)


def make_pieces():
    P = {}
    order = []

    def add(name, W, entries):
        P[name] = (W, entries)
        order.append(name)

    for i, c0 in enumerate((0, 256)):
        add("q%d" % i, 256, [("w_in", kc, c0) for kc in range(8)])
    for i, c0 in enumerate((512, 768)):
        add("k%d" % i, 256, [("w_in", kc, c0) for kc in range(8)])
    for i, c0 in enumerate((1024, 1280)):
        add("v%d" % i, 256, [("w_in", kc, c0) for kc in range(8)])
    add("f", 8, [("w_in", kc, 1536) for kc in range(8)])
    for cc in range(2):
        add("ga%d" % cc, 256, [("w_in", kc, 1544 + cc * 256) for kc in range(8)])
        add("gb%d" % cc, 256, [("w_in", kc, 2056 + cc * 256) for kc in range(8)])
    for d in range(8):
        add("gA%d" % d, 128, [("w_in", kc, 2568 + d * 128) for kc in range(8)])
        add("gB%d" % d, 128, [("w_in", kc, 3592 + d * 128) for kc in range(8)])
        add("pr%d" % d, 128, [("w_attn_proj", kc, d * 128) for kc in range(4)]
            + [("w_conv_proj", kc, d * 128) for kc in range(4)])
    for d in range(8):
        add("wo%d" % d, 128, [("w_out", kc, d * 128) for kc in range(8)])
    for i in range(NFF):
        add("upa%d" % i, 128, [("w_up", kc, i * 128) for kc in range(8)])
        add("upv%d" % i, 128, [("w_up", kc, DFF + i * 128) for kc in range(8)])
    for d in range(8):
        for part, (i0, i1) in enumerate(((0, 8), (8, 16), (16, 22))):
            add("dn%d_%d" % (d, part), 128, [("w_down", i, d * 128) for i in range(i0, i1)])
    return P, order


class _Stop(Exception):
    pass


def build_nc(NPS, SEQ, TS, PAST, TT, stop_after=None):
    try:
        return _build_nc(NPS, SEQ, TS, PAST, TT, stop_after)
    except _Stop as ex:
        return ex.args[0]


def _build_nc(NPS, SEQ, TS, PAST, TT, stop_after=None):
    NSEQ = NPS + 1
    NK = max(SEQ, PAST + TS)
    NB = (NK + 127) // 128
    NKP = NB * 128
    nc = bass.Bass("TRN2", target_bir_lowering=False)
    import contextlib
    stack = contextlib.ExitStack()

    def din(name, shape):
        return nc.dram_tensor(name, list(shape), F32, kind="ExternalInput").ap()

    def dout(name, shape):
        return nc.dram_tensor(name, list(shape), F32, kind="ExternalOutput").ap()

    xp = din("xp", (NPS, SEQ, D))
    xsm = din("xsm", (TS, D))
    call = din("c_all", (NSEQ, D))
    cache_k = din("cache_k", (PAST, 512))
    cache_v = din("cache_v", (PAST, 512))
    cache_lf = din("cache_logf", (PAST, 8))
    st_conv = din("state_conv", (CK - 1, 512))
    st_ffn = din("state_ffn", (2, DFF))
    W = {}
    for name, shp in (("w_ada", (D, 6 * D)), ("w_in", (D, 4616)), ("w_attn_proj", (512, D)),
                      ("w_conv_proj", (512, D)), ("w_out", (D, D)), ("w_up", (D, 2 * DFF)),
                      ("w_down", (DFF, D))):
        W[name] = din(name, shp)
    b_ada = din("b_ada", (6 * D,))
    b_f = din("b_f", (8,))
    conv_w = din("conv_w", (CK, 512))
    conv_b = din("conv_b", (512,))
    conv_ln_g = din("conv_ln_g", (512,))
    conv_ln_b = din("conv_ln_b", (512,))
    ln1_g = din("ln1_g", (D,))
    ln1_b = din("ln1_b", (D,))
    ffn_conv_w = din("ffn_conv_w", (3, DFF))
    ffn_conv_b = din("ffn_conv_b", (DFF,))
    ln2_g = din("ln2_g", (D,))
    ln2_b = din("ln2_b", (D,))
    ident_d = din("ident", (128, 128))
    mask_d = din("mask", (128, 128))

    y_p = dout("y_p", (NPS, SEQ, D))
    y_s = dout("y_s", (TS, D))
    k_p = dout("k_p", (NPS, SEQ, 512))
    v_p = dout("v_p", (NPS, SEQ, 512))
    lf_p = dout("lf_p", (NPS, SEQ, 8))
    cv_p = dout("cv_p", (NPS, CK - 1, 512))
    ff_p = dout("ff_p", (NPS, 2, DFF))
    k_s = dout("k_s", (TS, 512))
    v_s = dout("v_s", (TS, 512))
    lf_s = dout("lf_s", (TS, 8))
    cv_s = dout("cv_s", (CK - 1, 512))
    ff_s = dout("ff_s", (2, DFF))

    pieces, porder = make_pieces()
    pidx = {n: i for i, n in enumerate(porder)}
    wscr = nc.dram_tensor("wscr", [len(porder), 128, 2048], BF16, kind="Internal").ap()

    def sb(name, shape, dt=F32):
        return stack.enter_context(nc.sbuf_tensor("sb_" + name, list(shape), dt))

    KA = sb("KA", (128, H, NK), BF16)
    VA = sb("VA", (128, NB, H, 66), BF16)
    Gk = sb("Gk", (128, NB, H))
    NSUB = (TT + 127) // 128
    bufs = [sb("bufA", (128, 2048)), sb("bufB", (128, 2048))]
    xs_views = [bb[:, 0:NSUB * D].rearrange("p (a b) -> p a b", b=D) for bb in bufs]
    x1a_views = [bb[:, 0:8 * TT].rearrange("p (k t) -> p k t", t=TT) for bb in bufs]
    uT = sb("uT", (128, 8, TT), BF16)
    QA = sb("QA", (128, H, TT), BF16)
    NPT = 4
    PT = [sb("PT%d" % i, (128, TT), BF16) for i in range(NPT)]
    attn_tm = sb("attn_tm", (128, NSUB, 512))
    attnT = sb("attnT", (128, 4, TT), BF16)
    gluX = sb("gluX", (128, 4, CK - 1 + TT))
    R1 = sb("R1", (128, NFF * TT), BF16)
    hT = R1[:, :].rearrange("p (i t) -> p i t", t=TT)
    R1f = R1[:, :].bitcast(F32)
    hc = R1f[:, 0:4 * TT].rearrange("p (c t) -> p c t", t=TT)
    lnm = R1f[:, 4 * TT:5 * TT]
    lnv = R1f[:, 5 * TT:6 * TT]
    lnr = R1f[:, 6 * TT:7 * TT]
    hcb = R1[:, 14 * TT:18 * TT].rearrange("p (c t) -> p c t", t=TT)
    sqb = R1[:, 18 * TT:22 * TT].rearrange("p (c t) -> p c t", t=TT)
    sg = [sb("sg%d" % i, (128, TT)) for i in range(4)]
    t1 = [sb("t1_%d" % i, (128, TT)) for i in range(2)]
    mergedT = sb("mergedT", (128, 8, TT), BF16)
    tmpf = [sb("tmpf%d" % i, (128, TT)) for i in range(2)]
    a_sb = [sb("a_sb%d" % i, (128, TT + 2)) for i in range(2)]
    cbuf = [sb("cbuf%d" % i, (128, TT)) for i in range(2)]
    sbuf_s = [sb("sil%d" % i, (128, TT)) for i in range(2)]
    NUNIT = 8
    ring = sb("ring", (128, NUNIT * 1024), BF16)
    ln2gb = sb("ln2gb", (128, D))
    ln2bb = sb("ln2bb", (128, D))
    kst = [sb("kst%d" % i, (128, 512)) for i in range(2)]
    vst = [sb("vst%d" % i, (128, 512)) for i in range(2)]
    ident = sb("ident", (128, 128))
    maskb = sb("maskb", (128, 128), BF16)
    maskf = sb("maskf", (128, 128))
    onesb = sb("onesb", (128, 128), BF16)
    zerob = sb("zerob", (128, 160), BF16)
    ones8 = sb("ones8", (8, TT))
    l_sb = sb("l_sb", (8, TT))
    e_sb = sb("e_sb", (8, TT))
    Gt = sb("Gt", (8, TT))
    Gnb = sb("Gnb", (8, TT), BF16)
    carry = sb("carry", (8, 1))
    negbf = sb("negbf", (8, 1))
    bf_sb = sb("bf_sb", (8, 1))
    lstage = sb("lstage", (128, NSUB, 8))
    VS = [sg[i][:, 0:128] for i in range(3)]
    VT = [sb("VT%d" % i, (128, 128)) for i in range(3)]
    modT = sb("modT", (128, NSEQ, 48))
    cT = sb("cT", (128, 8, NSEQ))
    sc1p = sb("sc1p", (128, NSEQ, 8))
    G2 = sb("G2", (128, NSEQ, 8))
    B2 = sb("B2", (128, NSEQ, 8))
    AG = sb("AG", (128, 8))
    AB = sb("AB", (128, 8))
    hist = sb("hist", (128, NFF, 2))
    hist44 = sb("hist44", (128, 44))
    st44 = sb("st44", (44, 128))
    cvst = attn_tm[0:32, 0, :]
    stat = sb("stat", (128, NSUB, 2, 6))
    mv = sb("mv", (128, NSUB, 2))
    rstd = sb("rstd", (128, NSUB, 1))
    nbias = sb("nbias", (128, NSUB, 1))
    rc = sb("rc", (128, NSUB, 1))
    lc = sb("lc", (128, 2, 8))

    psb = [stack.enter_context(nc.psum_tensor("ps%d" % i, [128, 512], F32)) for i in range(8)]

    Pg = Prog(nc)
    op = Pg.op

    def ckpt(k):
        if stop_after is not None and k >= stop_after:
            Pg.emit(stack)
            stack.close()
            raise _Stop(nc)

    bank_ctr = [0]
    held = set()

    def bank(hold=False):
        while True:
            b = bank_ctr[0] % 6
            bank_ctr[0] += 1
            if b not in held:
                break
        if hold:
            held.add(b)
        return b

    def pr_(b):
        return ("ps", b)

    unit_ctr = [0]

    def alloc_units(nelem):
        nu = 1 if nelem <= 1024 else 2
        if nu == 2 and unit_ctr[0] % 2 == 1:
            unit_ctr[0] += 1
        u = unit_ctr[0] % NUNIT
        unit_ctr[0] += nu
        return u, [("unit", u + k) for k in range(nu)]

    def load_piece(name):
        Wd, ents = pieces[name]
        nn_ = len(ents) * Wd
        u, rs = alloc_units(nn_)
        Pg.dma("sp", ring[:, u * 1024:u * 1024 + nn_], wscr[pidx[name]][:, 0:nn_], ("unit", u),
               reads=[("wscr", name)], writes=rs)
        view = ring[:, u * 1024:u * 1024 + nn_].rearrange("p (e w) -> p e w", w=Wd)
        return rs, view

    Pg.dma("sp", ident[:, :], ident_d, "c_ident", writes=["ident"])
    Pg.dma("sp", maskf[:, :], mask_d, "c_mask", writes=["maskf"])
    op("dve", lambda e: e.tensor_copy(maskb[:, :], maskf[:, :]), ["maskf"], ["maskb"])
    op("dve", lambda e: e.memset(onesb[:, :], 1.0), [], ["onesb"])
    op("dve", lambda e: e.memset(zerob[:, :], 0.0), [], ["zerob"])
    op("dve", lambda e: e.memset(ones8[:, :], 1.0), [], ["ones8"])
    op("pool", lambda e: e.memset(KA[64:128, :, :].rearrange("p a b -> p (a b)"), 0.0), [], [("KA1",)])
    op("pool", lambda e: e.memset(KA[64:65, :, :].rearrange("p a b -> p (a b)"), 1.0), [], [("KA1",)])
    op("pool", lambda e: e.memset(QA[64:128, :, :].rearrange("p a b -> p (a b)"), 0.0), [],
       [("QAg", h) for h in range(H)])
    Pg.dma("sp", ln2gb[:, :], ln2_g.partition_broadcast(128), "c_l2g", writes=["ln2gb"])
    Pg.dma("sp", ln2bb[:, :], ln2_b.partition_broadcast(128), "c_l2b", writes=["ln2bb"])
    Pg.dma("sp", bf_sb[:, :], b_f.rearrange("(a b) -> a b", b=1), "c_bf", writes=["bf_sb"])
    op("act", lambda e: e.mul(negbf[:, :], bf_sb[:, :], -1.0), ["bf_sb"], ["negbf"])

    Pg.dma("sp", VS[0][0:124, :], conv_w.rearrange("j (c p) -> (j c) p", p=128), "c_vs0",
           writes=[("VS", 0)])
    r = 0
    VS1map = {}
    for nm, ap_, n in (("conv_b", conv_b, 4), ("conv_ln_g", conv_ln_g, 4), ("conv_ln_b", conv_ln_b, 4),
                       ("ln1_g", ln1_g, 8), ("ln1_b", ln1_b, 8), ("ffn_conv_b", ffn_conv_b, NFF)):
        Pg.dma("sp", VS[1][r:r + n, :], ap_.rearrange("(a b) -> a b", b=128), "c_vs1_" + nm,
               writes=[("VS", 1)])
        VS1map[nm] = r
        r += n
    VS1map["c"] = r
    for kc in range(8):
        Pg.dma("sp", VS[1][r + kc * NSEQ:r + (kc + 1) * NSEQ, :], call[:, kc * 128:(kc + 1) * 128],
               "c_vs1_c%d" % kc, writes=[("VS", 1)])
    r1rows = r + 8 * NSEQ
    Pg.dma("sp", VS[2][0:48, :], b_ada.rearrange("(a b) -> a b", b=128), "c_vs2a", writes=[("VS", 2)])
    Pg.dma("sp", VS[2][48:48 + 66, :], ffn_conv_w.rearrange("j (i p) -> (j i) p", p=128), "c_vs2b",
           writes=[("VS", 2)])
    for i, nrows in ((0, 124), (1, r1rows), (2, 114)):
        b = bank()
        op("pe", lambda e, i=i, nrows=nrows, b=b: e.transpose(psb[b][:, 0:nrows], VS[i][0:nrows, :],
                                                              ident[0:nrows, 0:nrows]),
           [("VS", i), "ident"], [pr_(b)])
        op("dve", lambda e, i=i, nrows=nrows, b=b: e.tensor_copy(VT[i][:, 0:nrows], psb[b][:, 0:nrows]),
           [pr_(b)], [("VT", i)])

    def cw(c, j):
        return VT[0][:, j * 4 + c:j * 4 + c + 1]

    def v1(nm, i):
        return VT[1][:, VS1map[nm] + i:VS1map[nm] + i + 1]

    def fcw(j, i):
        return VT[2][:, 48 + j * NFF + i:48 + j * NFF + i + 1]

    cb0 = VS1map["c"]
    op("dve", lambda e: e.tensor_copy(cT[:, :, :].rearrange("p k s -> p (k s)"),
                                      VT[1][:, cb0:cb0 + 8 * NSEQ]), [("VT", 1)], ["cT"])

    ckpt(1)
    VAf = VA[:, :, :, :].rearrange("p a b c -> p (a b c)").bitcast(F32)
    R1_NAMES = ([("hc", c) for c in range(4)] + [("hcs", c) for c in range(4)]
                + ["hcb", "sqb", "lnm", "lnv", "lnr"] + [("hT", i) for i in range(NFF)])
    stg = [bufs[0][:, :], bufs[1][:, :], R1f[:, 0:2048]]
    stg_alias = [[("B", 0, q) for q in range(8)], [("B", 1, q) for q in range(8)], R1_NAMES]
    if NB * H * 66 // 2 >= 4096:
        stg += [VAf[:, 0:2048], VAf[:, 2048:4096]]
        stg_alias += [[("VA", b) for b in range(NB)], [("VA", b) for b in range(NB)]]
    NSTG = len(stg)

    def stage_fence(sgi):
        nm = stg_alias[sgi] + [("stg", sgi, ei) for ei in range(24)]
        op("sp", lambda e: e.nop(), [], nm)

    for sgi in range(NSTG):
        stage_fence(sgi)
    for pi, name in enumerate(porder):
        Wd, ents = pieces[name]
        sgi = pi % NSTG
        Pg.dma_multi("sp", [(stg[sgi][:, ei * Wd:(ei + 1) * Wd],
                             W[mname][rcx * 128:(rcx + 1) * 128, c0:c0 + Wd])
                            for ei, (mname, rcx, c0) in enumerate(ents)], ("stg", sgi),
                     writes=[("stg", sgi, ei) for ei in range(len(ents))])
        n = len(ents) * Wd
        u, rs = alloc_units(n)
        rd = [("stg", sgi, ei) for ei in range(len(ents))]
        if pi % 2 == 0:
            op("dve", lambda e, u=u, sgi=sgi, n=n: e.tensor_copy(ring[:, u * 1024:u * 1024 + n], stg[sgi][:, 0:n]),
               rd, rs)
        else:
            op("act", lambda e, u=u, sgi=sgi, n=n: e.copy(ring[:, u * 1024:u * 1024 + n], stg[sgi][:, 0:n]),
               rd, rs)
        Pg.dma("pool", wscr[pidx[name]][:, 0:n], ring[:, u * 1024:u * 1024 + n], ("unitst", u),
               reads=rs, writes=[("wscr", name)])

    ckpt(2)
    bm = bank(hold=True)
    for pc in range(24):
        sgi = pc % 2
        Pg.dma_multi("sp", [(stg[sgi][:, kc * 256:(kc + 1) * 256],
                             W["w_ada"][kc * 128:(kc + 1) * 128, pc * 256:(pc + 1) * 256])
                            for kc in range(8)], ("stg", sgi), writes=[("stg", sgi, kc) for kc in range(8)])
        for mm_ in range(2):
            m = pc * 2 + mm_
            for kc in range(8):
                op("pe", lambda e, sgi=sgi, kc=kc, mm_=mm_, m=m: e.matmul(
                    psb[bm][:, m:m + 48 * (NSEQ - 1) + 1:48],
                    stg[sgi][:, kc * 256 + mm_ * 128:kc * 256 + (mm_ + 1) * 128],
                    cT[:, kc, :], start=(kc == 0), stop=(kc == 7)),
                   [("stg", sgi, kc), "cT"], [pr_(bm)])
    for s in range(NSEQ):
        op("dve", lambda e, s=s: e.tensor_tensor(out=modT[:, s, :], in0=psb[bm][:, s * 48:(s + 1) * 48],
                                                 in1=VT[2][:, 0:48], op=ALU.add),
           [pr_(bm), ("VT", 2)], ["modT"])
    held.discard(bm)
    for sgi in range(NSTG):
        stage_fence(sgi)
    op("pool", lambda e: e.memset(VA[:, :, :, :].rearrange("p a b c -> p (a b c)"), 1.0), [],
       [("VA", b) for b in range(NB)])
    g0 = VS1map["ln1_g"]
    b0 = VS1map["ln1_b"]
    for s in range(NSEQ):
        op("dve", lambda e, s=s: e.tensor_scalar_add(sc1p[:, s, :], modT[:, s, 8:16], 1.0), ["modT"], ["sc1p"])
        op("dve", lambda e, s=s: e.tensor_scalar_add(G2[:, s, :], modT[:, s, 32:40], 1.0), ["modT"], ["G2"])
        op("dve", lambda e, s=s: e.tensor_tensor(out=B2[:, s, :], in0=G2[:, s, :], in1=VT[1][:, b0:b0 + 8],
                                                 op=ALU.mult), ["G2", ("VT", 1)], ["B2"])
        op("dve", lambda e, s=s: e.tensor_tensor(out=B2[:, s, :], in0=B2[:, s, :], in1=modT[:, s, 24:32],
                                                 op=ALU.add), ["B2", "modT"], ["B2"])
        op("dve", lambda e, s=s: e.tensor_tensor(out=G2[:, s, :], in0=G2[:, s, :], in1=VT[1][:, g0:g0 + 8],
                                                 op=ALU.mult), ["G2", ("VT", 1)], ["G2"])
    op("dve", lambda e: e.tensor_scalar_mul(AG[:, :], VT[1][:, g0:g0 + 8], ALPHA), [("VT", 1)], ["AG"])
    op("dve", lambda e: e.tensor_scalar_mul(AB[:, :], VT[1][:, b0:b0 + 8], ALPHA), [("VT", 1)], ["AB"])

    ckpt(3)
    fdummy = sb("fdummy", (1, 8))

    def r1_fence():
        op("dve", lambda e: e.memset(fdummy[:, :], 0.0), [], R1_NAMES + ["fdummy"])

    def subs_of(n):
        out = []
        off = 0
        j = 0
        while off < n:
            m = min(128, n - off)
            out.append((j, off, m))
            off += m
            j += 1
        return out

    def append_G(n, kpos, lf_out=None, to_QA=False):
        append_G1(n, to_QA)
        append_G2(n, kpos, lf_out)

    def append_G1(n, to_QA):
        op("dve", lambda e: e.tensor_tensor_scan(out=Gt[:, 0:n], data0=ones8[:, 0:n], data1=l_sb[:, 0:n],
                                                 initial=carry[:, 0:1], op0=ALU.mult, op1=ALU.add),
           ["l_sb", "ones8", "carry"], ["Gt"])
        op("dve", lambda e: e.tensor_copy(carry[:, 0:1], Gt[:, n - 1:n]), ["Gt"], ["carry"])
        if to_QA:
            op("act", lambda e: e.mul(Gnb[:, 0:n], Gt[:, 0:n], -1.0), ["Gt"], ["Gnb"])
            for h in range(H):
                Pg.dma("pool", QA[64:65, h, 0:n], Gnb[h:h + 1, 0:n], ("qag", h), reads=["Gnb"],
                       writes=[("QAg", h)])

    def append_G2(n, kpos, lf_out):
        b = bank()
        sl = subs_of(n)
        for (j, off, m) in sl:
            if lf_out is not None:
                op("pe", lambda e, j=j, off=off, m=m: e.transpose(psb[b][0:m, j * 16:j * 16 + 8],
                                                                  l_sb[0:8, off:off + m], ident[0:8, 0:8]),
                   ["l_sb", "ident"], [pr_(b)])
            op("pe", lambda e, j=j, off=off, m=m: e.transpose(psb[b][0:m, j * 16 + 8:j * 16 + 16],
                                                              Gt[0:8, off:off + m], ident[0:8, 0:8]),
               ["Gt", "ident"], [pr_(b)])
        for (j, off, m) in sl:
            blk = (kpos + off) // 128
            if lf_out is not None:
                op("act", lambda e, j=j, m=m: e.mul(lstage[0:m, j, :], psb[b][0:m, j * 16:j * 16 + 8], -1.0),
                   [pr_(b)], [("lstage", j)])
                Pg.dma("pool", lf_out[off:off + m, :], lstage[0:m, j, :], ("lst", j), reads=[("lstage", j)])
            op("act", lambda e, j=j, m=m, blk=blk: e.copy(Gk[0:m, blk, :],
                                                          psb[b][0:m, j * 16 + 8:j * 16 + 16]),
               [pr_(b)], [("Gk", blk)])

    def layer_norm_tile(sl, xs, RX):
        for (j, off, m) in sl:
            for hh in range(2):
                op("dve", lambda e, hh=hh, j=j, m=m: e.bn_stats(stat[0:m, j, hh, :],
                                                                xs[0:m, j, hh * 512:(hh + 1) * 512]),
                   [*RX(j)], [("stat", j)])
        for (j, off, m) in sl:
            op("dve", lambda e, j=j, m=m: e.bn_aggr(mv[0:m, j, :], stat[0:m, j, :, :].rearrange("p a b -> p (a b)")),
               [("stat", j)], [("mv", j)])
            op("dve", lambda e, j=j, m=m: e.tensor_scalar_add(rstd[0:m, j, :], mv[0:m, j, 1:2], EPS),
               [("mv", j)], [("rstd", j)])
        for (j, off, m) in sl:
            op("act", lambda e, j=j, m=m: e.sqrt(rstd[0:m, j, :], rstd[0:m, j, :]), [("rstd", j)], [("rstd", j)])
        for (j, off, m) in sl:
            op("dve", lambda e, j=j, m=m: e.reciprocal(rstd[0:m, j, :], rstd[0:m, j, :]),
               [("rstd", j)], [("rstd", j)])
            op("dve", lambda e, j=j, m=m: e.scalar_tensor_tensor(out=nbias[0:m, j, :], in0=mv[0:m, j, 0:1],
                                                                 scalar=-1.0, in1=rstd[0:m, j, :], op0=ALU.mult,
                                                                 op1=ALU.mult),
               [("mv", j), ("rstd", j)], [("nbias", j)])
        for (j, off, m) in sl:
            op("act", lambda e, j=j, m=m: e.activation(xs[0:m, j, :], xs[0:m, j, :], AF.Identity,
                                                       bias=nbias[0:m, j, :], scale=rstd[0:m, j, :]),
               [*RX(j), ("rstd", j), ("nbias", j)], [*RX(j)])

    tile_ctr = [0]

    def tile_head(si, x_src, t0, n):
        sl = subs_of(n)
        par = tile_ctr[0] % 2
        tile_ctr[0] += 1
        xs = xs_views[par]

        def RX(j):
            return [("B", par, q) for q in range(4 * j, 4 * j + 4)]
        for (j, off, m) in sl:
            Pg.dma("sp", xs[0:m, j, :], x_src[t0 + off:t0 + off + m, :], ("xs", par, j), writes=RX(j))
        for kc in range(8):
            b = bank()
            for (j, off, m) in sl:
                op("pe", lambda e, kc=kc, j=j, off=off, m=m, b=b: e.transpose(
                    psb[b][:, off:off + m], xs[0:m, j, kc * 128:(kc + 1) * 128], ident[0:m, 0:m]),
                   [*RX(j), "ident"], [pr_(b)])
            op("act", lambda e, kc=kc, b=b: e.activation(uT[:, kc, 0:n], psb[b][:, 0:n], AF.Identity,
                                                         bias=modT[:, si, kc:kc + 1],
                                                         scale=sc1p[:, si, kc:kc + 1]),
               [pr_(b), "modT", "sc1p"], [("uT", kc)])
        return par

    def tile(si, x_src, t0, n, kpos, outs, last, par, next_head):
        (y_o, k_o, v_o, lf_o, cv_o, ff_o) = outs
        sl = subs_of(n)
        nsub = len(sl)
        xs = xs_views[par]
        x1a = x1a_views[1 - par]
        next_par = [None]

        def RX(j):
            return [("B", par, q) for q in range(4 * j, 4 * j + 4)]

        def RA(kc):
            return [("B", 1 - par, kc)]
        uT_all = [("uT", kc) for kc in range(8)]
        kblks = sorted(set((kpos + off) // 128 for (_, off, _) in sl))
        ckpt(13)
        s, wv = load_piece("f")
        b = bank()
        for kc in range(8):
            op("pe", lambda e, kc=kc, b=b, wv=wv: e.matmul(psb[b][0:8, 0:n], wv[:, kc, :], uT[:, kc, 0:n],
                                                           start=(kc == 0), stop=(kc == 7)),
               [*s, ("uT", kc)], [pr_(b)])
        op("act", lambda e, b=b: e.activation(e_sb[:, 0:n], psb[b][0:8, 0:n], AF.Exp, bias=negbf[:, 0:1],
                                              scale=-1.0), [pr_(b), "negbf"], ["e_sb"])
        op("act", lambda e: e.activation(l_sb[:, 0:n], e_sb[:, 0:n], AF.Ln, bias=1.0, scale=1.0),
           ["e_sb"], ["l_sb"])
        append_G1(n, True)
        ckpt(14)
        for cc in range(2):
            sa, wa = load_piece("ga%d" % cc)
            sb_, wb = load_piece("gb%d" % cc)
            for cl in range(2):
                c = cc * 2 + cl
                ba = bank()
                bb_ = bank()
                for kc in range(8):
                    op("pe", lambda e, kc=kc, cl=cl, ba=ba, wa=wa: e.matmul(
                        psb[ba][:, 0:n], wa[:, kc, cl * 128:(cl + 1) * 128], uT[:, kc, 0:n],
                        start=(kc == 0), stop=(kc == 7)), [*sa, ("uT", kc)], [pr_(ba)])
                for kc in range(8):
                    op("pe", lambda e, kc=kc, cl=cl, bb_=bb_, wb=wb: e.matmul(
                        psb[bb_][:, 0:n], wb[:, kc, cl * 128:(cl + 1) * 128], uT[:, kc, 0:n],
                        start=(kc == 0), stop=(kc == 7)), [*sb_, ("uT", kc)], [pr_(bb_)])
                g = sg[c % 2]
                op("act", lambda e, bb_=bb_, g=g: e.activation(g[:, 0:n], psb[bb_][:, 0:n], AF.Sigmoid),
                   [pr_(bb_)], [("sg", c % 2)])
                op("dve", lambda e, c=c, ba=ba, g=g: e.tensor_tensor(
                    out=gluX[:, c, CK - 1:CK - 1 + n], in0=psb[ba][:, 0:n], in1=g[:, 0:n], op=ALU.mult),
                   [pr_(ba), ("sg", c % 2)], [("gluXn", c)])
        ckpt(15)
        r1_fence()
        conv_thunks = []
        for c in range(4):
            conv_thunks.append((lambda e, c=c: e.tensor_scalar(
                out=hc[:, c, 0:n], in0=gluX[:, c, 0:n], scalar1=cw(c, 0), scalar2=v1("conv_b", c),
                op0=ALU.mult, op1=ALU.add),
                [("gluXn", c), ("gluXh", c), ("VT", 0), ("VT", 1)], [("hc", c)]))
        for jt in range(1, CK):
            for c in range(4):
                conv_thunks.append((lambda e, c=c, jt=jt: e.scalar_tensor_tensor(
                    out=hc[:, c, 0:n], in0=gluX[:, c, jt:jt + n], scalar=cw(c, jt), in1=hc[:, c, 0:n],
                    op0=ALU.mult, op1=ALU.add), [("gluXn", c), ("gluXh", c), ("hc", c)], [("hc", c)]))
        conv_pos = [0]

        def conv_emit(k):
            for (fn, rd, wr) in conv_thunks[conv_pos[0]:conv_pos[0] + k]:
                op("dve", fn, rd, wr)
            conv_pos[0] += k
        conv_emit(64)
        ckpt(10)
        for i in range(2):
            s, wv = load_piece("q%d" % i)
            for hl in range(4):
                h = i * 4 + hl
                b = bank()
                for kc in range(8):
                    op("pe", lambda e, kc=kc, hl=hl, b=b, wv=wv: e.matmul(
                        psb[b][0:64, 0:n], wv[:, kc, hl * 64:(hl + 1) * 64], uT[:, kc, 0:n],
                        start=(kc == 0), stop=(kc == 7)), [*s, ("uT", kc)], [pr_(b)])
                op("act", lambda e, h=h, b=b: e.mul(QA[0:64, h, 0:n], psb[b][0:64, 0:n], 0.125),
                   [pr_(b)], [("QA", h)])
        append_G2(n, kpos, lf_o[t0:t0 + n, :])
        ckpt(11)
        bk = [bank(hold=True) for _ in sl]
        for i in range(2):
            s, wv = load_piece("k%d" % i)
            for hl in range(4):
                h = i * 4 + hl
                b = bank()
                for kc in range(8):
                    op("pe", lambda e, kc=kc, hl=hl, b=b, wv=wv: e.matmul(
                        psb[b][0:64, 0:n], wv[:, kc, hl * 64:(hl + 1) * 64], uT[:, kc, 0:n],
                        start=(kc == 0), stop=(kc == 7)), [*s, ("uT", kc)], [pr_(b)])
                op("act", lambda e, h=h, b=b: e.copy(KA[0:64, h, kpos:kpos + n], psb[b][0:64, 0:n]),
                   [pr_(b)], [("KA", h, bb) for bb in kblks])
            for (j, off, m) in sl:
                for kc in range(8):
                    op("pe", lambda e, kc=kc, j=j, off=off, m=m, wv=wv, i=i: e.matmul(
                        psb[bk[j]][0:m, i * 256:(i + 1) * 256], uT[:, kc, off:off + m], wv[:, kc, :],
                        start=(kc == 0), stop=(kc == 7)), [*s, ("uT", kc)], [pr_(bk[j])])
        for (j, off, m) in sl:
            st = kst[j % 2]
            op("act", lambda e, j=j, m=m, st=st: e.copy(st[0:m, :], psb[bk[j]][0:m, :]),
               [pr_(bk[j])], [("kst", j % 2)])
            Pg.dma("pool", k_o[t0 + off:t0 + off + m, :], st[0:m, :], ("kst", j % 2), reads=[("kst", j % 2)])
            held.discard(bk[j])
        ckpt(12)
        bv = [bank(hold=True) for _ in sl]
        for i in range(2):
            s, wv = load_piece("v%d" % i)
            for (j, off, m) in sl:
                for kc in range(8):
                    op("pe", lambda e, kc=kc, j=j, off=off, m=m, wv=wv, i=i: e.matmul(
                        psb[bv[j]][0:m, i * 256:(i + 1) * 256], uT[:, kc, off:off + m], wv[:, kc, :],
                        start=(kc == 0), stop=(kc == 7)), [*s, ("uT", kc)], [pr_(bv[j])])
        for (j, off, m) in sl:
            st = vst[j % 2]
            blk = (kpos + off) // 128
            op("act", lambda e, j=j, m=m, st=st: e.copy(st[0:m, :], psb[bv[j]][0:m, :]),
               [pr_(bv[j])], [("vst", j % 2)])
            op("dve", lambda e, j=j, m=m, blk=blk, st=st: e.tensor_copy(
                VA[0:m, blk, :, 0:64], st[0:m, :].rearrange("p (h d) -> p h d", d=64)),
               [("vst", j % 2)], [("VA", blk)])
            Pg.dma("pool", v_o[t0 + off:t0 + off + m, :], st[0:m, :], ("vst", j % 2), reads=[("vst", j % 2)])
            held.discard(bv[j])
        if last:
            b = bank()
            for c in range(4):
                op("pe", lambda e, c=c, b=b: e.transpose(psb[b][0:CK - 1, c * 128:(c + 1) * 128],
                                                         gluX[:, c, n:n + CK - 1], ident[:, :]),
                   [("gluXn", c), ("gluXh", c), "ident"], [pr_(b)])
            op("act", lambda e, b=b: e.copy(cvst[0:CK - 1, :], psb[b][0:CK - 1, :]), [pr_(b)], ["cvst", ("attn_tm", 0)])
            Pg.dma("pool", cv_o, cvst[0:CK - 1, :], "cvst", reads=["cvst", ("attn_tm", 0)])
        ckpt(16)
        qa0 = kpos
        nkb = (kpos + n + 127) // 128
        m0 = sl[0][2]
        items = []
        for h in range(H):
            first = True
            for kb in range(nkb):
                ks = kb * 128
                kn = min(128, kpos + n - ks)
                need = [(j, off, m) for (j, off, m) in sl if ks <= qa0 + off + m - 1]
                if not need:
                    continue
                items.append(dict(h=h, kb=kb, ks=ks, kn=kn, need=need, c0=need[0][1], first=first, last=False))
                first = False
            items[-1]["last"] = True
        LOOK = 3

        def emit_qk(it, idx):
            b = bank()
            it["b"] = b
            it["pi"] = idx % NPT
            h, ks, kn, c0, kb = it["h"], it["ks"], it["kn"], it["c0"], it["kb"]
            op("pe", lambda e, h=h, ks=ks, kn=kn, c0=c0, b=b: e.matmul(
                psb[b][0:kn, c0:n], KA[0:128, h, ks:ks + kn], QA[0:128, h, c0:n], start=True, stop=True),
               [("KA", h, kb), ("KA1",), ("QA", h), ("QAg", h)], [pr_(b)])

        per_head = (len(conv_thunks) - conv_pos[0] + H - 1) // H
        for idx in range(min(LOOK, len(items))):
            emit_qk(items[idx], idx)
        for idx, it in enumerate(items):
            if idx + LOOK < len(items):
                emit_qk(items[idx + LOOK], idx + LOOK)
            h, kb, ks, kn, need, c0, b, pi_ = (it["h"], it["kb"], it["ks"], it["kn"], it["need"], it["c0"],
                                                it["b"], it["pi"])
            ob = 6 + (h % 2)
            O = psb[ob][:, 0:nsub * 65].rearrange("p (j d) -> p j d", d=65)
            if it["first"]:
                conv_emit(per_head)
                op("pe", lambda e, ob=ob: e.matmul(psb[ob][0:m0, 0:nsub * 65], zerob[:, 0:m0],
                                                   zerob[:, 0:nsub * 65], start=True, stop=False,
                                                   skip_group_check=True), ["zerob"], [pr_(ob)])
            pt = PT[pi_]
            op("act", lambda e, h=h, kb=kb, kn=kn, c0=c0, b=b, pt=pt: e.activation(
                pt[0:kn, c0:n], psb[b][0:kn, c0:n], AF.Exp, bias=Gk[0:kn, kb, h:h + 1], scale=1.0),
               [pr_(b), ("Gk", kb)], [("PT", pi_)])
            for (j, off, m) in need:
                if ks + kn - 1 > qa0 + off:
                    op("pool", lambda e, kn=kn, off=off, m=m, pt=pt: e.tensor_tensor(
                        out=pt[0:kn, off:off + m], in0=pt[0:kn, off:off + m], in1=maskb[0:kn, 0:m],
                        op=ALU.mult), [("PT", pi_), "maskb"], [("PT", pi_)])
            for (j, off, m) in need:
                op("pe", lambda e, h=h, kb=kb, kn=kn, j=j, off=off, m=m, O=O, pt=pt: e.matmul(
                    O[0:m, j, :], pt[0:kn, off:off + m], VA[0:kn, kb, h, 0:65], start=False, stop=True,
                    skip_group_check=True), [("PT", pi_), ("VA", kb)], [pr_(ob)])
            if it["last"]:
                for (j, off, m) in sl:
                    op("dve", lambda e, j=j, m=m, O=O: e.reciprocal(rc[0:m, j, :], O[0:m, j, 64:65]),
                       [pr_(ob)], ["rc"])
                    op("dve", lambda e, j=j, m=m, O=O, h=h: e.tensor_scalar(
                        out=attn_tm[0:m, j, h * 64:(h + 1) * 64], in0=O[0:m, j, 0:64], scalar1=rc[0:m, j, :],
                        scalar2=None, op0=ALU.mult), [pr_(ob), "rc"], [("attn_tm", j)])
        conv_emit(len(conv_thunks))
        if not last:
            for c in range(4):
                op("pool", lambda e, c=c: e.tensor_copy(gluX[:, c, 0:CK - 1], gluX[:, c, n:n + CK - 1]),
                   [("gluXn", c), ("gluXh", c)], [("gluXh", c)])
        for c in range(4):
            b = bank()
            for (j, off, m) in sl:
                op("pe", lambda e, c=c, j=j, off=off, m=m, b=b: e.transpose(
                    psb[b][:, off:off + m], attn_tm[0:m, j, c * 128:(c + 1) * 128], ident[0:m, 0:m]),
                   [("attn_tm", j), "ident"], [pr_(b)])
            op("act", lambda e, c=c, b=b: e.copy(attnT[:, c, 0:n], psb[b][:, 0:n]), [pr_(b)], [("attnT", c)])
        ckpt(17)
        op("act", lambda e: e.copy(hcb[:, :, 0:n], hc[:, :, 0:n]), [("hc", c) for c in range(4)], ["hcb"])
        op("act", lambda e: e.activation(sqb[:, :, 0:n], hc[:, :, 0:n], AF.Square),
           [("hc", c) for c in range(4)], ["sqb"])
        bs_ = bank()
        bq_ = bank()
        for c in range(4):
            op("pe", lambda e, c=c: e.matmul(psb[bs_][:, 0:n], onesb[:, :], hcb[:, c, 0:n], start=(c == 0),
                                             stop=(c == 3)), ["hcb", "onesb"], [pr_(bs_)])
        for c in range(4):
            op("pe", lambda e, c=c: e.matmul(psb[bq_][:, 0:n], onesb[:, :], sqb[:, c, 0:n], start=(c == 0),
                                             stop=(c == 3)), ["sqb", "onesb"], [pr_(bq_)])
        op("act", lambda e: e.mul(lnm[:, 0:n], psb[bs_][:, 0:n], 1.0 / 512), [pr_(bs_)], ["lnm"])
        op("dve", lambda e: e.tensor_tensor(out=lnv[:, 0:n], in0=lnm[:, 0:n], in1=lnm[:, 0:n], op=ALU.mult),
           ["lnm"], ["lnv"])
        op("dve", lambda e: e.scalar_tensor_tensor(out=lnv[:, 0:n], in0=psb[bq_][:, 0:n], scalar=1.0 / 512,
                                                   in1=lnv[:, 0:n], op0=ALU.mult, op1=ALU.subtract),
           [pr_(bq_), "lnv"], ["lnv"])
        op("dve", lambda e: e.tensor_scalar(out=lnv[:, 0:n], in0=lnv[:, 0:n], scalar1=0.0, scalar2=EPS,
                                            op0=ALU.max, op1=ALU.add), ["lnv"], ["lnv"])
        op("act", lambda e: e.activation(lnr[:, 0:n], lnv[:, 0:n], AF.Ln), ["lnv"], ["lnr"])
        op("act", lambda e: e.activation(lnr[:, 0:n], lnr[:, 0:n], AF.Exp, scale=-0.5), ["lnr"], ["lnr"])
        for c in range(4):
            op("dve", lambda e, c=c: e.tensor_tensor(out=hc[:, c, 0:n], in0=hc[:, c, 0:n], in1=lnm[:, 0:n],
                                                     op=ALU.subtract), [("hc", c), "lnm"], [("hc", c)])
            op("dve", lambda e, c=c: e.tensor_tensor(out=hc[:, c, 0:n], in0=hc[:, c, 0:n], in1=lnr[:, 0:n],
                                                     op=ALU.mult), [("hc", c), "lnr"], [("hc", c)])
            op("act", lambda e, c=c: e.activation(hcb[:, c, 0:n], hc[:, c, 0:n], AF.Silu,
                                                  bias=v1("conv_ln_b", c), scale=v1("conv_ln_g", c)),
               [("hc", c), ("VT", 1)], [("hcs", c), "hcb"])
        ckpt(18)
        for dch in range(8):
            sA, wA = load_piece("gA%d" % dch)
            sB, wB = load_piece("gB%d" % dch)
            sP, wP = load_piece("pr%d" % dch)
            for dl in range(1):
                bA = bank()
                bB = bank()
                bya = bank()
                byb = bank()
                for kc in range(8):
                    op("pe", lambda e, kc=kc, dl=dl, bA=bA, wA=wA: e.matmul(
                        psb[bA][:, 0:n], wA[:, kc, dl * 128:(dl + 1) * 128], uT[:, kc, 0:n],
                        start=(kc == 0), stop=(kc == 7)), [*sA, ("uT", kc)], [pr_(bA)])
                for kc in range(8):
                    op("pe", lambda e, kc=kc, dl=dl, bB=bB, wB=wB: e.matmul(
                        psb[bB][:, 0:n], wB[:, kc, dl * 128:(dl + 1) * 128], uT[:, kc, 0:n],
                        start=(kc == 0), stop=(kc == 7)), [*sB, ("uT", kc)], [pr_(bB)])
                for c in range(4):
                    op("pe", lambda e, c=c, dl=dl, bya=bya, wP=wP: e.matmul(
                        psb[bya][:, 0:n], wP[:, c, dl * 128:(dl + 1) * 128], attnT[:, c, 0:n],
                        start=(c == 0), stop=(c == 3)), [*sP, ("attnT", c)], [pr_(bya)])
                for c in range(4):
                    op("pe", lambda e, c=c, dl=dl, byb=byb, wP=wP: e.matmul(
                        psb[byb][:, 0:n], wP[:, 4 + c, dl * 128:(dl + 1) * 128], hcb[:, c, 0:n],
                        start=(c == 0), stop=(c == 3)), [*sP, ("hcs", c)], [pr_(byb)])
                gA_ = sg[0 + 2 * (dch % 2)]
                gB_ = sg[1 + 2 * (dch % 2)]
                rA = ("sg", 0 + 2 * (dch % 2))
                rB = ("sg", 1 + 2 * (dch % 2))
                tt = t1[dch % 2]
                rT = ("t1", dch % 2)
                op("act", lambda e, bA=bA, gA_=gA_: e.activation(gA_[:, 0:n], psb[bA][:, 0:n], AF.Sigmoid),
                   [pr_(bA)], [rA])
                op("act", lambda e, bB=bB, gB_=gB_: e.activation(gB_[:, 0:n], psb[bB][:, 0:n], AF.Sigmoid),
                   [pr_(bB)], [rB])
                op("dve", lambda e, bya=bya, gA_=gA_, tt=tt: e.tensor_tensor(
                    out=tt[:, 0:n], in0=psb[bya][:, 0:n], in1=gA_[:, 0:n], op=ALU.mult),
                   [pr_(bya), rA], [rT])
                op("dve", lambda e, byb=byb, gB_=gB_: e.tensor_tensor(
                    out=gB_[:, 0:n], in0=psb[byb][:, 0:n], in1=gB_[:, 0:n], op=ALU.mult),
                   [pr_(byb), rB], [rB])
                op("dve", lambda e, dch=dch, gB_=gB_, tt=tt: e.tensor_tensor(
                    out=mergedT[:, dch, 0:n], in0=tt[:, 0:n], in1=gB_[:, 0:n], op=ALU.add),
                   [rT, rB], [("mergedT", dch)])
        ckpt(19)
        def g_stage1(dch):
            s, wv = load_piece("wo%d" % dch)
            b = bank()
            for kc in range(8):
                op("pe", lambda e, kc=kc, b=b, wv=wv: e.matmul(
                    psb[b][:, 0:n], wv[:, kc, 0:128], mergedT[:, kc, 0:n],
                    start=(kc == 0), stop=(kc == 7)), [*s, ("mergedT", kc)], [pr_(b)])
            tf = tmpf[dch % 2]
            rtf = ("tmpf", dch % 2)
            op("act", lambda e, dch=dch, b=b, tf=tf: e.activation(
                tf[:, 0:n], psb[b][:, 0:n], AF.Copy, scale=modT[:, si, 16 + dch:17 + dch]),
               [pr_(b), "modT"], [rtf])
            return tf, rtf

        def g_stage2(dch, tf, rtf):
            b2 = bank()
            for (j, off, m) in sl:
                op("pe", lambda e, j=j, off=off, m=m, b2=b2, tf=tf: e.transpose(
                    psb[b2][0:m, j * 128:(j + 1) * 128], tf[:, off:off + m], ident[:, :]),
                   [rtf, "ident"], [pr_(b2)])
            for (j, off, m) in sl:
                op("dve", lambda e, j=j, m=m, dch=dch, b2=b2: e.scalar_tensor_tensor(
                    out=xs[0:m, j, dch * 128:(dch + 1) * 128], in0=xs[0:m, j, dch * 128:(dch + 1) * 128],
                    scalar=ALPHA, in1=psb[b2][0:m, j * 128:(j + 1) * 128], op0=ALU.mult, op1=ALU.add),
                   [pr_(b2), *RX(j)], [*RX(j)])

        st_ = g_stage1(0)
        for dch in range(8):
            nxt_ = g_stage1(dch + 1) if dch + 1 < 8 else None
            g_stage2(dch, *st_)
            st_ = nxt_
        layer_norm_tile(sl, xs, RX)
        ckpt(20)
        for kc in range(8):
            b = bank()
            for (j, off, m) in sl:
                op("pe", lambda e, kc=kc, j=j, off=off, m=m, b=b: e.transpose(
                    psb[b][:, off:off + m], xs[0:m, j, kc * 128:(kc + 1) * 128], ident[0:m, 0:m]),
                   [*RX(j), "ident"], [pr_(b)])
            op("act", lambda e, kc=kc, b=b: e.activation(uT[:, kc, 0:n], psb[b][:, 0:n], AF.Identity,
                                                         bias=B2[:, si, kc:kc + 1], scale=G2[:, si, kc:kc + 1]),
               [pr_(b), "B2", "G2"], [("uT", kc)])
            op("act", lambda e, kc=kc, b=b: e.activation(x1a[:, kc, 0:n], psb[b][:, 0:n], AF.Identity,
                                                         bias=AB[:, kc:kc + 1], scale=AG[:, kc:kc + 1]),
               [pr_(b), "AB", "AG"], [*RA(kc)])
        ckpt(21)
        r1_fence()
        for i in range(NFF):
            s, wv = load_piece("upa%d" % i)
            s2, wv2 = load_piece("upv%d" % i)
            ba = bank()
            bv_ = bank()
            for kc in range(8):
                op("pe", lambda e, kc=kc, ba=ba, wv=wv: e.matmul(psb[ba][:, 0:n], wv[:, kc, :], uT[:, kc, 0:n],
                                                                 start=(kc == 0), stop=(kc == 7)),
                   [*s, ("uT", kc)], [pr_(ba)])
            for kc in range(8):
                op("pe", lambda e, kc=kc, bv_=bv_, wv2=wv2: e.matmul(psb[bv_][:, 0:n], wv2[:, kc, :],
                                                                     uT[:, kc, 0:n], start=(kc == 0),
                                                                     stop=(kc == 7)),
                   [*s2, ("uT", kc)], [pr_(bv_)])
            ab = a_sb[i % 2]
            rab = ("a_sb", i % 2)
            cb_ = cbuf[i % 2]
            rcb = ("cbuf", i % 2)
            ss = sbuf_s[i % 2]
            rss = ("sil", i % 2)
            op("pool", lambda e, i=i, ab=ab: e.tensor_copy(ab[:, 0:2], hist[:, i, :]), [("hist", i), rab], [rab])
            op("act", lambda e, ba=ba, ab=ab: e.copy(ab[:, 2:2 + n], psb[ba][:, 0:n]), [pr_(ba), rab], [rab])
            op("act", lambda e, i=i, ba=ba, cb_=cb_: e.activation(cb_[:, 0:n], psb[ba][:, 0:n], AF.Identity,
                                                                  bias=v1("ffn_conv_b", i), scale=fcw(2, i)),
               [pr_(ba), ("VT", 1), ("VT", 2)], [rcb])
            op("dve", lambda e, i=i, ab=ab, cb_=cb_: e.scalar_tensor_tensor(
                out=cb_[:, 0:n], in0=ab[:, 1:1 + n], scalar=fcw(1, i), in1=cb_[:, 0:n], op0=ALU.mult,
                op1=ALU.add), [rab, rcb], [rcb])
            op("dve", lambda e, i=i, ab=ab, cb_=cb_: e.scalar_tensor_tensor(
                out=cb_[:, 0:n], in0=ab[:, 0:n], scalar=fcw(0, i), in1=cb_[:, 0:n], op0=ALU.mult,
                op1=ALU.add), [rab, rcb], [rcb])
            op("pool", lambda e, i=i, ab=ab: e.tensor_copy(hist[:, i, :], ab[:, n:n + 2]), [rab], [("hist", i)])
            op("act", lambda e, cb_=cb_, ss=ss: e.activation(ss[:, 0:n], cb_[:, 0:n], AF.Silu), [rcb], [rss])
            op("dve", lambda e, i=i, bv_=bv_, ss=ss: e.tensor_tensor(out=hT[:, i, 0:n], in0=psb[bv_][:, 0:n],
                                                                     in1=ss[:, 0:n], op=ALU.mult),
               [pr_(bv_), rss], [("hT", i)])
        if last:
            op("dve", lambda e: e.tensor_copy(hist44[:, :].rearrange("p (r i) -> p i r", i=NFF), hist[:, :, :]),
               [("hist", i) for i in range(NFF)], ["hist44"])
            b = bank()
            op("pe", lambda e, b=b: e.transpose(psb[b][0:44, 0:128], hist44[:, :], ident[:, :]),
               ["hist44", "ident"], [pr_(b)])
            op("act", lambda e, b=b: e.copy(st44[:, :], psb[b][0:44, 0:128]), [pr_(b)], ["st44"])
            Pg.dma("pool", ff_o.rearrange("r (i p) -> (r i) p", p=128), st44[:, :], "st44", reads=["st44"])
        ckpt(22)
        def j_stage1(dch):
            b = bank()
            for part, (i0, i1) in enumerate(((0, 8), (8, 16), (16, 22))):
                s, wv = load_piece("dn%d_%d" % (dch, part))
                for il in range(i1 - i0):
                    i = i0 + il
                    op("pe", lambda e, il=il, i=i, b=b, wv=wv: e.matmul(
                        psb[b][:, 0:n], wv[:, il, :], hT[:, i, 0:n], start=(i == 0), stop=(i == NFF - 1)),
                       [*s, ("hT", i)], [pr_(b)])
            tf = tmpf[dch % 2]
            rtf = ("tmpf", dch % 2)
            op("dve", lambda e, dch=dch, b=b, tf=tf: e.scalar_tensor_tensor(
                out=tf[:, 0:n], in0=psb[b][:, 0:n], scalar=modT[:, si, 40 + dch:41 + dch], in1=x1a[:, dch, 0:n],
                op0=ALU.mult, op1=ALU.add), [pr_(b), "modT", *RA(dch)], [rtf])
            return tf, rtf

        def j_stage2(dch, tf, rtf):
            b2 = bank()
            for (j, off, m) in sl:
                op("pe", lambda e, j=j, off=off, m=m, b2=b2, tf=tf: e.transpose(
                    psb[b2][0:m, j * 128:(j + 1) * 128], tf[:, off:off + m], ident[:, :]),
                   [rtf, "ident"], [pr_(b2)])
            for (j, off, m) in sl:
                op("act", lambda e, j=j, m=m, dch=dch, b2=b2: e.copy(
                    xs[0:m, j, dch * 128:(dch + 1) * 128], psb[b2][0:m, j * 128:(j + 1) * 128]),
                   [pr_(b2)], [*RX(j)])

        st_ = j_stage1(0)
        for dch in range(8):
            nxt_ = j_stage1(dch + 1) if dch + 1 < 8 else None
            j_stage2(dch, *st_)
            st_ = nxt_
        if next_head is not None:
            next_par[0] = next_head()
        layer_norm_tile(sl, xs, RX)
        for (j, off, m) in sl:
            op("dve", lambda e, j=j, m=m: e.tensor_tensor(out=xs[0:m, j, :], in0=xs[0:m, j, :],
                                                          in1=ln2gb[0:m, :], op=ALU.mult),
               [*RX(j), "ln2gb"], [*RX(j)])
            op("dve", lambda e, j=j, m=m: e.tensor_tensor(out=xs[0:m, j, :], in0=xs[0:m, j, :],
                                                          in1=ln2bb[0:m, :], op=ALU.add),
               [*RX(j), "ln2bb"], [*RX(j)])
            Pg.dma("pool", y_o[t0 + off:t0 + off + m, :], xs[0:m, j, :], ("xs", par, j), reads=RX(j))
        return next_par[0]

    def init_seq(P_, is_sample):
        op("dve", lambda e: e.memset(carry[:, :], 0.0), [], ["carry"])
        if not is_sample:
            for c in range(4):
                op("pool", lambda e, c=c: e.memset(gluX[:, c, 0:CK - 1], 0.0), [], [("gluXh", c)])
            op("pool", lambda e: e.memset(hist[:, :, :], 0.0), [], [("hist", i) for i in range(NFF)])
            return
        Pg.dma("sp", cvst[0:CK - 1, :], st_conv, "cvst", writes=["cvst", ("attn_tm", 0)])
        b = bank()
        for c in range(4):
            op("pe", lambda e, c=c, b=b: e.transpose(psb[b][:, c * 32:c * 32 + CK - 1],
                                                     cvst[0:CK - 1, c * 128:(c + 1) * 128],
                                                     ident[0:CK - 1, 0:CK - 1]), ["cvst", ("attn_tm", 0), "ident"], [pr_(b)])
        for c in range(4):
            op("dve", lambda e, c=c, b=b: e.tensor_copy(gluX[:, c, 0:CK - 1], psb[b][:, c * 32:c * 32 + CK - 1]),
               [pr_(b)], [("gluXh", c)])
        Pg.dma("sp", st44[:, :], st_ffn.rearrange("r (i p) -> (r i) p", p=128), "st44", writes=["st44"])
        b = bank()
        op("pe", lambda e, b=b: e.transpose(psb[b][:, 0:44], st44[:, :], ident[0:44, 0:44]),
           ["st44", "ident"], [pr_(b)])
        op("dve", lambda e, b=b: e.tensor_copy(hist[:, :, :], psb[b][:, 0:44].rearrange("p (r i) -> p i r", i=NFF)),
           [pr_(b)], [("hist", i) for i in range(NFF)])
        nblk = P_ // 128
        for blk in range(nblk):
            kx = kst[blk % 2]
            vx = vst[blk % 2]
            Pg.dma("sp", kx[:, :], cache_k[blk * 128:(blk + 1) * 128, :], ("kstl", blk % 2),
                   writes=[("kst", blk % 2)])
            Pg.dma("sp", vx[:, :], cache_v[blk * 128:(blk + 1) * 128, :], ("vstl", blk % 2),
                   writes=[("vst", blk % 2)])
            for g in range(2):
                b = bank()
                for hl in range(4):
                    h = g * 4 + hl
                    op("pe", lambda e, h=h, hl=hl, b=b, kx=kx: e.transpose(
                        psb[b][0:64, hl * 128:(hl + 1) * 128], kx[:, h * 64:(h + 1) * 64], ident[:, :]),
                       [("kst", blk % 2), "ident"], [pr_(b)])
                op("act", lambda e, g=g, b=b, blk=blk: e.copy(
                    KA[0:64, g * 4:(g + 1) * 4, blk * 128:(blk + 1) * 128],
                    psb[b][0:64, :].rearrange("p (h k) -> p h k", k=128)),
                   [pr_(b)], [("KA", h, blk) for h in range(g * 4, g * 4 + 4)])
            op("dve", lambda e, blk=blk, vx=vx: e.tensor_copy(
                VA[:, blk, :, 0:64], vx[:, :].rearrange("p (h d) -> p h d", d=64)),
               [("vst", blk % 2)], [("VA", blk)])
        for ch in range(0, P_, TT):
            nn = min(TT, P_ - ch)
            b = bank()
            for (j, off, m) in subs_of(nn):
                Pg.dma("sp", lc[0:m, j % 2, :], cache_lf[ch + off:ch + off + m, :], ("lc", j % 2),
                       writes=[("lc", j % 2)])
                op("pe", lambda e, j=j, off=off, m=m, b=b: e.transpose(psb[b][0:8, off:off + m], lc[0:m, j % 2, :],
                                                                      ident[0:m, 0:m]),
                   [("lc", j % 2), "ident"], [pr_(b)])
            op("act", lambda e, b=b, nn=nn: e.mul(l_sb[:, 0:nn], psb[b][0:8, 0:nn], -1.0), [pr_(b)], ["l_sb"])
            append_G(nn, ch)

    tiles = []
    for si in range(NSEQ):
        is_sample = (si == NSEQ - 1)
        if is_sample:
            T_, P_ = TS, PAST
            x_src = xsm
            outs = (y_s, k_s, v_s, lf_s, cv_s, ff_s)
        else:
            T_, P_ = SEQ, 0
            x_src = xp[si]
            outs = (y_p[si], k_p[si], v_p[si], lf_p[si], cv_p[si], ff_p[si])
        t0 = 0
        while t0 < T_:
            n = min(TT, T_ - t0)
            tiles.append(dict(si=si, x_src=x_src, t0=t0, n=n, kpos=P_ + t0, outs=outs, last=(t0 + n >= T_),
                              first=(t0 == 0), P=P_, is_sample=is_sample))
            t0 += n
    par = None
    for ti, T in enumerate(tiles):
        if T["first"]:
            init_seq(T["P"], T["is_sample"])
        if par is None:
            par = tile_head(T["si"], T["x_src"], T["t0"], T["n"])
        nh = None
        if ti + 1 < len(tiles):
            N_ = tiles[ti + 1]
            nh = (lambda N_=N_: tile_head(N_["si"], N_["x_src"], N_["t0"], N_["n"]))
        par = tile(T["si"], T["x_src"], T["t0"], T["n"], T["kpos"], T["outs"], T["last"], par, nh)

    Pg.emit(stack)
    stack.close()
    return nc


_CONSTS = None


def _consts():
    ident = np.eye(128, dtype=np.float32)
    mask = np.triu(np.ones((128, 128), dtype=np.float32))
    return ident, mask


def make_in_maps(inputs, NPS, n_cores):
    ident, mask = _consts()
    f = lambda a: np.ascontiguousarray(np.asarray(a, dtype=np.float32))
    maps = []
    for c in range(n_cores):
        m = {}
        m["xp"] = f(inputs["x_prompt"][c * NPS:(c + 1) * NPS])
        m["xsm"] = f(inputs["x_sample"][c])
        m["c_all"] = f(np.concatenate([inputs["c_prompt"][c * NPS:(c + 1) * NPS],
                                       inputs["c_sample"][c:c + 1]], axis=0))
        P = inputs["cache_k"].shape[2]
        m["cache_k"] = f(inputs["cache_k"][0, c].reshape(P, 512))
        m["cache_v"] = f(inputs["cache_v"][0, c].reshape(P, 512))
        m["cache_logf"] = f(inputs["cache_logf"][0, c])
        m["state_conv"] = f(inputs["state_conv"][0, c])
        m["state_ffn"] = f(inputs["state_ffn_conv"][0, c])
        for nm in ("w_ada", "w_in", "w_attn_proj", "w_conv_proj", "w_out", "w_up", "w_down", "b_ada", "b_f",
                   "conv_w", "conv_b", "conv_ln_g", "conv_ln_b", "ln1_g", "ln1_b", "ffn_conv_w", "ffn_conv_b",
                   "ln2_g", "ln2_b"):
            m[nm] = f(inputs[nm][0])
        m["ident"] = ident
        m["mask"] = mask
        maps.append(m)
    return maps


def gather(results, NPS, SEQ, TS):
    cat = lambda k: np.concatenate([r[k] for r in results], axis=0)
    stk = lambda k: np.stack([r[k] for r in results], axis=0)
    nb = len(results) * NPS
    y_p = cat("y_p")
    y_s = stk("y_s")
    k_p = cat("k_p").reshape(1, nb, SEQ, H, DH)
    v_p = cat("v_p").reshape(1, nb, SEQ, H, DH)
    lf_p = cat("lf_p").reshape(1, nb, SEQ, H)
    cv_p = cat("cv_p").reshape(1, nb, CK - 1, 512)
    ff_p = cat("ff_p").reshape(1, nb, 2, DFF)
    ns = len(results)
    k_s = stk("k_s").reshape(1, ns, TS, H, DH)
    v_s = stk("v_s").reshape(1, ns, TS, H, DH)
    lf_s = stk("lf_s").reshape(1, ns, TS, H)
    cv_s = stk("cv_s").reshape(1, ns, CK - 1, 512)
    ff_s = stk("ff_s").reshape(1, ns, 2, DFF)
    return tuple(np.ascontiguousarray(a, dtype=np.float32) for a in
                 (y_p, y_s, k_p, v_p, lf_p, cv_p, ff_p, k_s, v_s, lf_s, cv_s, ff_s))


def kernel(**inputs):
    inputs = {k: np.asarray(v) for k, v in inputs.items()}
    B, SEQ, _ = inputs["x_prompt"].shape
    TS = inputs["x_sample"].shape[1]
    PAST = inputs["cache_k"].shape[2]
    NPS = B // N_CORES
    nc = build_nc(NPS, SEQ, TS, PAST, 256)
    in_maps = make_in_maps(inputs, NPS, N_CORES)
    res = run_bass_kernel_spmd(nc, in_maps, core_ids=list(range(N_CORES)))
    return gather(res.results, NPS, SEQ, TS)
```

```python
import numpy as np
import concourse.bass as bass
import concourse.mybir as mybir
from concourse.bass_utils import run_bass_kernel_spmd

F32 = mybir.dt.float32
BF16 = mybir.dt.bfloat16
AF = mybir.ActivationFunctionType
ALU = mybir.AluOpType

D = 1024
H = 8
DH = 64
CK = 31
DFF = 2816
NFF = 22
ALPHA = float(2 ** 0.25)
EPS = 1e-5
N_CORES = 8


class Op:
    __slots__ = ("eng", "fn", "deps", "marked", "mark_idx", "chan", "target")


class Prog:
    ENGS = ("pe", "act", "dve", "pool", "sp")

    def __init__(self, nc, same_engine_sync=True):
        self.nc = nc
        self.ops = {e: [] for e in self.ENGS}
        self.last_w = {}
        self.readers = {}
        self.chan_cnt = {}
        self.same = same_engine_sync
        self.all_dma = []

    def _dep(self, o, reads, writes):
        deps = set()
        for r in reads:
            w = self.last_w.get(r)
            if w is not None:
                deps.add(w)
        for r in writes:
            w = self.last_w.get(r)
            if w is not None:
                deps.add(w)
            for rd in self.readers.get(r, ()):
                deps.add(rd)
        deps.discard(o)
        o.deps = list(deps)
        for r in reads:
            self.readers.setdefault(r, []).append(o)
        for r in writes:
            self.last_w[r] = o
            self.readers[r] = []

    def op(self, eng, fn, reads=(), writes=()):
        o = Op()
        o.eng = eng
        o.fn = fn
        o.marked = False
        o.mark_idx = 0
        o.chan = None
        o.target = 0
        self._dep(o, reads, writes)
        self.ops[eng].append(o)
        return o

    def dma(self, eng, out, in_, chan, reads=(), writes=()):
        return self.dma_multi(eng, [(out, in_)], chan, reads, writes)

    def dma_multi(self, eng, pairs, chan, reads=(), writes=()):
        o = Op()
        o.eng = eng
        o.fn = list(pairs)
        o.marked = False
        o.mark_idx = 0
        chan = (eng, chan)
        o.chan = chan
        self.chan_cnt[chan] = self.chan_cnt.get(chan, 0) + 16 * len(pairs)
        o.target = self.chan_cnt[chan]
        self._dep(o, reads, writes)
        self.ops[eng].append(o)
        self.all_dma.append(o)
        return o

    def emit(self, stack):
        nc = self.nc
        for e in self.ENGS:
            for o in self.ops[e]:
                for d in o.deps:
                    d.marked = True
        for e in self.ENGS:
            c = 0
            for o in self.ops[e]:
                if o.chan is None and o.marked:
                    c += 1
                    o.mark_idx = c
        esem = {e: stack.enter_context(nc.semaphore("s_" + e)) for e in self.ENGS}
        csem = {}
        for i, ch in enumerate(self.chan_cnt):
            csem[ch] = stack.enter_context(nc.semaphore("c%d" % i))
        finals = [(csem[ch], cnt) for ch, cnt in self.chan_cnt.items()]

        def run(e, eng):
            waited = {}
            for o in self.ops[e]:
                for d in o.deps:
                    if d.chan is not None:
                        key = ("c", d.chan)
                        val = d.target
                        sem = csem[d.chan]
                    else:
                        if d.eng == e and (e == "pe" or not self.same):
                            continue
                        key = d.eng
                        val = d.mark_idx
                        sem = esem[d.eng]
                    if waited.get(key, 0) >= val:
                        continue
                    eng.wait_ge(sem, val)
                    waited[key] = val
                if o.chan is not None:
                    for (do, di) in o.fn:
                        eng.dma_start(out=do, in_=di).then_inc(csem[o.chan], 16)
                    continue
                ins = o.fn(eng)
                if o.marked:
                    ins.then_inc(esem[e], 1)
            if e == "sp":
                for sem, cnt in finals:
                    eng.wait_ge(sem, cnt)

        block = stack.enter_context(nc.Block())

        @block.tensor
        def _(eng):
            run("pe", eng)

        @block.scalar
        def _(eng):
            run("act", eng)

        @block.vector
        def _(eng):
            run("dve", eng)

        @block.gpsimd
        def _(eng):
            run("pool", eng)

        @block.sync
        def _(eng):
            run("sp", eng)


def make_pieces():
    P = {}
    order = []

    def add(name, W, entries):
        P[name] = (W, entries)
        order.append(name)

    for i, c0 in enumerate((0, 256)):
        add("q%d" % i, 256, [("w_in", kc, c0) for kc in range(8)])
    for i, c0 in enumerate((512, 768)):
        add("k%d" % i, 256, [("w_in", kc, c0) for kc in range(8)])
    for i, c0 in enumerate((1024, 1280)):
        add("v%d" % i, 256, [("w_in", kc, c0) for kc in range(8)])
    add("f", 8, [("w_in", kc, 1536) for kc in range(8)])
    for cc in range(2):
        add("ga%d" % cc, 256, [("w_in", kc, 1544 + cc * 256) for kc in range(8)])
        add("gb%d" % cc, 256, [("w_in", kc, 2056 + cc * 256) for kc in range(8)])
    for d in range(8):
        add("gA%d" % d, 128, [("w_in", kc, 2568 + d * 128) for kc in range(8)])
        add("gB%d" % d, 128, [("w_in", kc, 3592 + d * 128) for kc in range(8)])
        add("pr%d" % d, 128, [("w_attn_proj", kc, d * 128) for kc in range(4)]
            + [("w_conv_proj", kc, d * 128) for kc in range(4)])
    for d in range(8):
        add("wo%d" % d, 128, [("w_out", kc, d * 128) for kc in range(8)])
    for i in range(NFF):
        add("upa%d" % i, 128, [("w_up", kc, i * 128) for kc in range(8)])
        add("upv%d" % i, 128, [("w_up", kc, DFF + i * 128) for kc in range(8)])
    for d in range(8):
        for part, (i0, i1) in enumerate(((0, 8), (8, 16), (16, 22))):
            add("dn%d_%d" % (d, part), 128, [("w_down", i, d * 128) for i in range(i0, i1)])
    return P, order


class _Stop(Exception):
    pass


def build_nc(NPS, SEQ, TS, PAST, TT, stop_after=None):
    try:
        return _build_nc(NPS, SEQ, TS, PAST, TT, stop_after)
    except _Stop as ex:
        return ex.args[0]


def _build_nc(NPS, SEQ, TS, PAST, TT, stop_after=None):
    NSEQ = NPS + 1
    NK = max(SEQ, PAST + TS)
    NB = (NK + 127) // 128
    NKP = NB * 128
    nc = bass.Bass("TRN2", target_bir_lowering=False)
    import contextlib
    stack = contextlib.ExitStack()

    def din(name, shape):
        return nc.dram_tensor(name, list(shape), F32, kind="ExternalInput").ap()

    def dout(name, shape):
        return nc.dram_tensor(name, list(shape), F32, kind="ExternalOutput").ap()

    xp = din("xp", (NPS, SEQ, D))
    xsm = din("xsm", (TS, D))
    call = din("c_all", (NSEQ, D))
    cache_k = din("cache_k", (PAST, 512))
    cache_v = din("cache_v", (PAST, 512))
    cache_lf = din("cache_logf", (PAST, 8))
    st_conv = din("state_conv", (CK - 1, 512))
    st_ffn = din("state_ffn", (2, DFF))
    W = {}
    for name, shp in (("w_ada", (D, 6 * D)), ("w_in", (D, 4616)), ("w_attn_proj", (512, D)),
                      ("w_conv_proj", (512, D)), ("w_out", (D, D)), ("w_up", (D, 2 * DFF)),
                      ("w_down", (DFF, D))):
        W[name] = din(name, shp)
    b_ada = din("b_ada", (6 * D,))
    b_f = din("b_f", (8,))
    conv_w = din("conv_w", (CK, 512))
    conv_b = din("conv_b", (512,))
    conv_ln_g = din("conv_ln_g", (512,))
    conv_ln_b = din("conv_ln_b", (512,))
    ln1_g = din("ln1_g", (D,))
    ln1_b = din("ln1_b", (D,))
    ffn_conv_w = din("ffn_conv_w", (3, DFF))
    ffn_conv_b = din("ffn_conv_b", (DFF,))
    ln2_g = din("ln2_g", (D,))
    ln2_b = din("ln2_b", (D,))
    ident_d = din("ident", (128, 128))
    mask_d = din("mask", (128, 128))

    y_p = dout("y_p", (NPS, SEQ, D))
    y_s = dout("y_s", (TS, D))
    k_p = dout("k_p", (NPS, SEQ, 512))
    v_p = dout("v_p", (NPS, SEQ, 512))
    lf_p = dout("lf_p", (NPS, SEQ, 8))
    cv_p = dout("cv_p", (NPS, CK - 1, 512))
    ff_p = dout("ff_p", (NPS, 2, DFF))
    k_s = dout("k_s", (TS, 512))
    v_s = dout("v_s", (TS, 512))
    lf_s = dout("lf_s", (TS, 8))
    cv_s = dout("cv_s", (CK - 1, 512))
    ff_s = dout("ff_s", (2, DFF))

    pieces, porder = make_pieces()
    pidx = {n: i for i, n in enumerate(porder)}
    wscr = nc.dram_tensor("wscr", [len(porder), 128, 2048], BF16, kind="Internal").ap()

    def sb(name, shape, dt=F32):
        return stack.enter_context(nc.sbuf_tensor("sb_" + name, list(shape), dt))

    KA = sb("KA", (128, H, NK), BF16)
    VA = sb("VA", (128, NB, H, 66), BF16)
    Gk = sb("Gk", (128, NB, H))
    NSUB = (TT + 127) // 128
    bufs = [sb("bufA", (128, 2048)), sb("bufB", (128, 2048))]
    xs_views = [bb[:, 0:NSUB * D].rearrange("p (a b) -> p a b", b=D) for bb in bufs]
    x1a_views = [bb[:, 0:8 * TT].rearrange("p (k t) -> p k t", t=TT) for bb in bufs]
    uT = sb("uT", (128, 8, TT), BF16)
    QA = sb("QA", (128, H, TT), BF16)
    NPT = 4
    PT = [sb("PT%d" % i, (128, TT), BF16) for i in range(NPT)]
    attn_tm = sb("attn_tm", (128, NSUB, 512))
    attnT = sb("attnT", (128, 4, TT), BF16)
    gluX = sb("gluX", (128, 4, CK - 1 + TT))
    R1 = sb("R1", (128, NFF * TT), BF16)
    hT = R1[:, :].rearrange("p (i t) -> p i t", t=TT)
    R1f = R1[:, :].bitcast(F32)
    hc = R1f[:, 0:4 * TT].rearrange("p (c t) -> p c t", t=TT)
    lnm = R1f[:, 4 * TT:5 * TT]
    lnv = R1f[:, 5 * TT:6 * TT]
    lnr = R1f[:, 6 * TT:7 * TT]
    hcb = R1[:, 14 * TT:18 * TT].rearrange("p (c t) -> p c t", t=TT)
    sqb = R1[:, 18 * TT:22 * TT].rearrange("p (c t) -> p c t", t=TT)
    sg = [sb("sg%d" % i, (128, TT)) for i in range(4)]
    t1 = [sb("t1_%d" % i, (128, TT)) for i in range(2)]
    mergedT = sb("mergedT", (128, 8, TT), BF16)
    tmpf = [sb("tmpf%d" % i, (128, TT)) for i in range(2)]
    a_sb = [sb("a_sb%d" % i, (128, TT + 2)) for i in range(2)]
    cbuf = [sb("cbuf%d" % i, (128, TT)) for i in range(2)]
    sbuf_s = [sb("sil%d" % i, (128, TT)) for i in range(2)]
    NUNIT = 8
    ring = sb("ring", (128, NUNIT * 1024), BF16)
    ln2gb = sb("ln2gb", (128, D))
    ln2bb = sb("ln2bb", (128, D))
    kst = [sb("kst%d" % i, (128, 512)) for i in range(2)]
    vst = [sb("vst%d" % i, (128, 512)) for i in range(2)]
    ident = sb("ident", (128, 128))
    maskb = sb("maskb", (128, 128), BF16)
    maskf = sb("maskf", (128, 128))
    onesb = sb("onesb", (128, 128), BF16)
    zerob = sb("zerob", (128, 160), BF16)
    ones8 = sb("ones8", (8, TT))
    l_sb = sb("l_sb", (8, TT))
    e_sb = sb("e_sb", (8, TT))
    Gt = sb("Gt", (8, TT))
    Gnb = sb("Gnb", (8, TT), BF16)
    carry = sb("carry", (8, 1))
    negbf = sb("negbf", (8, 1))
    bf_sb = sb("bf_sb", (8, 1))
    lstage = sb("lstage", (128, NSUB, 8))
    VS = [sg[i][:, 0:128] for i in range(3)]
    VT = [sb("VT%d" % i, (128, 128)) for i in range(3)]
    modT = sb("modT", (128, NSEQ, 48))
    cT = sb("cT", (128, 8, NSEQ))
    sc1p = sb("sc1p", (128, NSEQ, 8))
    G2 = sb("G2", (128, NSEQ, 8))
    B2 = sb("B2", (128, NSEQ, 8))
    AG = sb("AG", (128, 8))
    AB = sb("AB", (128, 8))
    hist = sb("hist", (128, NFF, 2))
    hist44 = sb("hist44", (128, 44))
    st44 = sb("st44", (44, 128))
    cvst = attn_tm[0:32, 0, :]
    stat = sb("stat", (128, NSUB, 2, 6))
    mv = sb("mv", (128, NSUB, 2))
    rstd = sb("rstd", (128, NSUB, 1))
    nbias = sb("nbias", (128, NSUB, 1))
    rc = sb("rc", (128, NSUB, 1))
    lc = sb("lc", (128, 2, 8))

    psb = [stack.enter_context(nc.psum_tensor("ps%d" % i, [128, 512], F32)) for i in range(8)]

    Pg = Prog(nc)
    op = Pg.op

    def ckpt(k):
        if stop_after is not None and k >= stop_after:
            Pg.emit(stack)
            stack.close()
            raise _Stop(nc)

    bank_ctr = [0]
    held = set()

    def bank(hold=False):
        while True:
            b = bank_ctr[0] % 6
            bank_ctr[0] += 1
            if b not in held:
                break
        if hold:
            held.add(b)
        return b

    def pr_(b):
        return ("ps", b)

    unit_ctr = [0]

    def alloc_units(nelem):
        nu = 1 if nelem <= 1024 else 2
        if nu == 2 and unit_ctr[0] % 2 == 1:
            unit_ctr[0] += 1
        u = unit_ctr[0] % NUNIT
        unit_ctr[0] += nu
        return u, [("unit", u + k) for k in range(nu)]

    def load_piece(name):
        Wd, ents = pieces[name]
        nn_ = len(ents) * Wd
        u, rs = alloc_units(nn_)
        Pg.dma("sp", ring[:, u * 1024:u * 1024 + nn_], wscr[pidx[name]][:, 0:nn_], ("unit", u),
               reads=[("wscr", name)], writes=rs)
        view = ring[:, u * 1024:u * 1024 + nn_].rearrange("p (e w) -> p e w", w=Wd)
        return rs, view

    Pg.dma("sp", ident[:, :], ident_d, "c_ident", writes=["ident"])
    Pg.dma("sp", maskf[:, :], mask_d, "c_mask", writes=["maskf"])
    op("dve", lambda e: e.tensor_copy(maskb[:, :], maskf[:, :]), ["maskf"], ["maskb"])
    op("dve", lambda e: e.memset(onesb[:, :], 1.0), [], ["onesb"])
    op("dve", lambda e: e.memset(zerob[:, :], 0.0), [], ["zerob"])
    op("dve", lambda e: e.memset(ones8[:, :], 1.0), [], ["ones8"])
    op("pool", lambda e: e.memset(KA[64:128, :, :].rearrange("p a b -> p (a b)"), 0.0), [], [("KA1",)])
    op("pool", lambda e: e.memset(KA[64:65, :, :].rearrange("p a b -> p (a b)"), 1.0), [], [("KA1",)])
    op("pool", lambda e: e.memset(QA[64:128, :, :].rearrange("p a b -> p (a b)"), 0.0), [],
       [("QAg", h) for h in range(H)])
    Pg.dma("sp", ln2gb[:, :], ln2_g.partition_broadcast(128), "c_l2g", writes=["ln2gb"])
    Pg.dma("sp", ln2bb[:, :], ln2_b.partition_broadcast(128), "c_l2b", writes=["ln2bb"])
    Pg.dma("sp", bf_sb[:, :], b_f.rearrange("(a b) -> a b", b=1), "c_bf", writes=["bf_sb"])
    op("act", lambda e: e.mul(negbf[:, :], bf_sb[:, :], -1.0), ["bf_sb"], ["negbf"])

    Pg.dma("sp", VS[0][0:124, :], conv_w.rearrange("j (c p) -> (j c) p", p=128), "c_vs0",
           writes=[("VS", 0)])
    r = 0
    VS1map = {}
    for nm, ap_, n in (("conv_b", conv_b, 4), ("conv_ln_g", conv_ln_g, 4), ("conv_ln_b", conv_ln_b, 4),
                       ("ln1_g", ln1_g, 8), ("ln1_b", ln1_b, 8), ("ffn_conv_b", ffn_conv_b, NFF)):
        Pg.dma("sp", VS[1][r:r + n, :], ap_.rearrange("(a b) -> a b", b=128), "c_vs1_" + nm,
               writes=[("VS", 1)])
        VS1map[nm] = r
        r += n
    VS1map["c"] = r
    for kc in range(8):
        Pg.dma("sp", VS[1][r + kc * NSEQ:r + (kc + 1) * NSEQ, :], call[:, kc * 128:(kc + 1) * 128],
               "c_vs1_c%d" % kc, writes=[("VS", 1)])
    r1rows = r + 8 * NSEQ
    Pg.dma("sp", VS[2][0:48, :], b_ada.rearrange("(a b) -> a b", b=128), "c_vs2a", writes=[("VS", 2)])
    Pg.dma("sp", VS[2][48:48 + 66, :], ffn_conv_w.rearrange("j (i p) -> (j i) p", p=128), "c_vs2b",
           writes=[("VS", 2)])
    for i, nrows in ((0, 124), (1, r1rows), (2, 114)):
        b = bank()
        op("pe", lambda e, i=i, nrows=nrows, b=b: e.transpose(psb[b][:, 0:nrows], VS[i][0:nrows, :],
                                                              ident[0:nrows, 0:nrows]),
           [("VS", i), "ident"], [pr_(b)])
        op("dve", lambda e, i=i, nrows=nrows, b=b: e.tensor_copy(VT[i][:, 0:nrows], psb[b][:, 0:nrows]),
           [pr_(b)], [("VT", i)])

    def cw(c, j):
        return VT[0][:, j * 4 + c:j * 4 + c + 1]

    def v1(nm, i):
        return VT[1][:, VS1map[nm] + i:VS1map[nm] + i + 1]

    def fcw(j, i):
        return VT[2][:, 48 + j * NFF + i:48 + j * NFF + i + 1]

    cb0 = VS1map["c"]
    op("dve", lambda e: e.tensor_copy(cT[:, :, :].rearrange("p k s -> p (k s)"),
                                      VT[1][:, cb0:cb0 + 8 * NSEQ]), [("VT", 1)], ["cT"])

    ckpt(1)
    VAf = VA[:, :, :, :].rearrange("p a b c -> p (a b c)").bitcast(F32)
    R1_NAMES = ([("hc", c) for c in range(4)] + [("hcs", c) for c in range(4)]
                + ["hcb", "sqb", "lnm", "lnv", "lnr"] + [("hT", i) for i in range(NFF)])
    stg = [bufs[0][:, :], bufs[1][:, :], R1f[:, 0:2048]]
    stg_alias = [[("B", 0, q) for q in range(8)], [("B", 1, q) for q in range(8)], R1_NAMES]
    if NB * H * 66 // 2 >= 4096:
        stg += [VAf[:, 0:2048], VAf[:, 2048:4096]]
        stg_alias += [[("VA", b) for b in range(NB)], [("VA", b) for b in range(NB)]]
    NSTG = len(stg)

    def stage_fence(sgi):
        nm = stg_alias[sgi] + [("stg", sgi, ei) for ei in range(24)]
        op("sp", lambda e: e.nop(), [], nm)

    for sgi in range(NSTG):
        stage_fence(sgi)
    bm = bank(hold=True)
    jobs = []
    mod_it = iter(range(24))
    for pi, name in enumerate(porder):
        jobs.append(("conv", name))
        if pi % 4 == 3:
            pc = next(mod_it, None)
            if pc is not None:
                jobs.append(("mod", pc))
    for pc in mod_it:
        jobs.append(("mod", pc))
    ncv = 0
    for ji, (kind, arg) in enumerate(jobs):
        sgi = ji % NSTG
        if kind == "conv":
            name = arg
            Wd, ents = pieces[name]
            Pg.dma_multi("sp", [(stg[sgi][:, ei * Wd:(ei + 1) * Wd],
                                 W[mname][rcx * 128:(rcx + 1) * 128, c0:c0 + Wd])
                                for ei, (mname, rcx, c0) in enumerate(ents)], ("stg", sgi),
                         writes=[("stg", sgi, ei) for ei in range(len(ents))])
            n = len(ents) * Wd
            u, rs = alloc_units(n)
            rd = [("stg", sgi, ei) for ei in range(len(ents))]
            if ncv % 2 == 0:
                op("dve", lambda e, u=u, sgi=sgi, n=n: e.tensor_copy(ring[:, u * 1024:u * 1024 + n],
                                                                     stg[sgi][:, 0:n]), rd, rs)
            else:
                op("act", lambda e, u=u, sgi=sgi, n=n: e.copy(ring[:, u * 1024:u * 1024 + n], stg[sgi][:, 0:n]),
                   rd, rs)
            ncv += 1
            Pg.dma("pool", wscr[pidx[name]][:, 0:n], ring[:, u * 1024:u * 1024 + n], ("unitst", u),
                   reads=rs, writes=[("wscr", name)])
        else:
            pc = arg
            Pg.dma_multi("sp", [(stg[sgi][:, kc * 256:(kc + 1) * 256],
                                 W["w_ada"][kc * 128:(kc + 1) * 128, pc * 256:(pc + 1) * 256])
                                for kc in range(8)], ("stg", sgi), writes=[("stg", sgi, kc) for kc in range(8)])
            for mm_ in range(2):
                m = pc * 2 + mm_
                for kc in range(8):
                    op("pe", lambda e, sgi=sgi, kc=kc, mm_=mm_, m=m: e.matmul(
                        psb[bm][:, m:m + 48 * (NSEQ - 1) + 1:48],
                        stg[sgi][:, kc * 256 + mm_ * 128:kc * 256 + (mm_ + 1) * 128],
                        cT[:, kc, :], start=(kc == 0), stop=(kc == 7)),
                       [("stg", sgi, kc), "cT"], [pr_(bm)])
    ckpt(2)
    for s in range(NSEQ):
        op("dve", lambda e, s=s: e.tensor_tensor(out=modT[:, s, :], in0=psb[bm][:, s * 48:(s + 1) * 48],
                                                 in1=VT[2][:, 0:48], op=ALU.add),
           [pr_(bm), ("VT", 2)], ["modT"])
    held.discard(bm)
    for sgi in range(NSTG):
        stage_fence(sgi)
    op("pool", lambda e: e.memset(VA[:, :, :, :].rearrange("p a b c -> p (a b c)"), 1.0), [],
       [("VA", b) for b in range(NB)])
    g0 = VS1map["ln1_g"]
    b0 = VS1map["ln1_b"]
    for s in range(NSEQ):
        op("dve", lambda e, s=s: e.tensor_scalar_add(sc1p[:, s, :], modT[:, s, 8:16], 1.0), ["modT"], ["sc1p"])
        op("dve", lambda e, s=s: e.tensor_scalar_add(G2[:, s, :], modT[:, s, 32:40], 1.0), ["modT"], ["G2"])
        op("dve", lambda e, s=s: e.tensor_tensor(out=B2[:, s, :], in0=G2[:, s, :], in1=VT[1][:, b0:b0 + 8],
                                                 op=ALU.mult), ["G2", ("VT", 1)], ["B2"])
        op("dve", lambda e, s=s: e.tensor_tensor(out=B2[:, s, :], in0=B2[:, s, :], in1=modT[:, s, 24:32],
                                                 op=ALU.add), ["B2", "modT"], ["B2"])
        op("dve", lambda e, s=s: e.tensor_tensor(out=G2[:, s, :], in0=G2[:, s, :], in1=VT[1][:, g0:g0 + 8],
                                                 op=ALU.mult), ["G2", ("VT", 1)], ["G2"])
    op("dve", lambda e: e.tensor_scalar_mul(AG[:, :], VT[1][:, g0:g0 + 8], ALPHA), [("VT", 1)], ["AG"])
    op("dve", lambda e: e.tensor_scalar_mul(AB[:, :], VT[1][:, b0:b0 + 8], ALPHA), [("VT", 1)], ["AB"])

    ckpt(3)
    fdummy = sb("fdummy", (1, 8))

    def r1_fence():
        op("dve", lambda e: e.memset(fdummy[:, :], 0.0), [], R1_NAMES + ["fdummy"])

    def subs_of(n):
        out = []
        off = 0
        j = 0
        while off < n:
            m = min(128, n - off)
            out.append((j, off, m))
            off += m
            j += 1
        return out

    def append_G(n, kpos, lf_out=None, to_QA=False):
        append_G1(n, to_QA)
        append_G2(n, kpos, lf_out)

    def append_G1(n, to_QA):
        op("dve", lambda e: e.tensor_tensor_scan(out=Gt[:, 0:n], data0=ones8[:, 0:n], data1=l_sb[:, 0:n],
                                                 initial=carry[:, 0:1], op0=ALU.mult, op1=ALU.add),
           ["l_sb", "ones8", "carry"], ["Gt"])
        op("dve", lambda e: e.tensor_copy(carry[:, 0:1], Gt[:, n - 1:n]), ["Gt"], ["carry"])
        if to_QA:
            op("act", lambda e: e.mul(Gnb[:, 0:n], Gt[:, 0:n], -1.0), ["Gt"], ["Gnb"])
            for h in range(H):
                Pg.dma("pool", QA[64:65, h, 0:n], Gnb[h:h + 1, 0:n], ("qag", h), reads=["Gnb"],
                       writes=[("QAg", h)])

    def append_G2(n, kpos, lf_out):
        b = bank()
        sl = subs_of(n)
        for (j, off, m) in sl:
            if lf_out is not None:
                op("pe", lambda e, j=j, off=off, m=m: e.transpose(psb[b][0:m, j * 16:j * 16 + 8],
                                                                  l_sb[0:8, off:off + m], ident[0:8, 0:8]),
                   ["l_sb", "ident"], [pr_(b)])
            op("pe", lambda e, j=j, off=off, m=m: e.transpose(psb[b][0:m, j * 16 + 8:j * 16 + 16],
                                                              Gt[0:8, off:off + m], ident[0:8, 0:8]),
               ["Gt", "ident"], [pr_(b)])
        for (j, off, m) in sl:
            blk = (kpos + off) // 128
            if lf_out is not None:
                op("act", lambda e, j=j, m=m: e.mul(lstage[0:m, j, :], psb[b][0:m, j * 16:j * 16 + 8], -1.0),
                   [pr_(b)], [("lstage", j)])
                Pg.dma("pool", lf_out[off:off + m, :], lstage[0:m, j, :], ("lst", j), reads=[("lstage", j)])
            op("act", lambda e, j=j, m=m, blk=blk: e.copy(Gk[0:m, blk, :],
                                                          psb[b][0:m, j * 16 + 8:j * 16 + 16]),
               [pr_(b)], [("Gk", blk)])

    def layer_norm_tile(sl, xs, RX):
        for (j, off, m) in sl:
            for hh in range(2):
                op("dve", lambda e, hh=hh, j=j, m=m: e.bn_stats(stat[0:m, j, hh, :],
                                                                xs[0:m, j, hh * 512:(hh + 1) * 512]),
                   [*RX(j)], [("stat", j)])
        for (j, off, m) in sl:
            op("dve", lambda e, j=j, m=m: e.bn_aggr(mv[0:m, j, :], stat[0:m, j, :, :].rearrange("p a b -> p (a b)")),
               [("stat", j)], [("mv", j)])
            op("dve", lambda e, j=j, m=m: e.tensor_scalar_add(rstd[0:m, j, :], mv[0:m, j, 1:2], EPS),
               [("mv", j)], [("rstd", j)])
        for (j, off, m) in sl:
            op("act", lambda e, j=j, m=m: e.sqrt(rstd[0:m, j, :], rstd[0:m, j, :]), [("rstd", j)], [("rstd", j)])
        for (j, off, m) in sl:
            op("dve", lambda e, j=j, m=m: e.reciprocal(rstd[0:m, j, :], rstd[0:m, j, :]),
               [("rstd", j)], [("rstd", j)])
            op("dve", lambda e, j=j, m=m: e.scalar_tensor_tensor(out=nbias[0:m, j, :], in0=mv[0:m, j, 0:1],
                                                                 scalar=-1.0, in1=rstd[0:m, j, :], op0=ALU.mult,
                                                                 op1=ALU.mult),
               [("mv", j), ("rstd", j)], [("nbias", j)])
        for (j, off, m) in sl:
            op("act", lambda e, j=j, m=m: e.activation(xs[0:m, j, :], xs[0:m, j, :], AF.Identity,
                                                       bias=nbias[0:m, j, :], scale=rstd[0:m, j, :]),
               [*RX(j), ("rstd", j), ("nbias", j)], [*RX(j)])

    tile_ctr = [0]

    def tile_head(si, x_src, t0, n):
        sl = subs_of(n)
        par = tile_ctr[0] % 2
        tile_ctr[0] += 1
        xs = xs_views[par]

        def RX(j):
            return [("B", par, q) for q in range(4 * j, 4 * j + 4)]
        for (j, off, m) in sl:
            Pg.dma("sp", xs[0:m, j, :], x_src[t0 + off:t0 + off + m, :], ("xs", par, j), writes=RX(j))
        for kc in range(8):
            b = bank()
            for (j, off, m) in sl:
                op("pe", lambda e, kc=kc, j=j, off=off, m=m, b=b: e.transpose(
                    psb[b][:, off:off + m], xs[0:m, j, kc * 128:(kc + 1) * 128], ident[0:m, 0:m]),
                   [*RX(j), "ident"], [pr_(b)])
            op("act", lambda e, kc=kc, b=b: e.activation(uT[:, kc, 0:n], psb[b][:, 0:n], AF.Identity,
                                                         bias=modT[:, si, kc:kc + 1],
                                                         scale=sc1p[:, si, kc:kc + 1]),
               [pr_(b), "modT", "sc1p"], [("uT", kc)])
        return par

    def tile(si, x_src, t0, n, kpos, outs, last, par, next_head):
        (y_o, k_o, v_o, lf_o, cv_o, ff_o) = outs
        sl = subs_of(n)
        nsub = len(sl)
        xs = xs_views[par]
        x1a = x1a_views[1 - par]
        next_par = [None]

        def RX(j):
            return [("B", par, q) for q in range(4 * j, 4 * j + 4)]

        def RA(kc):
            return [("B", 1 - par, kc)]
        uT_all = [("uT", kc) for kc in range(8)]
        kblks = sorted(set((kpos + off) // 128 for (_, off, _) in sl))
        ckpt(13)
        s, wv = load_piece("f")
        b = bank()
        for kc in range(8):
            op("pe", lambda e, kc=kc, b=b, wv=wv: e.matmul(psb[b][0:8, 0:n], wv[:, kc, :], uT[:, kc, 0:n],
                                                           start=(kc == 0), stop=(kc == 7)),
               [*s, ("uT", kc)], [pr_(b)])
        op("act", lambda e, b=b: e.activation(e_sb[:, 0:n], psb[b][0:8, 0:n], AF.Exp, bias=negbf[:, 0:1],
                                              scale=-1.0), [pr_(b), "negbf"], ["e_sb"])
        op("act", lambda e: e.activation(l_sb[:, 0:n], e_sb[:, 0:n], AF.Ln, bias=1.0, scale=1.0),
           ["e_sb"], ["l_sb"])
        append_G1(n, True)
        ckpt(14)
        for cc in range(2):
            sa, wa = load_piece("ga%d" % cc)
            sb_, wb = load_piece("gb%d" % cc)
            for cl in range(2):
                c = cc * 2 + cl
                ba = bank()
                bb_ = bank()
                for kc in range(8):
                    op("pe", lambda e, kc=kc, cl=cl, ba=ba, wa=wa: e.matmul(
                        psb[ba][:, 0:n], wa[:, kc, cl * 128:(cl + 1) * 128], uT[:, kc, 0:n],
                        start=(kc == 0), stop=(kc == 7)), [*sa, ("uT", kc)], [pr_(ba)])
                for kc in range(8):
                    op("pe", lambda e, kc=kc, cl=cl, bb_=bb_, wb=wb: e.matmul(
                        psb[bb_][:, 0:n], wb[:, kc, cl * 128:(cl + 1) * 128], uT[:, kc, 0:n],
                        start=(kc == 0), stop=(kc == 7)), [*sb_, ("uT", kc)], [pr_(bb_)])
                g = sg[c % 2]
                op("act", lambda e, bb_=bb_, g=g: e.activation(g[:, 0:n], psb[bb_][:, 0:n], AF.Sigmoid),
                   [pr_(bb_)], [("sg", c % 2)])
                op("dve", lambda e, c=c, ba=ba, g=g: e.tensor_tensor(
                    out=gluX[:, c, CK - 1:CK - 1 + n], in0=psb[ba][:, 0:n], in1=g[:, 0:n], op=ALU.mult),
                   [pr_(ba), ("sg", c % 2)], [("gluXn", c)])
        ckpt(15)
        r1_fence()
        conv_thunks = []
        for c in range(4):
            conv_thunks.append((lambda e, c=c: e.tensor_scalar(
                out=hc[:, c, 0:n], in0=gluX[:, c, 0:n], scalar1=cw(c, 0), scalar2=v1("conv_b", c),
                op0=ALU.mult, op1=ALU.add),
                [("gluXn", c), ("gluXh", c), ("VT", 0), ("VT", 1)], [("hc", c)]))
        for jt in range(1, CK):
            for c in range(4):
                conv_thunks.append((lambda e, c=c, jt=jt: e.scalar_tensor_tensor(
                    out=hc[:, c, 0:n], in0=gluX[:, c, jt:jt + n], scalar=cw(c, jt), in1=hc[:, c, 0:n],
                    op0=ALU.mult, op1=ALU.add), [("gluXn", c), ("gluXh", c), ("hc", c)], [("hc", c)]))
        conv_pos = [0]

        def conv_emit(k):
            for (fn, rd, wr) in conv_thunks[conv_pos[0]:conv_pos[0] + k]:
                op("dve", fn, rd, wr)
            conv_pos[0] += k
        conv_emit(64)
        ckpt(10)
        for i in range(2):
            s, wv = load_piece("q%d" % i)
            for hl in range(4):
                h = i * 4 + hl
                b = bank()
                for kc in range(8):
                    op("pe", lambda e, kc=kc, hl=hl, b=b, wv=wv: e.matmul(
                        psb[b][0:64, 0:n], wv[:, kc, hl * 64:(hl + 1) * 64], uT[:, kc, 0:n],
                        start=(kc == 0), stop=(kc == 7)), [*s, ("uT", kc)], [pr_(b)])
                op("act", lambda e, h=h, b=b: e.mul(QA[0:64, h, 0:n], psb[b][0:64, 0:n], 0.125),
                   [pr_(b)], [("QA", h)])
        append_G2(n, kpos, lf_o[t0:t0 + n, :])
        ckpt(11)
        bk = [bank(hold=True) for _ in sl]
        for i in range(2):
            s, wv = load_piece("k%d" % i)
            for hl in range(4):
                h = i * 4 + hl
                b = bank()
                for kc in range(8):
                    op("pe", lambda e, kc=kc, hl=hl, b=b, wv=wv: e.matmul(
                        psb[b][0:64, 0:n], wv[:, kc, hl * 64:(hl + 1) * 64], uT[:, kc, 0:n],
                        start=(kc == 0), stop=(kc == 7)), [*s, ("uT", kc)], [pr_(b)])
                op("act", lambda e, h=h, b=b: e.copy(KA[0:64, h, kpos:kpos + n], psb[b][0:64, 0:n]),
                   [pr_(b)], [("KA", h, bb) for bb in kblks])
            for (j, off, m) in sl:
                for kc in range(8):
                    op("pe", lambda e, kc=kc, j=j, off=off, m=m, wv=wv, i=i: e.matmul(
                        psb[bk[j]][0:m, i * 256:(i + 1) * 256], uT[:, kc, off:off + m], wv[:, kc, :],
                        start=(kc == 0), stop=(kc == 7)), [*s, ("uT", kc)], [pr_(bk[j])])
        for (j, off, m) in sl:
            st = kst[j % 2]
            op("act", lambda e, j=j, m=m, st=st: e.copy(st[0:m, :], psb[bk[j]][0:m, :]),
               [pr_(bk[j])], [("kst", j % 2)])
            Pg.dma("pool", k_o[t0 + off:t0 + off + m, :], st[0:m, :], ("kst", j % 2), reads=[("kst", j % 2)])
            held.discard(bk[j])
        ckpt(12)
        bv = [bank(hold=True) for _ in sl]
        for i in range(2):
            s, wv = load_piece("v%d" % i)
            for (j, off, m) in sl:
                for kc in range(8):
                    op("pe", lambda e, kc=kc, j=j, off=off, m=m, wv=wv, i=i: e.matmul(
                        psb[bv[j]][0:m, i * 256:(i + 1) * 256], uT[:, kc, off:off + m], wv[:, kc, :],
                        start=(kc == 0), stop=(kc == 7)), [*s, ("uT", kc)], [pr_(bv[j])])
        for (j, off, m) in sl:
            st = vst[j % 2]
            blk = (kpos + off) // 128
            op("act", lambda e, j=j, m=m, st=st: e.copy(st[0:m, :], psb[bv[j]][0:m, :]),
               [pr_(bv[j])], [("vst", j % 2)])
            op("dve", lambda e, j=j, m=m, blk=blk, st=st: e.tensor_copy(
                VA[0:m, blk, :, 0:64], st[0:m, :].rearrange("p (h d) -> p h d", d=64)),
               [("vst", j % 2)], [("VA", blk)])
            Pg.dma("pool", v_o[t0 + off:t0 + off + m, :], st[0:m, :], ("vst", j % 2), reads=[("vst", j % 2)])
            held.discard(bv[j])
        if last:
            b = bank()
            for c in range(4):
                op("pe", lambda e, c=c, b=b: e.transpose(psb[b][0:CK - 1, c * 128:(c + 1) * 128],
                                                         gluX[:, c, n:n + CK - 1], ident[:, :]),
                   [("gluXn", c), ("gluXh", c), "ident"], [pr_(b)])
            op("act", lambda e, b=b: e.copy(cvst[0:CK - 1, :], psb[b][0:CK - 1, :]), [pr_(b)], ["cvst", ("attn_tm", 0)])
            Pg.dma("pool", cv_o, cvst[0:CK - 1, :], "cvst", reads=["cvst", ("attn_tm", 0)])
        ckpt(16)
        qa0 = kpos
        nkb = (kpos + n + 127) // 128
        m0 = sl[0][2]
        items = []
        for h in range(H):
            first = True
            for kb in range(nkb):
                ks = kb * 128
                kn = min(128, kpos + n - ks)
                need = [(j, off, m) for (j, off, m) in sl if ks <= qa0 + off + m - 1]
                if not need:
                    continue
                items.append(dict(h=h, kb=kb, ks=ks, kn=kn, need=need, c0=need[0][1], first=first, last=False))
                first = False
            items[-1]["last"] = True
        LOOK = 3

        def emit_qk(it, idx):
            b = bank()
            it["b"] = b
            it["pi"] = idx % NPT
            h, ks, kn, c0, kb = it["h"], it["ks"], it["kn"], it["c0"], it["kb"]
            op("pe", lambda e, h=h, ks=ks, kn=kn, c0=c0, b=b: e.matmul(
                psb[b][0:kn, c0:n], KA[0:128, h, ks:ks + kn], QA[0:128, h, c0:n], start=True, stop=True),
               [("KA", h, kb), ("KA1",), ("QA", h), ("QAg", h)], [pr_(b)])

        per_head = (len(conv_thunks) - conv_pos[0] + H - 1) // H
        for idx in range(min(LOOK, len(items))):
            emit_qk(items[idx], idx)
        for idx, it in enumerate(items):
            if idx + LOOK < len(items):
                emit_qk(items[idx + LOOK], idx + LOOK)
            h, kb, ks, kn, need, c0, b, pi_ = (it["h"], it["kb"], it["ks"], it["kn"], it["need"], it["c0"],
                                                it["b"], it["pi"])
            ob = 6 + (h % 2)
            O = psb[ob][:, 0:nsub * 65].rearrange("p (j d) -> p j d", d=65)
            if it["first"]:
                conv_emit(per_head)
                op("pe", lambda e, ob=ob: e.matmul(psb[ob][0:m0, 0:nsub * 65], zerob[:, 0:m0],
                                                   zerob[:, 0:nsub * 65], start=True, stop=False,
                                                   skip_group_check=True), ["zerob"], [pr_(ob)])
            pt = PT[pi_]
            op("act", lambda e, h=h, kb=kb, kn=kn, c0=c0, b=b, pt=pt: e.activation(
                pt[0:kn, c0:n], psb[b][0:kn, c0:n], AF.Exp, bias=Gk[0:kn, kb, h:h + 1], scale=1.0),
               [pr_(b), ("Gk", kb)], [("PT", pi_)])
            for (j, off, m) in need:
                if ks + kn - 1 > qa0 + off:
                    op("pool", lambda e, kn=kn, off=off, m=m, pt=pt: e.tensor_tensor(
                        out=pt[0:kn, off:off + m], in0=pt[0:kn, off:off + m], in1=maskb[0:kn, 0:m],
                        op=ALU.mult), [("PT", pi_), "maskb"], [("PT", pi_)])
            for (j, off, m) in need:
                op("pe", lambda e, h=h, kb=kb, kn=kn, j=j, off=off, m=m, O=O, pt=pt: e.matmul(
                    O[0:m, j, :], pt[0:kn, off:off + m], VA[0:kn, kb, h, 0:65], start=False, stop=True,
                    skip_group_check=True), [("PT", pi_), ("VA", kb)], [pr_(ob)])
            if it["last"]:
                for (j, off, m) in sl:
                    op("dve", lambda e, j=j, m=m, O=O: e.reciprocal(rc[0:m, j, :], O[0:m, j, 64:65]),
                       [pr_(ob)], ["rc"])
                    op("dve", lambda e, j=j, m=m, O=O, h=h: e.tensor_scalar(
                        out=attn_tm[0:m, j, h * 64:(h + 1) * 64], in0=O[0:m, j, 0:64], scalar1=rc[0:m, j, :],
                        scalar2=None, op0=ALU.mult), [pr_(ob), "rc"], [("attn_tm", j)])
        conv_emit(len(conv_thunks))
        if not last:
            for c in range(4):
                op("pool", lambda e, c=c: e.tensor_copy(gluX[:, c, 0:CK - 1], gluX[:, c, n:n + CK - 1]),
                   [("gluXn", c), ("gluXh", c)], [("gluXh", c)])
        for c in range(4):
            b = bank()
            for (j, off, m) in sl:
                op("pe", lambda e, c=c, j=j, off=off, m=m, b=b: e.transpose(
                    psb[b][:, off:off + m], attn_tm[0:m, j, c * 128:(c + 1) * 128], ident[0:m, 0:m]),
                   [("attn_tm", j), "ident"], [pr_(b)])
            op("act", lambda e, c=c, b=b: e.copy(attnT[:, c, 0:n], psb[b][:, 0:n]), [pr_(b)], [("attnT", c)])
        ckpt(17)
        op("act", lambda e: e.copy(hcb[:, :, 0:n], hc[:, :, 0:n]), [("hc", c) for c in range(4)], ["hcb"])
        op("act", lambda e: e.activation(sqb[:, :, 0:n], hc[:, :, 0:n], AF.Square),
           [("hc", c) for c in range(4)], ["sqb"])
        bs_ = bank()
        bq_ = bank()
        for c in range(4):
            op("pe", lambda e, c=c: e.matmul(psb[bs_][:, 0:n], onesb[:, :], hcb[:, c, 0:n], start=(c == 0),
                                             stop=(c == 3)), ["hcb", "onesb"], [pr_(bs_)])
        for c in range(4):
            op("pe", lambda e, c=c: e.matmul(psb[bq_][:, 0:n], onesb[:, :], sqb[:, c, 0:n], start=(c == 0),
                                             stop=(c == 3)), ["sqb", "onesb"], [pr_(bq_)])
        op("act", lambda e: e.mul(lnm[:, 0:n], psb[bs_][:, 0:n], 1.0 / 512), [pr_(bs_)], ["lnm"])
        op("dve", lambda e: e.tensor_tensor(out=lnv[:, 0:n], in0=lnm[:, 0:n], in1=lnm[:, 0:n], op=ALU.mult),
           ["lnm"], ["lnv"])
        op("dve", lambda e: e.scalar_tensor_tensor(out=lnv[:, 0:n], in0=psb[bq_][:, 0:n], scalar=1.0 / 512,
                                                   in1=lnv[:, 0:n], op0=ALU.mult, op1=ALU.subtract),
           [pr_(bq_), "lnv"], ["lnv"])
        op("dve", lambda e: e.tensor_scalar(out=lnv[:, 0:n], in0=lnv[:, 0:n], scalar1=0.0, scalar2=EPS,
                                            op0=ALU.max, op1=ALU.add), ["lnv"], ["lnv"])
        op("act", lambda e: e.activation(lnr[:, 0:n], lnv[:, 0:n], AF.Ln), ["lnv"], ["lnr"])
        op("act", lambda e: e.activation(lnr[:, 0:n], lnr[:, 0:n], AF.Exp, scale=-0.5), ["lnr"], ["lnr"])
        for c in range(4):
            op("dve", lambda e, c=c: e.tensor_tensor(out=hc[:, c, 0:n], in0=hc[:, c, 0:n], in1=lnm[:, 0:n],
                                                     op=ALU.subtract), [("hc", c), "lnm"], [("hc", c)])
            op("dve", lambda e, c=c: e.tensor_tensor(out=hc[:, c, 0:n], in0=hc[:, c, 0:n], in1=lnr[:, 0:n],
                                                     op=ALU.mult), [("hc", c), "lnr"], [("hc", c)])
            op("act", lambda e, c=c: e.activation(hcb[:, c, 0:n], hc[:, c, 0:n], AF.Silu,
                                                  bias=v1("conv_ln_b", c), scale=v1("conv_ln_g", c)),
               [("hc", c), ("VT", 1)], [("hcs", c), "hcb"])
        ckpt(18)
        for dch in range(8):
            sA, wA = load_piece("gA%d" % dch)
            sB, wB = load_piece("gB%d" % dch)
            sP, wP = load_piece("pr%d" % dch)
            for dl in range(1):
                bA = bank()
                bB = bank()
                bya = bank()
                byb = bank()
                for kc in range(8):
                    op("pe", lambda e, kc=kc, dl=dl, bA=bA, wA=wA: e.matmul(
                        psb[bA][:, 0:n], wA[:, kc, dl * 128:(dl + 1) * 128], uT[:, kc, 0:n],
                        start=(kc == 0), stop=(kc == 7)), [*sA, ("uT", kc)], [pr_(bA)])
                for kc in range(8):
                    op("pe", lambda e, kc=kc, dl=dl, bB=bB, wB=wB: e.matmul(
                        psb[bB][:, 0:n], wB[:, kc, dl * 128:(dl + 1) * 128], uT[:, kc, 0:n],
                        start=(kc == 0), stop=(kc == 7)), [*sB, ("uT", kc)], [pr_(bB)])
                for c in range(4):
                    op("pe", lambda e, c=c, dl=dl, bya=bya, wP=wP: e.matmul(
                        psb[bya][:, 0:n], wP[:, c, dl * 128:(dl + 1) * 128], attnT[:, c, 0:n],
                        start=(c == 0), stop=(c == 3)), [*sP, ("attnT", c)], [pr_(bya)])
                for c in range(4):
                    op("pe", lambda e, c=c, dl=dl, byb=byb, wP=wP: e.matmul(
                        psb[byb][:, 0:n], wP[:, 4 + c, dl * 128:(dl + 1) * 128], hcb[:, c, 0:n],
                        start=(c == 0), stop=(c == 3)), [*sP, ("hcs", c)], [pr_(byb)])
                gA_ = sg[0 + 2 * (dch % 2)]
                gB_ = sg[1 + 2 * (dch % 2)]
                rA = ("sg", 0 + 2 * (dch % 2))
                rB = ("sg", 1 + 2 * (dch % 2))
                tt = t1[dch % 2]
                rT = ("t1", dch % 2)
                op("act", lambda e, bA=bA, gA_=gA_: e.activation(gA_[:, 0:n], psb[bA][:, 0:n], AF.Sigmoid),
                   [pr_(bA)], [rA])
                op("act", lambda e, bB=bB, gB_=gB_: e.activation(gB_[:, 0:n], psb[bB][:, 0:n], AF.Sigmoid),
                   [pr_(bB)], [rB])
                op("dve", lambda e, bya=bya, gA_=gA_, tt=tt: e.tensor_tensor(
                    out=tt[:, 0:n], in0=psb[bya][:, 0:n], in1=gA_[:, 0:n], op=ALU.mult),
                   [pr_(bya), rA], [rT])
                op("dve", lambda e, byb=byb, gB_=gB_: e.tensor_tensor(
                    out=gB_[:, 0:n], in0=psb[byb][:, 0:n], in1=gB_[:, 0:n], op=ALU.mult),
                   [pr_(byb), rB], [rB])
                op("dve", lambda e, dch=dch, gB_=gB_, tt=tt: e.tensor_tensor(
                    out=mergedT[:, dch, 0:n], in0=tt[:, 0:n], in1=gB_[:, 0:n], op=ALU.add),
                   [rT, rB], [("mergedT", dch)])
        ckpt(19)
        def g_stage1(dch):
            s, wv = load_piece("wo%d" % dch)
            b = bank()
            for kc in range(8):
                op("pe", lambda e, kc=kc, b=b, wv=wv: e.matmul(
                    psb[b][:, 0:n], wv[:, kc, 0:128], mergedT[:, kc, 0:n],
                    start=(kc == 0), stop=(kc == 7)), [*s, ("mergedT", kc)], [pr_(b)])
            tf = tmpf[dch % 2]
            rtf = ("tmpf", dch % 2)
            op("act", lambda e, dch=dch, b=b, tf=tf: e.activation(
                tf[:, 0:n], psb[b][:, 0:n], AF.Copy, scale=modT[:, si, 16 + dch:17 + dch]),
               [pr_(b), "modT"], [rtf])
            return tf, rtf

        def g_stage2(dch, tf, rtf):
            b2 = bank()
            for (j, off, m) in sl:
                op("pe", lambda e, j=j, off=off, m=m, b2=b2, tf=tf: e.transpose(
                    psb[b2][0:m, j * 128:(j + 1) * 128], tf[:, off:off + m], ident[:, :]),
                   [rtf, "ident"], [pr_(b2)])
            for (j, off, m) in sl:
                op("dve", lambda e, j=j, m=m, dch=dch, b2=b2: e.scalar_tensor_tensor(
                    out=xs[0:m, j, dch * 128:(dch + 1) * 128], in0=xs[0:m, j, dch * 128:(dch + 1) * 128],
                    scalar=ALPHA, in1=psb[b2][0:m, j * 128:(j + 1) * 128], op0=ALU.mult, op1=ALU.add),
                   [pr_(b2), *RX(j)], [*RX(j)])

        st_ = g_stage1(0)
        for dch in range(8):
            nxt_ = g_stage1(dch + 1) if dch + 1 < 8 else None
            g_stage2(dch, *st_)
            st_ = nxt_
        layer_norm_tile(sl, xs, RX)
        ckpt(20)
        for kc in range(8):
            b = bank()
            for (j, off, m) in sl:
                op("pe", lambda e, kc=kc, j=j, off=off, m=m, b=b: e.transpose(
                    psb[b][:, off:off + m], xs[0:m, j, kc * 128:(kc + 1) * 128], ident[0:m, 0:m]),
                   [*RX(j), "ident"], [pr_(b)])
            op("act", lambda e, kc=kc, b=b: e.activation(uT[:, kc, 0:n], psb[b][:, 0:n], AF.Identity,
                                                         bias=B2[:, si, kc:kc + 1], scale=G2[:, si, kc:kc + 1]),
               [pr_(b), "B2", "G2"], [("uT", kc)])
            op("act", lambda e, kc=kc, b=b: e.activation(x1a[:, kc, 0:n], psb[b][:, 0:n], AF.Identity,
                                                         bias=AB[:, kc:kc + 1], scale=AG[:, kc:kc + 1]),
               [pr_(b), "AB", "AG"], [*RA(kc)])
        ckpt(21)
        r1_fence()
        for i in range(NFF):
            s, wv = load_piece("upa%d" % i)
            s2, wv2 = load_piece("upv%d" % i)
            ba = bank()
            bv_ = bank()
            for kc in range(8):
                op("pe", lambda e, kc=kc, ba=ba, wv=wv: e.matmul(psb[ba][:, 0:n], wv[:, kc, :], uT[:, kc, 0:n],
                                                                 start=(kc == 0), stop=(kc == 7)),
                   [*s, ("uT", kc)], [pr_(ba)])
            for kc in range(8):
                op("pe", lambda e, kc=kc, bv_=bv_, wv2=wv2: e.matmul(psb[bv_][:, 0:n], wv2[:, kc, :],
                                                                     uT[:, kc, 0:n], start=(kc == 0),
                                                                     stop=(kc == 7)),
                   [*s2, ("uT", kc)], [pr_(bv_)])
            ab = a_sb[i % 2]
            rab = ("a_sb", i % 2)
            cb_ = cbuf[i % 2]
            rcb = ("cbuf", i % 2)
            ss = sbuf_s[i % 2]
            rss = ("sil", i % 2)
            op("pool", lambda e, i=i, ab=ab: e.tensor_copy(ab[:, 0:2], hist[:, i, :]), [("hist", i), rab], [rab])
            op("act", lambda e, ba=ba, ab=ab: e.copy(ab[:, 2:2 + n], psb[ba][:, 0:n]), [pr_(ba), rab], [rab])
            op("act", lambda e, i=i, ba=ba, cb_=cb_: e.activation(cb_[:, 0:n], psb[ba][:, 0:n], AF.Identity,
                                                                  bias=v1("ffn_conv_b", i), scale=fcw(2, i)),
               [pr_(ba), ("VT", 1), ("VT", 2)], [rcb])
            op("dve", lambda e, i=i, ab=ab, cb_=cb_: e.scalar_tensor_tensor(
                out=cb_[:, 0:n], in0=ab[:, 1:1 + n], scalar=fcw(1, i), in1=cb_[:, 0:n], op0=ALU.mult,
                op1=ALU.add), [rab, rcb], [rcb])
            op("dve", lambda e, i=i, ab=ab, cb_=cb_: e.scalar_tensor_tensor(
                out=cb_[:, 0:n], in0=ab[:, 0:n], scalar=fcw(0, i), in1=cb_[:, 0:n], op0=ALU.mult,
                op1=ALU.add), [rab, rcb], [rcb])
            op("pool", lambda e, i=i, ab=ab: e.tensor_copy(hist[:, i, :], ab[:, n:n + 2]), [rab], [("hist", i)])
            op("act", lambda e, cb_=cb_, ss=ss: e.activation(ss[:, 0:n], cb_[:, 0:n], AF.Silu), [rcb], [rss])
            op("dve", lambda e, i=i, bv_=bv_, ss=ss: e.tensor_tensor(out=hT[:, i, 0:n], in0=psb[bv_][:, 0:n],
                                                                     in1=ss[:, 0:n], op=ALU.mult),
               [pr_(bv_), rss], [("hT", i)])
        if last:
            op("dve", lambda e: e.tensor_copy(hist44[:, :].rearrange("p (r i) -> p i r", i=NFF), hist[:, :, :]),
               [("hist", i) for i in range(NFF)], ["hist44"])
            b = bank()
            op("pe", lambda e, b=b: e.transpose(psb[b][0:44, 0:128], hist44[:, :], ident[:, :]),
               ["hist44", "ident"], [pr_(b)])
            op("act", lambda e, b=b: e.copy(st44[:, :], psb[b][0:44, 0:128]), [pr_(b)], ["st44"])
            Pg.dma("pool", ff_o.rearrange("r (i p) -> (r i) p", p=128), st44[:, :], "st44", reads=["st44"])
        ckpt(22)
        def j_stage1(dch):
            b = bank()
            for part, (i0, i1) in enumerate(((0, 8), (8, 16), (16, 22))):
                s, wv = load_piece("dn%d_%d" % (dch, part))
                for il in range(i1 - i0):
                    i = i0 + il
                    op("pe", lambda e, il=il, i=i, b=b, wv=wv: e.matmul(
                        psb[b][:, 0:n], wv[:, il, :], hT[:, i, 0:n], start=(i == 0), stop=(i == NFF - 1)),
                       [*s, ("hT", i)], [pr_(b)])
            tf = tmpf[dch % 2]
            rtf = ("tmpf", dch % 2)
            op("dve", lambda e, dch=dch, b=b, tf=tf: e.scalar_tensor_tensor(
                out=tf[:, 0:n], in0=psb[b][:, 0:n], scalar=modT[:, si, 40 + dch:41 + dch], in1=x1a[:, dch, 0:n],
                op0=ALU.mult, op1=ALU.add), [pr_(b), "modT", *RA(dch)], [rtf])
            return tf, rtf

        def j_stage2(dch, tf, rtf):
            b2 = bank()
            for (j, off, m) in sl:
                op("pe", lambda e, j=j, off=off, m=m, b2=b2, tf=tf: e.transpose(
                    psb[b2][0:m, j * 128:(j + 1) * 128], tf[:, off:off + m], ident[:, :]),
                   [rtf, "ident"], [pr_(b2)])
            for (j, off, m) in sl:
                op("act", lambda e, j=j, m=m, dch=dch, b2=b2: e.copy(
                    xs[0:m, j, dch * 128:(dch + 1) * 128], psb[b2][0:m, j * 128:(j + 1) * 128]),
                   [pr_(b2)], [*RX(j)])

        st_ = j_stage1(0)
        for dch in range(8):
            nxt_ = j_stage1(dch + 1) if dch + 1 < 8 else None
            j_stage2(dch, *st_)
            st_ = nxt_
        if next_head is not None:
            next_par[0] = next_head()
        layer_norm_tile(sl, xs, RX)
        for (j, off, m) in sl:
            op("dve", lambda e, j=j, m=m: e.tensor_tensor(out=xs[0:m, j, :], in0=xs[0:m, j, :],
                                                          in1=ln2gb[0:m, :], op=ALU.mult),
               [*RX(j), "ln2gb"], [*RX(j)])
            op("dve", lambda e, j=j, m=m: e.tensor_tensor(out=xs[0:m, j, :], in0=xs[0:m, j, :],
                                                          in1=ln2bb[0:m, :], op=ALU.add),
               [*RX(j), "ln2bb"], [*RX(j)])
            Pg.dma("pool", y_o[t0 + off:t0 + off + m, :], xs[0:m, j, :], ("xs", par, j), reads=RX(j))
        return next_par[0]

    def init_seq(P_, is_sample):
        op("dve", lambda e: e.memset(carry[:, :], 0.0), [], ["carry"])
        if not is_sample:
            for c in range(4):
                op("pool", lambda e, c=c: e.memset(gluX[:, c, 0:CK - 1], 0.0), [], [("gluXh", c)])
            op("pool", lambda e: e.memset(hist[:, :, :], 0.0), [], [("hist", i) for i in range(NFF)])
            return
        Pg.dma("sp", cvst[0:CK - 1, :], st_conv, "cvst", writes=["cvst", ("attn_tm", 0)])
        b = bank()
        for c in range(4):
            op("pe", lambda e, c=c, b=b: e.transpose(psb[b][:, c * 32:c * 32 + CK - 1],
                                                     cvst[0:CK - 1, c * 128:(c + 1) * 128],
                                                     ident[0:CK - 1, 0:CK - 1]), ["cvst", ("attn_tm", 0), "ident"], [pr_(b)])
        for c in range(4):
            op("dve", lambda e, c=c, b=b: e.tensor_copy(gluX[:, c, 0:CK - 1], psb[b][:, c * 32:c * 32 + CK - 1]),
               [pr_(b)], [("gluXh", c)])
        Pg.dma("sp", st44[:, :], st_ffn.rearrange("r (i p) -> (r i) p", p=128), "st44", writes=["st44"])
        b = bank()
        op("pe", lambda e, b=b: e.transpose(psb[b][:, 0:44], st44[:, :], ident[0:44, 0:44]),
           ["st44", "ident"], [pr_(b)])
        op("dve", lambda e, b=b: e.tensor_copy(hist[:, :, :], psb[b][:, 0:44].rearrange("p (r i) -> p i r", i=NFF)),
           [pr_(b)], [("hist", i) for i in range(NFF)])
        nblk = P_ // 128
        for blk in range(nblk):
            kx = kst[blk % 2]
            vx = vst[blk % 2]
            Pg.dma("sp", kx[:, :], cache_k[blk * 128:(blk + 1) * 128, :], ("kstl", blk % 2),
                   writes=[("kst", blk % 2)])
            Pg.dma("sp", vx[:, :], cache_v[blk * 128:(blk + 1) * 128, :], ("vstl", blk % 2),
                   writes=[("vst", blk % 2)])
            for g in range(2):
                b = bank()
                for hl in range(4):
                    h = g * 4 + hl
                    op("pe", lambda e, h=h, hl=hl, b=b, kx=kx: e.transpose(
                        psb[b][0:64, hl * 128:(hl + 1) * 128], kx[:, h * 64:(h + 1) * 64], ident[:, :]),
                       [("kst", blk % 2), "ident"], [pr_(b)])
                op("act", lambda e, g=g, b=b, blk=blk: e.copy(
                    KA[0:64, g * 4:(g + 1) * 4, blk * 128:(blk + 1) * 128],
                    psb[b][0:64, :].rearrange("p (h k) -> p h k", k=128)),
                   [pr_(b)], [("KA", h, blk) for h in range(g * 4, g * 4 + 4)])
            op("dve", lambda e, blk=blk, vx=vx: e.tensor_copy(
                VA[:, blk, :, 0:64], vx[:, :].rearrange("p (h d) -> p h d", d=64)),
               [("vst", blk % 2)], [("VA", blk)])
        for ch in range(0, P_, TT):
            nn = min(TT, P_ - ch)
            b = bank()
            for (j, off, m) in subs_of(nn):
                Pg.dma("sp", lc[0:m, j % 2, :], cache_lf[ch + off:ch + off + m, :], ("lc", j % 2),
                       writes=[("lc", j % 2)])
                op("pe", lambda e, j=j, off=off, m=m, b=b: e.transpose(psb[b][0:8, off:off + m], lc[0:m, j % 2, :],
                                                                      ident[0:m, 0:m]),
                   [("lc", j % 2), "ident"], [pr_(b)])
            op("act", lambda e, b=b, nn=nn: e.mul(l_sb[:, 0:nn], psb[b][0:8, 0:nn], -1.0), [pr_(b)], ["l_sb"])
            append_G(nn, ch)

    tiles = []
    for si in range(NSEQ):
        is_sample = (si == NSEQ - 1)
        if is_sample:
            T_, P_ = TS, PAST
            x_src = xsm
            outs = (y_s, k_s, v_s, lf_s, cv_s, ff_s)
        else:
            T_, P_ = SEQ, 0
            x_src = xp[si]
            outs = (y_p[si], k_p[si], v_p[si], lf_p[si], cv_p[si], ff_p[si])
        t0 = 0
        while t0 < T_:
            n = min(TT, T_ - t0)
            tiles.append(dict(si=si, x_src=x_src, t0=t0, n=n, kpos=P_ + t0, outs=outs, last=(t0 + n >= T_),
                              first=(t0 == 0), P=P_, is_sample=is_sample))
            t0 += n
    par = None
    for ti, T in enumerate(tiles):
        if T["first"]:
            init_seq(T["P"], T["is_sample"])
        if par is None:
            par = tile_head(T["si"], T["x_src"], T["t0"], T["n"])
        nh = None
        if ti + 1 < len(tiles):
            N_ = tiles[ti + 1]
            nh = (lambda N_=N_: tile_head(N_["si"], N_["x_src"], N_["t0"], N_["n"]))
        par = tile(T["si"], T["x_src"], T["t0"], T["n"], T["kpos"], T["outs"], T["last"], par, nh)

    Pg.emit(stack)
    stack.close()
    return nc


_CONSTS = None


def _consts():
    ident = np.eye(128, dtype=np.float32)
    mask = np.triu(np.ones((128, 128), dtype=np.float32))
    return ident, mask


def make_in_maps(inputs, NPS, n_cores):
    ident, mask = _consts()
    f = lambda a: np.ascontiguousarray(np.asarray(a, dtype=np.float32))
    maps = []
    for c in range(n_cores):
        m = {}
        m["xp"] = f(inputs["x_prompt"][c * NPS:(c + 1) * NPS])
        m["xsm"] = f(inputs["x_sample"][c])
        m["c_all"] = f(np.concatenate([inputs["c_prompt"][c * NPS:(c + 1) * NPS],
                                       inputs["c_sample"][c:c + 1]], axis=0))
        P = inputs["cache_k"].shape[2]
        m["cache_k"] = f(inputs["cache_k"][0, c].reshape(P, 512))
        m["cache_v"] = f(inputs["cache_v"][0, c].reshape(P, 512))
        m["cache_logf"] = f(inputs["cache_logf"][0, c])
        m["state_conv"] = f(inputs["state_conv"][0, c])
        m["state_ffn"] = f(inputs["state_ffn_conv"][0, c])
        for nm in ("w_ada", "w_in", "w_attn_proj", "w_conv_proj", "w_out", "w_up", "w_down", "b_ada", "b_f",
                   "conv_w", "conv_b", "conv_ln_g", "conv_ln_b", "ln1_g", "ln1_b", "ffn_conv_w", "ffn_conv_b",
                   "ln2_g", "ln2_b"):
            m[nm] = f(inputs[nm][0])
        m["ident"] = ident
        m["mask"] = mask
        maps.append(m)
    return maps


def gather(results, NPS, SEQ, TS):
    cat = lambda k: np.concatenate([r[k] for r in results], axis=0)
    stk = lambda k: np.stack([r[k] for r in results], axis=0)
    nb = len(results) * NPS
    y_p = cat("y_p")
    y_s = stk("y_s")
    k_p = cat("k_p").reshape(1, nb, SEQ, H, DH)
    v_p = cat("v_p").reshape(1, nb, SEQ, H, DH)
    lf_p = cat("lf_p").reshape(1, nb, SEQ, H)
    cv_p = cat("cv_p").reshape(1, nb, CK - 1, 512)
    ff_p = cat("ff_p").reshape(1, nb, 2, DFF)
    ns = len(results)
    k_s = stk("k_s").reshape(1, ns, TS, H, DH)
    v_s = stk("v_s").reshape(1, ns, TS, H, DH)
    lf_s = stk("lf_s").reshape(1, ns, TS, H)
    cv_s = stk("cv_s").reshape(1, ns, CK - 1, 512)
    ff_s = stk("ff_s").reshape(1, ns, 2, DFF)
    return tuple(np.ascontiguousarray(a, dtype=np.float32) for a in
                 (y_p, y_s, k_p, v_p, lf_p, cv_p, ff_p, k_s, v_s, lf_s, cv_s, ff_s))


def kernel(**inputs):
    inputs = {k: np.asarray(v) for k, v in inputs.items()}
    B, SEQ, _ = inputs["x_prompt"].shape
    TS = inputs["x_sample"].shape[1]
    PAST = inputs["cache_k"].shape[2]
    NPS = B // N_CORES
    nc = build_nc(NPS, SEQ, TS, PAST, 256)
    in_maps = make_in_maps(inputs, NPS, N_CORES)
    res = run_bass_kernel_spmd(nc, in_maps, core_ids=list(range(N_CORES)))
    return gather(res.results, NPS, SEQ, TS)
```

```python
import numpy as np
import concourse.bass as bass
import concourse.mybir as mybir
from concourse.bass_utils import run_bass_kernel_spmd

F32 = mybir.dt.float32
BF16 = mybir.dt.bfloat16
AF = mybir.ActivationFunctionType
ALU = mybir.AluOpType

D = 1024
H = 8
DH = 64
CK = 31
DFF = 2816
NFF = 22
ALPHA = float(2 ** 0.25)
EPS = 1e-5
N_CORES = 8


class Op:
    __slots__ = ("eng", "fn", "deps", "marked", "mark_idx", "chan", "target")


class Prog:
    ENGS = ("pe", "act", "dve", "pool", "sp")

    def __init__(self, nc, same_engine_sync=True):
        self.nc = nc
        self.ops = {e: [] for e in self.ENGS}
        self.last_w = {}
        self.readers = {}
        self.chan_cnt = {}
        self.same = same_engine_sync
        self.all_dma = []

    def _dep(self, o, reads, writes):
        deps = set()
        for r in reads:
            w = self.last_w.get(r)
            if w is not None:
                deps.add(w)
        for r in writes:
            w = self.last_w.get(r)
            if w is not None:
                deps.add(w)
            for rd in self.readers.get(r, ()):
                deps.add(rd)
        deps.discard(o)
        o.deps = list(deps)
        for r in reads:
            self.readers.setdefault(r, []).append(o)
        for r in writes:
            self.last_w[r] = o
            self.readers[r] = []

    def op(self, eng, fn, reads=(), writes=()):
        o = Op()
        o.eng = eng
        o.fn = fn
        o.marked = False
        o.mark_idx = 0
        o.chan = None
        o.target = 0
        self._dep(o, reads, writes)
        self.ops[eng].append(o)
        return o

    def dma(self, eng, out, in_, chan, reads=(), writes=()):
        return self.dma_multi(eng, [(out, in_)], chan, reads, writes)

    def dma_multi(self, eng, pairs, chan, reads=(), writes=()):
        o = Op()
        o.eng = eng
        o.fn = list(pairs)
        o.marked = False
        o.mark_idx = 0
        chan = (eng, chan)
        o.chan = chan
        self.chan_cnt[chan] = self.chan_cnt.get(chan, 0) + 16 * len(pairs)
        o.target = self.chan_cnt[chan]
        self._dep(o, reads, writes)
        self.ops[eng].append(o)
        self.all_dma.append(o)
        return o

    def emit(self, stack):
        nc = self.nc
        for e in self.ENGS:
            for o in self.ops[e]:
                for d in o.deps:
                    d.marked = True
        for e in self.ENGS:
            c = 0
            for o in self.ops[e]:
                if o.chan is None and o.marked:
                    c += 1
                    o.mark_idx = c
        esem = {e: stack.enter_context(nc.semaphore("s_" + e)) for e in self.ENGS}
        csem = {}
        for i, ch in enumerate(self.chan_cnt):
            csem[ch] = stack.enter_context(nc.semaphore("c%d" % i))
        finals = [(csem[ch], cnt) for ch, cnt in self.chan_cnt.items()]

        def run(e, eng):
            waited = {}
            for o in self.ops[e]:
                for d in o.deps:
                    if d.chan is not None:
                        key = ("c", d.chan)
                        val = d.target
                        sem = csem[d.chan]
                    else:
                        if d.eng == e and (e == "pe" or not self.same):
                            continue
                        key = d.eng
                        val = d.mark_idx
                        sem = esem[d.eng]
                    if waited.get(key, 0) >= val:
                        continue
                    eng.wait_ge(sem, val)
                    waited[key] = val
                if o.chan is not None:
                    for (do, di) in o.fn:
                        eng.dma_start(out=do, in_=di).then_inc(csem[o.chan], 16)
                    continue
                ins = o.fn(eng)
                if o.marked:
                    ins.then_inc(esem[e], 1)
            if e == "sp":
                for sem, cnt in finals:
                    eng.wait_ge(sem, cnt)

        block = stack.enter_context(nc.Block())

        @block.tensor
        def _(eng):
            run("pe", eng)

        @block.scalar
        def _(eng):
            run("act", eng)

        @block.vector
        def _(eng):
            run("dve", eng)

        @block.gpsimd
        def _(eng):
            run("pool", eng)

        @block.sync
        def _(eng):
            run("sp", eng)


def make_pieces():
    P = {}
    order = []

    def add(name, W, entries):
        P[name] = (W, entries)
        order.append(name)

    for i, c0 in enumerate((0, 256)):
        add("q%d" % i, 256, [("w_in", kc, c0) for kc in range(8)])
    for i, c0 in enumerate((512, 768)):
        add("k%d" % i, 256, [("w_in", kc, c0) for kc in range(8)])
    for i, c0 in enumerate((1024, 1280)):
        add("v%d" % i, 256, [("w_in", kc, c0) for kc in range(8)])
    add("f", 8, [("w_in", kc, 1536) for kc in range(8)])
    for cc in range(2):
        add("ga%d" % cc, 256, [("w_in", kc, 1544 + cc * 256) for kc in range(8)])
        add("gb%d" % cc, 256, [("w_in", kc, 2056 + cc * 256) for kc in range(8)])
    for d in range(8):
        add("gA%d" % d, 128, [("w_in", kc, 2568 + d * 128) for kc in range(8)])
        add("gB%d" % d, 128, [("w_in", kc, 3592 + d * 128) for kc in range(8)])
        add("pr%d" % d, 128, [("w_attn_proj", kc, d * 128) for kc in range(4)]
            + [("w_conv_proj", kc, d * 128) for kc in range(4)])
    for d in range(8):
        add("wo%d" % d, 128, [("w_out", kc, d * 128) for kc in range(8)])
    for i in range(NFF):
        add("upa%d" % i, 128, [("w_up", kc, i * 128) for kc in range(8)])
        add("upv%d" % i, 128, [("w_up", kc, DFF + i * 128) for kc in range(8)])
    for d in range(8):
        for part, (i0, i1) in enumerate(((0, 8), (8, 16), (16, 22))):
            add("dn%d_%d" % (d, part), 128, [("w_down", i, d * 128) for i in range(i0, i1)])
    return P, order


class _Stop(Exception):
    pass


def build_nc(NPS, SEQ, TS, PAST, TT, stop_after=None):
    try:
        return _build_nc(NPS, SEQ, TS, PAST, TT, stop_after)
    except _Stop as ex:
        return ex.args[0]


def _build_nc(NPS, SEQ, TS, PAST, TT, stop_after=None):
    NSEQ = NPS + 1
    NK = max(SEQ, PAST + TS)
    NB = (NK + 127) // 128
    NKP = NB * 128
    nc = bass.Bass("TRN2", target_bir_lowering=False)
    import contextlib
    stack = contextlib.ExitStack()

    def din(name, shape):
        return nc.dram_tensor(name, list(shape), F32, kind="ExternalInput").ap()

    def dout(name, shape):
        return nc.dram_tensor(name, list(shape), F32, kind="ExternalOutput").ap()

    xp = din("xp", (NPS, SEQ, D))
    xsm = din("xsm", (TS, D))
    call = din("c_all", (NSEQ, D))
    cache_k = din("cache_k", (PAST, 512))
    cache_v = din("cache_v", (PAST, 512))
    cache_lf = din("cache_logf", (PAST, 8))
    st_conv = din("state_conv", (CK - 1, 512))
    st_ffn = din("state_ffn", (2, DFF))
    W = {}
    for name, shp in (("w_ada", (D, 6 * D)), ("w_in", (D, 4616)), ("w_attn_proj", (512, D)),
                      ("w_conv_proj", (512, D)), ("w_out", (D, D)), ("w_up", (D, 2 * DFF)),
                      ("w_down", (DFF, D))):
        W[name] = din(name, shp)
    b_ada = din("b_ada", (6 * D,))
    b_f = din("b_f", (8,))
    conv_w = din("conv_w", (CK, 512))
    conv_b = din("conv_b", (512,))
    conv_ln_g = din("conv_ln_g", (512,))
    conv_ln_b = din("conv_ln_b", (512,))
    ln1_g = din("ln1_g", (D,))
    ln1_b = din("ln1_b", (D,))
    ffn_conv_w = din("ffn_conv_w", (3, DFF))
    ffn_conv_b = din("ffn_conv_b", (DFF,))
    ln2_g = din("ln2_g", (D,))
    ln2_b = din("ln2_b", (D,))
    ident_d = din("ident", (128, 128))
    mask_d = din("mask", (128, 128))

    y_p = dout("y_p", (NPS, SEQ, D))
    y_s = dout("y_s", (TS, D))
    k_p = dout("k_p", (NPS, SEQ, 512))
    v_p = dout("v_p", (NPS, SEQ, 512))
    lf_p = dout("lf_p", (NPS, SEQ, 8))
    cv_p = dout("cv_p", (NPS, CK - 1, 512))
    ff_p = dout("ff_p", (NPS, 2, DFF))
    k_s = dout("k_s", (TS, 512))
    v_s = dout("v_s", (TS, 512))
    lf_s = dout("lf_s", (TS, 8))
    cv_s = dout("cv_s", (CK - 1, 512))
    ff_s = dout("ff_s", (2, DFF))

    pieces, porder = make_pieces()
    pidx = {n: i for i, n in enumerate(porder)}
    wscr = nc.dram_tensor("wscr", [len(porder), 128, 2048], BF16, kind="Internal").ap()

    def sb(name, shape, dt=F32):
        return stack.enter_context(nc.sbuf_tensor("sb_" + name, list(shape), dt))

    KA = sb("KA", (128, H, NK), BF16)
    VA = sb("VA", (128, NB, H, 66), BF16)
    Gk = sb("Gk", (128, NB, H))
    NSUB = (TT + 127) // 128
    bufs = [sb("bufA", (128, 2048)), sb("bufB", (128, 2048))]
    xs_views = [bb[:, 0:NSUB * D].rearrange("p (a b) -> p a b", b=D) for bb in bufs]
    x1a_views = [bb[:, 0:8 * TT].rearrange("p (k t) -> p k t", t=TT) for bb in bufs]
    uT = sb("uT", (128, 8, TT), BF16)
    QA = sb("QA", (128, H, TT), BF16)
    NPT = 5
    PT = [sb("PT%d" % i, (128, TT), BF16) for i in range(NPT)]
    attn_tm = sb("attn_tm", (128, NSUB, 512))
    attnT = sb("attnT", (128, 4, TT), BF16)
    gluX = sb("gluX", (128, 4, CK - 1 + TT))
    R1 = sb("R1", (128, NFF * TT), BF16)
    hT = R1[:, :].rearrange("p (i t) -> p i t", t=TT)
    R1f = R1[:, :].bitcast(F32)
    hc = R1f[:, 0:4 * TT].rearrange("p (c t) -> p c t", t=TT)
    lnm = R1f[:, 4 * TT:5 * TT]
    lnv = R1f[:, 5 * TT:6 * TT]
    lnr = R1f[:, 6 * TT:7 * TT]
    hcb = R1[:, 14 * TT:18 * TT].rearrange("p (c t) -> p c t", t=TT)
    sqb = R1[:, 18 * TT:22 * TT].rearrange("p (c t) -> p c t", t=TT)
    sg = [sb("sg%d" % i, (128, TT)) for i in range(4)]
    t1 = [sb("t1_%d" % i, (128, TT)) for i in range(2)]
    mergedT = sb("mergedT", (128, 8, TT), BF16)
    tmpf = [sb("tmpf%d" % i, (128, TT)) for i in range(2)]
    a_sb = [sb("a_sb%d" % i, (128, TT + 2)) for i in range(2)]
    cbuf = [sb("cbuf%d" % i, (128, TT)) for i in range(2)]
    sbuf_s = [sb("sil%d" % i, (128, TT)) for i in range(2)]
    NUNIT = 8
    ring = sb("ring", (128, NUNIT * 1024), BF16)
    ln2gb = sb("ln2gb", (128, D))
    ln2bb = sb("ln2bb", (128, D))
    kst = [sb("kst%d" % i, (128, 512)) for i in range(2)]
    vst = [sb("vst%d" % i, (128, 512)) for i in range(2)]
    ident = sb("ident", (128, 128))
    maskb = sb("maskb", (128, 128), BF16)
    maskf = sb("maskf", (128, 128))
    onesb = sb("onesb", (128, 128), BF16)
    zerob = sb("zerob", (128, 160), BF16)
    ones8 = sb("ones8", (8, TT))
    l_sb = sb("l_sb", (8, TT))
    e_sb = sb("e_sb", (8, TT))
    Gt = sb("Gt", (8, TT))
    Gnb = sb("Gnb", (8, TT), BF16)
    carry = sb("carry", (8, 1))
    negbf = sb("negbf", (8, 1))
    bf_sb = sb("bf_sb", (8, 1))
    lstage = sb("lstage", (128, NSUB, 8))
    VS = [sg[i][:, 0:128] for i in range(3)]
    VT = [sb("VT%d" % i, (128, 128)) for i in range(3)]
    modT = sb("modT", (128, NSEQ, 48))
    cT = sb("cT", (128, 8, NSEQ))
    sc1p = sb("sc1p", (128, NSEQ, 8))
    G2 = sb("G2", (128, NSEQ, 8))
    B2 = sb("B2", (128, NSEQ, 8))
    AG = sb("AG", (128, 8))
    AB = sb("AB", (128, 8))
    hist = sb("hist", (128, NFF, 2))
    hist44 = sb("hist44", (128, 44))
    st44 = sb("st44", (44, 128))
    cvst = attn_tm[0:32, 0, :]
    stat = sb("stat", (128, NSUB, 2, 6))
    mv = sb("mv", (128, NSUB, 2))
    rstd = sb("rstd", (128, NSUB, 1))
    nbias = sb("nbias", (128, NSUB, 1))
    rc = sb("rc", (128, NSUB, 1))
    lc = sb("lc", (128, 2, 8))

    psb = [stack.enter_context(nc.psum_tensor("ps%d" % i, [128, 512], F32)) for i in range(8)]

    Pg = Prog(nc)
    op = Pg.op

    def ckpt(k):
        if stop_after is not None and k >= stop_after:
            Pg.emit(stack)
            stack.close()
            raise _Stop(nc)

    bank_ctr = [0]
    held = set()

    def bank(hold=False):
        while True:
            b = bank_ctr[0] % 6
            bank_ctr[0] += 1
            if b not in held:
                break
        if hold:
            held.add(b)
        return b

    def pr_(b):
        return ("ps", b)

    unit_ctr = [0]

    def alloc_units(nelem):
        nu = 1 if nelem <= 1024 else 2
        if nu == 2 and unit_ctr[0] % 2 == 1:
            unit_ctr[0] += 1
        u = unit_ctr[0] % NUNIT
        unit_ctr[0] += nu
        return u, [("unit", u + k) for k in range(nu)]

    def load_piece(name):
        Wd, ents = pieces[name]
        nn_ = len(ents) * Wd
        u, rs = alloc_units(nn_)
        Pg.dma("sp", ring[:, u * 1024:u * 1024 + nn_], wscr[pidx[name]][:, 0:nn_], ("unit", u),
               reads=[("wscr", name)], writes=rs)
        view = ring[:, u * 1024:u * 1024 + nn_].rearrange("p (e w) -> p e w", w=Wd)
        return rs, view

    Pg.dma("sp", ident[:, :], ident_d, "c_ident", writes=["ident"])
    Pg.dma("sp", maskf[:, :], mask_d, "c_mask", writes=["maskf"])
    op("dve", lambda e: e.tensor_copy(maskb[:, :], maskf[:, :]), ["maskf"], ["maskb"])
    op("dve", lambda e: e.memset(onesb[:, :], 1.0), [], ["onesb"])
    op("dve", lambda e: e.memset(zerob[:, :], 0.0), [], ["zerob"])
    op("dve", lambda e: e.memset(ones8[:, :], 1.0), [], ["ones8"])
    op("pool", lambda e: e.memset(KA[64:128, :, :].rearrange("p a b -> p (a b)"), 0.0), [], [("KA1",)])
    op("pool", lambda e: e.memset(KA[64:65, :, :].rearrange("p a b -> p (a b)"), 1.0), [], [("KA1",)])
    op("pool", lambda e: e.memset(QA[64:128, :, :].rearrange("p a b -> p (a b)"), 0.0), [],
       [("QAg", h) for h in range(H)])
    Pg.dma("sp", ln2gb[:, :], ln2_g.partition_broadcast(128), "c_l2g", writes=["ln2gb"])
    Pg.dma("sp", ln2bb[:, :], ln2_b.partition_broadcast(128), "c_l2b", writes=["ln2bb"])
    Pg.dma("sp", bf_sb[:, :], b_f.rearrange("(a b) -> a b", b=1), "c_bf", writes=["bf_sb"])
    op("act", lambda e: e.mul(negbf[:, :], bf_sb[:, :], -1.0), ["bf_sb"], ["negbf"])

    Pg.dma("sp", VS[0][0:124, :], conv_w.rearrange("j (c p) -> (j c) p", p=128), "c_vs0",
           writes=[("VS", 0)])
    r = 0
    VS1map = {}
    for nm, ap_, n in (("conv_b", conv_b, 4), ("conv_ln_g", conv_ln_g, 4), ("conv_ln_b", conv_ln_b, 4),
                       ("ln1_g", ln1_g, 8), ("ln1_b", ln1_b, 8), ("ffn_conv_b", ffn_conv_b, NFF)):
        Pg.dma("sp", VS[1][r:r + n, :], ap_.rearrange("(a b) -> a b", b=128), "c_vs1_" + nm,
               writes=[("VS", 1)])
        VS1map[nm] = r
        r += n
    VS1map["c"] = r
    for kc in range(8):
        Pg.dma("sp", VS[1][r + kc * NSEQ:r + (kc + 1) * NSEQ, :], call[:, kc * 128:(kc + 1) * 128],
               "c_vs1_c%d" % kc, writes=[("VS", 1)])
    r1rows = r + 8 * NSEQ
    Pg.dma("sp", VS[2][0:48, :], b_ada.rearrange("(a b) -> a b", b=128), "c_vs2a", writes=[("VS", 2)])
    Pg.dma("sp", VS[2][48:48 + 66, :], ffn_conv_w.rearrange("j (i p) -> (j i) p", p=128), "c_vs2b",
           writes=[("VS", 2)])
    for i, nrows in ((0, 124), (1, r1rows), (2, 114)):
        b = bank()
        op("pe", lambda e, i=i, nrows=nrows, b=b: e.transpose(psb[b][:, 0:nrows], VS[i][0:nrows, :],
                                                              ident[0:nrows, 0:nrows]),
           [("VS", i), "ident"], [pr_(b)])
        op("dve", lambda e, i=i, nrows=nrows, b=b: e.tensor_copy(VT[i][:, 0:nrows], psb[b][:, 0:nrows]),
           [pr_(b)], [("VT", i)])

    def cw(c, j):
        return VT[0][:, j * 4 + c:j * 4 + c + 1]

    def v1(nm, i):
        return VT[1][:, VS1map[nm] + i:VS1map[nm] + i + 1]

    def fcw(j, i):
        return VT[2][:, 48 + j * NFF + i:48 + j * NFF + i + 1]

    cb0 = VS1map["c"]
    op("dve", lambda e: e.tensor_copy(cT[:, :, :].rearrange("p k s -> p (k s)"),
                                      VT[1][:, cb0:cb0 + 8 * NSEQ]), [("VT", 1)], ["cT"])

    ckpt(1)
    VAf = VA[:, :, :, :].rearrange("p a b c -> p (a b c)").bitcast(F32)
    R1_NAMES = ([("hc", c) for c in range(4)] + [("hcs", c) for c in range(4)]
                + ["hcb", "sqb", "lnm", "lnv", "lnr"] + [("hT", i) for i in range(NFF)])
    stg = [bufs[0][:, :], bufs[1][:, :], R1f[:, 0:2048]]
    stg_alias = [[("B", 0, q) for q in range(8)], [("B", 1, q) for q in range(8)], R1_NAMES]
    if NB * H * 66 // 2 >= 4096:
        stg += [VAf[:, 0:2048], VAf[:, 2048:4096]]
        stg_alias += [[("VA", b) for b in range(NB)], [("VA", b) for b in range(NB)]]
    NSTG = len(stg)

    def stage_fence(sgi):
        nm = stg_alias[sgi] + [("stg", sgi, ei) for ei in range(24)]
        op("sp", lambda e: e.nop(), [], nm)

    for sgi in range(NSTG):
        stage_fence(sgi)
    bm = bank(hold=True)
    jobs = []
    mod_it = iter(range(24))
    for pi, name in enumerate(porder):
        jobs.append(("conv", name))
        if pi % 4 == 3:
            pc = next(mod_it, None)
            if pc is not None:
                jobs.append(("mod", pc))
    for pc in mod_it:
        jobs.append(("mod", pc))
    ncv = 0
    for ji, (kind, arg) in enumerate(jobs):
        sgi = ji % NSTG
        if kind == "conv":
            name = arg
            Wd, ents = pieces[name]
            Pg.dma_multi("sp", [(stg[sgi][:, ei * Wd:(ei + 1) * Wd],
                                 W[mname][rcx * 128:(rcx + 1) * 128, c0:c0 + Wd])
                                for ei, (mname, rcx, c0) in enumerate(ents)], ("stg", sgi),
                         writes=[("stg", sgi, ei) for ei in range(len(ents))])
            n = len(ents) * Wd
            u, rs = alloc_units(n)
            rd = [("stg", sgi, ei) for ei in range(len(ents))]
            if ncv % 2 == 0:
                op("dve", lambda e, u=u, sgi=sgi, n=n: e.tensor_copy(ring[:, u * 1024:u * 1024 + n],
                                                                     stg[sgi][:, 0:n]), rd, rs)
            else:
                op("act", lambda e, u=u, sgi=sgi, n=n: e.copy(ring[:, u * 1024:u * 1024 + n], stg[sgi][:, 0:n]),
                   rd, rs)
            ncv += 1
            Pg.dma("pool", wscr[pidx[name]][:, 0:n], ring[:, u * 1024:u * 1024 + n], ("unitst", u),
                   reads=rs, writes=[("wscr", name)])
        else:
            pc = arg
            Pg.dma_multi("sp", [(stg[sgi][:, kc * 256:(kc + 1) * 256],
                                 W["w_ada"][kc * 128:(kc + 1) * 128, pc * 256:(pc + 1) * 256])
                                for kc in range(8)], ("stg", sgi), writes=[("stg", sgi, kc) for kc in range(8)])
            for mm_ in range(2):
                m = pc * 2 + mm_
                for kc in range(8):
                    op("pe", lambda e, sgi=sgi, kc=kc, mm_=mm_, m=m: e.matmul(
                        psb[bm][:, m:m + 48 * (NSEQ - 1) + 1:48],
                        stg[sgi][:, kc * 256 + mm_ * 128:kc * 256 + (mm_ + 1) * 128],
                        cT[:, kc, :], start=(kc == 0), stop=(kc == 7)),
                       [("stg", sgi, kc), "cT"], [pr_(bm)])
    ckpt(2)
    for s in range(NSEQ):
        op("dve", lambda e, s=s: e.tensor_tensor(out=modT[:, s, :], in0=psb[bm][:, s * 48:(s + 1) * 48],
                                                 in1=VT[2][:, 0:48], op=ALU.add),
           [pr_(bm), ("VT", 2)], ["modT"])
    held.discard(bm)
    for sgi in range(NSTG):
        stage_fence(sgi)
    op("pool", lambda e: e.memset(VA[:, :, :, :].rearrange("p a b c -> p (a b c)"), 1.0), [],
       [("VA", b) for b in range(NB)])
    g0 = VS1map["ln1_g"]
    b0 = VS1map["ln1_b"]
    for s in range(NSEQ):
        op("dve", lambda e, s=s: e.tensor_scalar_add(sc1p[:, s, :], modT[:, s, 8:16], 1.0), ["modT"], ["sc1p"])
        op("dve", lambda e, s=s: e.tensor_scalar_add(G2[:, s, :], modT[:, s, 32:40], 1.0), ["modT"], ["G2"])
        op("dve", lambda e, s=s: e.tensor_tensor(out=B2[:, s, :], in0=G2[:, s, :], in1=VT[1][:, b0:b0 + 8],
                                                 op=ALU.mult), ["G2", ("VT", 1)], ["B2"])
        op("dve", lambda e, s=s: e.tensor_tensor(out=B2[:, s, :], in0=B2[:, s, :], in1=modT[:, s, 24:32],
                                                 op=ALU.add), ["B2", "modT"], ["B2"])
        op("dve", lambda e, s=s: e.tensor_tensor(out=G2[:, s, :], in0=G2[:, s, :], in1=VT[1][:, g0:g0 + 8],
                                                 op=ALU.mult), ["G2", ("VT", 1)], ["G2"])
    op("dve", lambda e: e.tensor_scalar_mul(AG[:, :], VT[1][:, g0:g0 + 8], ALPHA), [("VT", 1)], ["AG"])
    op("dve", lambda e: e.tensor_scalar_mul(AB[:, :], VT[1][:, b0:b0 + 8], ALPHA), [("VT", 1)], ["AB"])

    ckpt(3)
    fdummy = sb("fdummy", (1, 8))

    def r1_fence():
        op("dve", lambda e: e.memset(fdummy[:, :], 0.0), [], R1_NAMES + ["fdummy"])

    def subs_of(n):
        out = []
        off = 0
        j = 0
        while off < n:
            m = min(128, n - off)
            out.append((j, off, m))
            off += m
            j += 1
        return out

    def append_G(n, kpos, lf_out=None, to_QA=False):
        append_G1(n, to_QA)
        append_G2(n, kpos, lf_out)

    def append_G1(n, to_QA):
        op("dve", lambda e: e.tensor_tensor_scan(out=Gt[:, 0:n], data0=ones8[:, 0:n], data1=l_sb[:, 0:n],
                                                 initial=carry[:, 0:1], op0=ALU.mult, op1=ALU.add),
           ["l_sb", "ones8", "carry"], ["Gt"])
        op("dve", lambda e: e.tensor_copy(carry[:, 0:1], Gt[:, n - 1:n]), ["Gt"], ["carry"])
        if to_QA:
            op("act", lambda e: e.mul(Gnb[:, 0:n], Gt[:, 0:n], -1.0), ["Gt"], ["Gnb"])
            for h in range(H):
                Pg.dma("pool", QA[64:65, h, 0:n], Gnb[h:h + 1, 0:n], ("qag", h), reads=["Gnb"],
                       writes=[("QAg", h)])

    def append_G2(n, kpos, lf_out):
        b = bank()
        sl = subs_of(n)
        for (j, off, m) in sl:
            if lf_out is not None:
                op("pe", lambda e, j=j, off=off, m=m: e.transpose(psb[b][0:m, j * 16:j * 16 + 8],
                                                                  l_sb[0:8, off:off + m], ident[0:8, 0:8]),
                   ["l_sb", "ident"], [pr_(b)])
            op("pe", lambda e, j=j, off=off, m=m: e.transpose(psb[b][0:m, j * 16 + 8:j * 16 + 16],
                                                              Gt[0:8, off:off + m], ident[0:8, 0:8]),
               ["Gt", "ident"], [pr_(b)])
        for (j, off, m) in sl:
            blk = (kpos + off) // 128
            if lf_out is not None:
                op("act", lambda e, j=j, m=m: e.mul(lstage[0:m, j, :], psb[b][0:m, j * 16:j * 16 + 8], -1.0),
                   [pr_(b)], [("lstage", j)])
                Pg.dma("pool", lf_out[off:off + m, :], lstage[0:m, j, :], ("lst", j), reads=[("lstage", j)])
            op("act", lambda e, j=j, m=m, blk=blk: e.copy(Gk[0:m, blk, :],
                                                          psb[b][0:m, j * 16 + 8:j * 16 + 16]),
               [pr_(b)], [("Gk", blk)])

    def layer_norm_tile(sl, xs, RX):
        for (j, off, m) in sl:
            for hh in range(2):
                op("dve", lambda e, hh=hh, j=j, m=m: e.bn_stats(stat[0:m, j, hh, :],
                                                                xs[0:m, j, hh * 512:(hh + 1) * 512]),
                   [*RX(j)], [("stat", j)])
        for (j, off, m) in sl:
            op("dve", lambda e, j=j, m=m: e.bn_aggr(mv[0:m, j, :], stat[0:m, j, :, :].rearrange("p a b -> p (a b)")),
               [("stat", j)], [("mv", j)])
            op("dve", lambda e, j=j, m=m: e.tensor_scalar_add(rstd[0:m, j, :], mv[0:m, j, 1:2], EPS),
               [("mv", j)], [("rstd", j)])
        for (j, off, m) in sl:
            op("act", lambda e, j=j, m=m: e.sqrt(rstd[0:m, j, :], rstd[0:m, j, :]), [("rstd", j)], [("rstd", j)])
        for (j, off, m) in sl:
            op("dve", lambda e, j=j, m=m: e.reciprocal(rstd[0:m, j, :], rstd[0:m, j, :]),
               [("rstd", j)], [("rstd", j)])
            op("dve", lambda e, j=j, m=m: e.scalar_tensor_tensor(out=nbias[0:m, j, :], in0=mv[0:m, j, 0:1],
                                                                 scalar=-1.0, in1=rstd[0:m, j, :], op0=ALU.mult,
                                                                 op1=ALU.mult),
               [("mv", j), ("rstd", j)], [("nbias", j)])
        for (j, off, m) in sl:
            op("act", lambda e, j=j, m=m: e.activation(xs[0:m, j, :], xs[0:m, j, :], AF.Identity,
                                                       bias=nbias[0:m, j, :], scale=rstd[0:m, j, :]),
               [*RX(j), ("rstd", j), ("nbias", j)], [*RX(j)])

    tile_ctr = [0]

    def tile_head(si, x_src, t0, n):
        sl = subs_of(n)
        par = tile_ctr[0] % 2
        tile_ctr[0] += 1
        xs = xs_views[par]

        def RX(j):
            return [("B", par, q) for q in range(4 * j, 4 * j + 4)]
        for (j, off, m) in sl:
            Pg.dma("sp", xs[0:m, j, :], x_src[t0 + off:t0 + off + m, :], ("xs", par, j), writes=RX(j))
        for kc in range(8):
            b = bank()
            for (j, off, m) in sl:
                op("pe", lambda e, kc=kc, j=j, off=off, m=m, b=b: e.transpose(
                    psb[b][:, off:off + m], xs[0:m, j, kc * 128:(kc + 1) * 128], ident[0:m, 0:m]),
                   [*RX(j), "ident"], [pr_(b)])
            op("act", lambda e, kc=kc, b=b: e.activation(uT[:, kc, 0:n], psb[b][:, 0:n], AF.Identity,
                                                         bias=modT[:, si, kc:kc + 1],
                                                         scale=sc1p[:, si, kc:kc + 1]),
               [pr_(b), "modT", "sc1p"], [("uT", kc)])
        return par

    def tile(si, x_src, t0, n, kpos, outs, last, par, next_head):
        (y_o, k_o, v_o, lf_o, cv_o, ff_o) = outs
        sl = subs_of(n)
        nsub = len(sl)
        xs = xs_views[par]
        x1a = x1a_views[1 - par]
        next_par = [None]

        def RX(j):
            return [("B", par, q) for q in range(4 * j, 4 * j + 4)]

        def RA(kc):
            return [("B", 1 - par, kc)]
        uT_all = [("uT", kc) for kc in range(8)]
        kblks = sorted(set((kpos + off) // 128 for (_, off, _) in sl))
        ckpt(13)
        s, wv = load_piece("f")
        b = bank()
        for kc in range(8):
            op("pe", lambda e, kc=kc, b=b, wv=wv: e.matmul(psb[b][0:8, 0:n], wv[:, kc, :], uT[:, kc, 0:n],
                                                           start=(kc == 0), stop=(kc == 7)),
               [*s, ("uT", kc)], [pr_(b)])
        op("act", lambda e, b=b: e.activation(e_sb[:, 0:n], psb[b][0:8, 0:n], AF.Exp, bias=negbf[:, 0:1],
                                              scale=-1.0), [pr_(b), "negbf"], ["e_sb"])
        op("act", lambda e: e.activation(l_sb[:, 0:n], e_sb[:, 0:n], AF.Ln, bias=1.0, scale=1.0),
           ["e_sb"], ["l_sb"])
        append_G1(n, True)
        ckpt(14)
        for cc in range(2):
            sa, wa = load_piece("ga%d" % cc)
            sb_, wb = load_piece("gb%d" % cc)
            for cl in range(2):
                c = cc * 2 + cl
                ba = bank()
                bb_ = bank()
                for kc in range(8):
                    op("pe", lambda e, kc=kc, cl=cl, ba=ba, wa=wa: e.matmul(
                        psb[ba][:, 0:n], wa[:, kc, cl * 128:(cl + 1) * 128], uT[:, kc, 0:n],
                        start=(kc == 0), stop=(kc == 7)), [*sa, ("uT", kc)], [pr_(ba)])
                for kc in range(8):
                    op("pe", lambda e, kc=kc, cl=cl, bb_=bb_, wb=wb: e.matmul(
                        psb[bb_][:, 0:n], wb[:, kc, cl * 128:(cl + 1) * 128], uT[:, kc, 0:n],
                        start=(kc == 0), stop=(kc == 7)), [*sb_, ("uT", kc)], [pr_(bb_)])
                g = sg[c % 2]
                op("act", lambda e, bb_=bb_, g=g: e.activation(g[:, 0:n], psb[bb_][:, 0:n], AF.Sigmoid),
                   [pr_(bb_)], [("sg", c % 2)])
                op("dve", lambda e, c=c, ba=ba, g=g: e.tensor_tensor(
                    out=gluX[:, c, CK - 1:CK - 1 + n], in0=psb[ba][:, 0:n], in1=g[:, 0:n], op=ALU.mult),
                   [pr_(ba), ("sg", c % 2)], [("gluXn", c)])
        ckpt(15)
        r1_fence()
        conv_thunks = []
        for c in range(4):
            conv_thunks.append((lambda e, c=c: e.tensor_scalar(
                out=hc[:, c, 0:n], in0=gluX[:, c, 0:n], scalar1=cw(c, 0), scalar2=v1("conv_b", c),
                op0=ALU.mult, op1=ALU.add),
                [("gluXn", c), ("gluXh", c), ("VT", 0), ("VT", 1)], [("hc", c)]))
        for jt in range(1, CK):
            for c in range(4):
                conv_thunks.append((lambda e, c=c, jt=jt: e.scalar_tensor_tensor(
                    out=hc[:, c, 0:n], in0=gluX[:, c, jt:jt + n], scalar=cw(c, jt), in1=hc[:, c, 0:n],
                    op0=ALU.mult, op1=ALU.add), [("gluXn", c), ("gluXh", c), ("hc", c)], [("hc", c)]))
        conv_pos = [0]

        def conv_emit(k):
            for (fn, rd, wr) in conv_thunks[conv_pos[0]:conv_pos[0] + k]:
                op("dve", fn, rd, wr)
            conv_pos[0] += k
        conv_emit(64)
        ckpt(10)
        for i in range(2):
            s, wv = load_piece("q%d" % i)
            for hl in range(4):
                h = i * 4 + hl
                b = bank()
                for kc in range(8):
                    op("pe", lambda e, kc=kc, hl=hl, b=b, wv=wv: e.matmul(
                        psb[b][0:64, 0:n], wv[:, kc, hl * 64:(hl + 1) * 64], uT[:, kc, 0:n],
                        start=(kc == 0), stop=(kc == 7)), [*s, ("uT", kc)], [pr_(b)])
                op("act", lambda e, h=h, b=b: e.mul(QA[0:64, h, 0:n], psb[b][0:64, 0:n], 0.125),
                   [pr_(b)], [("QA", h)])
        append_G2(n, kpos, lf_o[t0:t0 + n, :])
        ckpt(11)
        bk = [bank(hold=True) for _ in sl]
        for i in range(2):
            s, wv = load_piece("k%d" % i)
            for hl in range(4):
                h = i * 4 + hl
                b = bank()
                for kc in range(8):
                    op("pe", lambda e, kc=kc, hl=hl, b=b, wv=wv: e.matmul(
                        psb[b][0:64, 0:n], wv[:, kc, hl * 64:(hl + 1) * 64], uT[:, kc, 0:n],
                        start=(kc == 0), stop=(kc == 7)), [*s, ("uT", kc)], [pr_(b)])
                op("act", lambda e, h=h, b=b: e.copy(KA[0:64, h, kpos:kpos + n], psb[b][0:64, 0:n]),
                   [pr_(b)], [("KA", h, bb) for bb in kblks])
            for (j, off, m) in sl:
                for kc in range(8):
                    op("pe", lambda e, kc=kc, j=j, off=off, m=m, wv=wv, i=i: e.matmul(
                        psb[bk[j]][0:m, i * 256:(i + 1) * 256], uT[:, kc, off:off + m], wv[:, kc, :],
                        start=(kc == 0), stop=(kc == 7)), [*s, ("uT", kc)], [pr_(bk[j])])
        for (j, off, m) in sl:
            st = kst[j % 2]
            op("act", lambda e, j=j, m=m, st=st: e.copy(st[0:m, :], psb[bk[j]][0:m, :]),
               [pr_(bk[j])], [("kst", j % 2)])
            Pg.dma("pool", k_o[t0 + off:t0 + off + m, :], st[0:m, :], ("kst", j % 2), reads=[("kst", j % 2)])
            held.discard(bk[j])
        ckpt(12)
        bv = [bank(hold=True) for _ in sl]
        for i in range(2):
            s, wv = load_piece("v%d" % i)
            for (j, off, m) in sl:
                for kc in range(8):
                    op("pe", lambda e, kc=kc, j=j, off=off, m=m, wv=wv, i=i: e.matmul(
                        psb[bv[j]][0:m, i * 256:(i + 1) * 256], uT[:, kc, off:off + m], wv[:, kc, :],
                        start=(kc == 0), stop=(kc == 7)), [*s, ("uT", kc)], [pr_(bv[j])])
        for (j, off, m) in sl:
            st = vst[j % 2]
            blk = (kpos + off) // 128
            op("act", lambda e, j=j, m=m, st=st: e.copy(st[0:m, :], psb[bv[j]][0:m, :]),
               [pr_(bv[j])], [("vst", j % 2)])
            op("dve", lambda e, j=j, m=m, blk=blk, st=st: e.tensor_copy(
                VA[0:m, blk, :, 0:64], st[0:m, :].rearrange("p (h d) -> p h d", d=64)),
               [("vst", j % 2)], [("VA", blk)])
            Pg.dma("pool", v_o[t0 + off:t0 + off + m, :], st[0:m, :], ("vst", j % 2), reads=[("vst", j % 2)])
            held.discard(bv[j])
        if last:
            b = bank()
            for c in range(4):
                op("pe", lambda e, c=c, b=b: e.transpose(psb[b][0:CK - 1, c * 128:(c + 1) * 128],
                                                         gluX[:, c, n:n + CK - 1], ident[:, :]),
                   [("gluXn", c), ("gluXh", c), "ident"], [pr_(b)])
            op("act", lambda e, b=b: e.copy(cvst[0:CK - 1, :], psb[b][0:CK - 1, :]), [pr_(b)], ["cvst", ("attn_tm", 0)])
            Pg.dma("pool", cv_o, cvst[0:CK - 1, :], "cvst", reads=["cvst", ("attn_tm", 0)])
        ckpt(16)
        qa0 = kpos
        nkb = (kpos + n + 127) // 128
        m0 = sl[0][2]
        items = []
        for h in range(H):
            first = True
            for kb in range(nkb):
                ks = kb * 128
                kn = min(128, kpos + n - ks)
                need = [(j, off, m) for (j, off, m) in sl if ks <= qa0 + off + m - 1]
                if not need:
                    continue
                items.append(dict(h=h, kb=kb, ks=ks, kn=kn, need=need, c0=need[0][1], first=first, last=False))
                first = False
            items[-1]["last"] = True
        LOOK = 4

        def emit_qk(it, idx):
            b = bank()
            it["b"] = b
            it["pi"] = idx % NPT
            h, ks, kn, c0, kb = it["h"], it["ks"], it["kn"], it["c0"], it["kb"]
            op("pe", lambda e, h=h, ks=ks, kn=kn, c0=c0, b=b: e.matmul(
                psb[b][0:kn, c0:n], KA[0:128, h, ks:ks + kn], QA[0:128, h, c0:n], start=True, stop=True),
               [("KA", h, kb), ("KA1",), ("QA", h), ("QAg", h)], [pr_(b)])

        per_head = (len(conv_thunks) - conv_pos[0] + H - 1) // H
        for idx in range(min(LOOK, len(items))):
            emit_qk(items[idx], idx)
        for idx, it in enumerate(items):
            if idx + LOOK < len(items):
                emit_qk(items[idx + LOOK], idx + LOOK)
            h, kb, ks, kn, need, c0, b, pi_ = (it["h"], it["kb"], it["ks"], it["kn"], it["need"], it["c0"],
                                                it["b"], it["pi"])
            ob = 6 + (h % 2)
            O = psb[ob][:, 0:nsub * 65].rearrange("p (j d) -> p j d", d=65)
            if it["first"]:
                conv_emit(per_head)
                op("pe", lambda e, ob=ob: e.matmul(psb[ob][0:m0, 0:nsub * 65], zerob[:, 0:m0],
                                                   zerob[:, 0:nsub * 65], start=True, stop=False,
                                                   skip_group_check=True), ["zerob"], [pr_(ob)])
            pt = PT[pi_]
            op("act", lambda e, h=h, kb=kb, kn=kn, c0=c0, b=b, pt=pt: e.activation(
                pt[0:kn, c0:n], psb[b][0:kn, c0:n], AF.Exp, bias=Gk[0:kn, kb, h:h + 1], scale=1.0),
               [pr_(b), ("Gk", kb)], [("PT", pi_)])
            for (j, off, m) in need:
                if ks + kn - 1 > qa0 + off:
                    op("pool", lambda e, kn=kn, off=off, m=m, pt=pt: e.tensor_tensor(
                        out=pt[0:kn, off:off + m], in0=pt[0:kn, off:off + m], in1=maskb[0:kn, 0:m],
                        op=ALU.mult), [("PT", pi_), "maskb"], [("PT", pi_)])
            for (j, off, m) in need:
                op("pe", lambda e, h=h, kb=kb, kn=kn, j=j, off=off, m=m, O=O, pt=pt: e.matmul(
                    O[0:m, j, :], pt[0:kn, off:off + m], VA[0:kn, kb, h, 0:65], start=False, stop=True,
                    skip_group_check=True), [("PT", pi_), ("VA", kb)], [pr_(ob)])
            if it["last"]:
                for (j, off, m) in sl:
                    op("dve", lambda e, j=j, m=m, O=O: e.reciprocal(rc[0:m, j, :], O[0:m, j, 64:65]),
                       [pr_(ob)], ["rc"])
                    op("dve", lambda e, j=j, m=m, O=O, h=h: e.tensor_scalar(
                        out=attn_tm[0:m, j, h * 64:(h + 1) * 64], in0=O[0:m, j, 0:64], scalar1=rc[0:m, j, :],
                        scalar2=None, op0=ALU.mult), [pr_(ob), "rc"], [("attn_tm", j)])
        conv_emit(len(conv_thunks))
        if not last:
            for c in range(4):
                op("pool", lambda e, c=c: e.tensor_copy(gluX[:, c, 0:CK - 1], gluX[:, c, n:n + CK - 1]),
                   [("gluXn", c), ("gluXh", c)], [("gluXh", c)])
        for c in range(4):
            b = bank()
            for (j, off, m) in sl:
                op("pe", lambda e, c=c, j=j, off=off, m=m, b=b: e.transpose(
                    psb[b][:, off:off + m], attn_tm[0:m, j, c * 128:(c + 1) * 128], ident[0:m, 0:m]),
                   [("attn_tm", j), "ident"], [pr_(b)])
            op("act", lambda e, c=c, b=b: e.copy(attnT[:, c, 0:n], psb[b][:, 0:n]), [pr_(b)], [("attnT", c)])
        ckpt(17)
        op("act", lambda e: e.copy(hcb[:, :, 0:n], hc[:, :, 0:n]), [("hc", c) for c in range(4)], ["hcb"])
        op("act", lambda e: e.activation(sqb[:, :, 0:n], hc[:, :, 0:n], AF.Square),
           [("hc", c) for c in range(4)], ["sqb"])
        bs_ = bank()
        bq_ = bank()
        for c in range(4):
            op("pe", lambda e, c=c: e.matmul(psb[bs_][:, 0:n], onesb[:, :], hcb[:, c, 0:n], start=(c == 0),
                                             stop=(c == 3)), ["hcb", "onesb"], [pr_(bs_)])
        for c in range(4):
            op("pe", lambda e, c=c: e.matmul(psb[bq_][:, 0:n], onesb[:, :], sqb[:, c, 0:n], start=(c == 0),
                                             stop=(c == 3)), ["sqb", "onesb"], [pr_(bq_)])
        op("act", lambda e: e.mul(lnm[:, 0:n], psb[bs_][:, 0:n], 1.0 / 512), [pr_(bs_)], ["lnm"])
        op("dve", lambda e: e.tensor_tensor(out=lnv[:, 0:n], in0=lnm[:, 0:n], in1=lnm[:, 0:n], op=ALU.mult),
           ["lnm"], ["lnv"])
        op("dve", lambda e: e.scalar_tensor_tensor(out=lnv[:, 0:n], in0=psb[bq_][:, 0:n], scalar=1.0 / 512,
                                                   in1=lnv[:, 0:n], op0=ALU.mult, op1=ALU.subtract),
           [pr_(bq_), "lnv"], ["lnv"])
        op("dve", lambda e: e.tensor_scalar(out=lnv[:, 0:n], in0=lnv[:, 0:n], scalar1=0.0, scalar2=EPS,
                                            op0=ALU.max, op1=ALU.add), ["lnv"], ["lnv"])
        op("act", lambda e: e.activation(lnr[:, 0:n], lnv[:, 0:n], AF.Ln), ["lnv"], ["lnr"])
        op("act", lambda e: e.activation(lnr[:, 0:n], lnr[:, 0:n], AF.Exp, scale=-0.5), ["lnr"], ["lnr"])
        for c in range(4):
            op("dve", lambda e, c=c: e.tensor_tensor(out=hc[:, c, 0:n], in0=hc[:, c, 0:n], in1=lnm[:, 0:n],
                                                     op=ALU.subtract), [("hc", c), "lnm"], [("hc", c)])
            op("dve", lambda e, c=c: e.tensor_tensor(out=hc[:, c, 0:n], in0=hc[:, c, 0:n], in1=lnr[:, 0:n],
                                                     op=ALU.mult), [("hc", c), "lnr"], [("hc", c)])
            op("act", lambda e, c=c: e.activation(hcb[:, c, 0:n], hc[:, c, 0:n], AF.Silu,
                                                  bias=v1("conv_ln_b", c), scale=v1("conv_ln_g", c)),
               [("hc", c), ("VT", 1)], [("hcs", c), "hcb"])
        ckpt(18)
        for dch in range(8):
            sA, wA = load_piece("gA%d" % dch)
            sB, wB = load_piece("gB%d" % dch)
            sP, wP = load_piece("pr%d" % dch)
            for dl in range(1):
                bA = bank()
                bB = bank()
                bya = bank()
                byb = bank()
                for kc in range(8):
                    op("pe", lambda e, kc=kc, dl=dl, bA=bA, wA=wA: e.matmul(
                        psb[bA][:, 0:n], wA[:, kc, dl * 128:(dl + 1) * 128], uT[:, kc, 0:n],
                        start=(kc == 0), stop=(kc == 7)), [*sA, ("uT", kc)], [pr_(bA)])
                for kc in range(8):
                    op("pe", lambda e, kc=kc, dl=dl, bB=bB, wB=wB: e.matmul(
                        psb[bB][:, 0:n], wB[:, kc, dl * 128:(dl + 1) * 128], uT[:, kc, 0:n],
                        start=(kc == 0), stop=(kc == 7)), [*sB, ("uT", kc)], [pr_(bB)])
                for c in range(4):
                    op("pe", lambda e, c=c, dl=dl, bya=bya, wP=wP: e.matmul(
                        psb[bya][:, 0:n], wP[:, c, dl * 128:(dl + 1) * 128], attnT[:, c, 0:n],
                        start=(c == 0), stop=(c == 3)), [*sP, ("attnT", c)], [pr_(bya)])
                for c in range(4):
                    op("pe", lambda e, c=c, dl=dl, byb=byb, wP=wP: e.matmul(
                        psb[byb][:, 0:n], wP[:, 4 + c, dl * 128:(dl + 1) * 128], hcb[:, c, 0:n],
                        start=(c == 0), stop=(c == 3)), [*sP, ("hcs", c)], [pr_(byb)])
                gA_ = sg[0 + 2 * (dch % 2)]
                gB_ = sg[1 + 2 * (dch % 2)]
                rA = ("sg", 0 + 2 * (dch % 2))
                rB = ("sg", 1 + 2 * (dch % 2))
                tt = t1[dch % 2]
                rT = ("t1", dch % 2)
                op("act", lambda e, bA=bA, gA_=gA_: e.activation(gA_[:, 0:n], psb[bA][:, 0:n], AF.Sigmoid),
                   [pr_(bA)], [rA])
                op("act", lambda e, bB=bB, gB_=gB_: e.activation(gB_[:, 0:n], psb[bB][:, 0:n], AF.Sigmoid),
                   [pr_(bB)], [rB])
                op("dve", lambda e, bya=bya, gA_=gA_, tt=tt: e.tensor_tensor(
                    out=tt[:, 0:n], in0=psb[bya][:, 0:n], in1=gA_[:, 0:n], op=ALU.mult),
                   [pr_(bya), rA], [rT])
                op("dve", lambda e, byb=byb, gB_=gB_: e.tensor_tensor(
                    out=gB_[:, 0:n], in0=psb[byb][:, 0:n], in1=gB_[:, 0:n], op=ALU.mult),
                   [pr_(byb), rB], [rB])
                op("dve", lambda e, dch=dch, gB_=gB_, tt=tt: e.tensor_tensor(
                    out=mergedT[:, dch, 0:n], in0=tt[:, 0:n], in1=gB_[:, 0:n], op=ALU.add),
                   [rT, rB], [("mergedT", dch)])
        ckpt(19)
        def g_stage1(dch):
            s, wv = load_piece("wo%d" % dch)
            b = bank()
            for kc in range(8):
                op("pe", lambda e, kc=kc, b=b, wv=wv: e.matmul(
                    psb[b][:, 0:n], wv[:, kc, 0:128], mergedT[:, kc, 0:n],
                    start=(kc == 0), stop=(kc == 7)), [*s, ("mergedT", kc)], [pr_(b)])
            tf = tmpf[dch % 2]
            rtf = ("tmpf", dch % 2)
            op("act", lambda e, dch=dch, b=b, tf=tf: e.activation(
                tf[:, 0:n], psb[b][:, 0:n], AF.Copy, scale=modT[:, si, 16 + dch:17 + dch]),
               [pr_(b), "modT"], [rtf])
            return tf, rtf

        def g_stage2(dch, tf, rtf):
            b2 = bank()
            for (j, off, m) in sl:
                op("pe", lambda e, j=j, off=off, m=m, b2=b2, tf=tf: e.transpose(
                    psb[b2][0:m, j * 128:(j + 1) * 128], tf[:, off:off + m], ident[:, :]),
                   [rtf, "ident"], [pr_(b2)])
            for (j, off, m) in sl:
                op("dve", lambda e, j=j, m=m, dch=dch, b2=b2: e.scalar_tensor_tensor(
                    out=xs[0:m, j, dch * 128:(dch + 1) * 128], in0=xs[0:m, j, dch * 128:(dch + 1) * 128],
                    scalar=ALPHA, in1=psb[b2][0:m, j * 128:(j + 1) * 128], op0=ALU.mult, op1=ALU.add),
                   [pr_(b2), *RX(j)], [*RX(j)])

        st_ = g_stage1(0)
        for dch in range(8):
            nxt_ = g_stage1(dch + 1) if dch + 1 < 8 else None
            g_stage2(dch, *st_)
            st_ = nxt_
        layer_norm_tile(sl, xs, RX)
        ckpt(20)
        for kc in range(8):
            b = bank()
            for (j, off, m) in sl:
                op("pe", lambda e, kc=kc, j=j, off=off, m=m, b=b: e.transpose(
                    psb[b][:, off:off + m], xs[0:m, j, kc * 128:(kc + 1) * 128], ident[0:m, 0:m]),
                   [*RX(j), "ident"], [pr_(b)])
            op("act", lambda e, kc=kc, b=b: e.activation(uT[:, kc, 0:n], psb[b][:, 0:n], AF.Identity,
                                                         bias=B2[:, si, kc:kc + 1], scale=G2[:, si, kc:kc + 1]),
               [pr_(b), "B2", "G2"], [("uT", kc)])
            op("act", lambda e, kc=kc, b=b: e.activation(x1a[:, kc, 0:n], psb[b][:, 0:n], AF.Identity,
                                                         bias=AB[:, kc:kc + 1], scale=AG[:, kc:kc + 1]),
               [pr_(b), "AB", "AG"], [*RA(kc)])
        ckpt(21)
        r1_fence()
        for i in range(NFF):
            s, wv = load_piece("upa%d" % i)
            s2, wv2 = load_piece("upv%d" % i)
            ba = bank()
            bv_ = bank()
            for kc in range(8):
                op("pe", lambda e, kc=kc, ba=ba, wv=wv: e.matmul(psb[ba][:, 0:n], wv[:, kc, :], uT[:, kc, 0:n],
                                                                 start=(kc == 0), stop=(kc == 7)),
                   [*s, ("uT", kc)], [pr_(ba)])
            for kc in range(8):
                op("pe", lambda e, kc=kc, bv_=bv_, wv2=wv2: e.matmul(psb[bv_][:, 0:n], wv2[:, kc, :],
                                                                     uT[:, kc, 0:n], start=(kc == 0),
                                                                     stop=(kc == 7)),
                   [*s2, ("uT", kc)], [pr_(bv_)])
            ab = a_sb[i % 2]
            rab = ("a_sb", i % 2)
            cb_ = cbuf[i % 2]
            rcb = ("cbuf", i % 2)
            ss = sbuf_s[i % 2]
            rss = ("sil", i % 2)
            op("pool", lambda e, i=i, ab=ab: e.tensor_copy(ab[:, 0:2], hist[:, i, :]), [("hist", i), rab], [rab])
            op("act", lambda e, ba=ba, ab=ab: e.copy(ab[:, 2:2 + n], psb[ba][:, 0:n]), [pr_(ba), rab], [rab])
            op("act", lambda e, i=i, ba=ba, cb_=cb_: e.activation(cb_[:, 0:n], psb[ba][:, 0:n], AF.Identity,
                                                                  bias=v1("ffn_conv_b", i), scale=fcw(2, i)),
               [pr_(ba), ("VT", 1), ("VT", 2)], [rcb])
            op("dve", lambda e, i=i, ab=ab, cb_=cb_: e.scalar_tensor_tensor(
                out=cb_[:, 0:n], in0=ab[:, 1:1 + n], scalar=fcw(1, i), in1=cb_[:, 0:n], op0=ALU.mult,
                op1=ALU.add), [rab, rcb], [rcb])
            op("dve", lambda e, i=i, ab=ab, cb_=cb_: e.scalar_tensor_tensor(
                out=cb_[:, 0:n], in0=ab[:, 0:n], scalar=fcw(0, i), in1=cb_[:, 0:n], op0=ALU.mult,
                op1=ALU.add), [rab, rcb], [rcb])
            op("pool", lambda e, i=i, ab=ab: e.tensor_copy(hist[:, i, :], ab[:, n:n + 2]), [rab], [("hist", i)])
            op("act", lambda e, cb_=cb_, ss=ss: e.activation(ss[:, 0:n], cb_[:, 0:n], AF.Silu), [rcb], [rss])
            op("dve", lambda e, i=i, bv_=bv_, ss=ss: e.tensor_tensor(out=hT[:, i, 0:n], in0=psb[bv_][:, 0:n],
                                                                     in1=ss[:, 0:n], op=ALU.mult),
               [pr_(bv_), rss], [("hT", i)])
        if last:
            op("dve", lambda e: e.tensor_copy(hist44[:, :].rearrange("p (r i) -> p i r", i=NFF), hist[:, :, :]),
               [("hist", i) for i in range(NFF)], ["hist44"])
            b = bank()
            op("pe", lambda e, b=b: e.transpose(psb[b][0:44, 0:128], hist44[:, :], ident[:, :]),
               ["hist44", "ident"], [pr_(b)])
            op("act", lambda e, b=b: e.copy(st44[:, :], psb[b][0:44, 0:128]), [pr_(b)], ["st44"])
            Pg.dma("pool", ff_o.rearrange("r (i p) -> (r i) p", p=128), st44[:, :], "st44", reads=["st44"])
        ckpt(22)
        def j_stage1(dch):
            b = bank()
            for part, (i0, i1) in enumerate(((0, 8), (8, 16), (16, 22))):
                s, wv = load_piece("dn%d_%d" % (dch, part))
                for il in range(i1 - i0):
                    i = i0 + il
                    op("pe", lambda e, il=il, i=i, b=b, wv=wv: e.matmul(
                        psb[b][:, 0:n], wv[:, il, :], hT[:, i, 0:n], start=(i == 0), stop=(i == NFF - 1)),
                       [*s, ("hT", i)], [pr_(b)])
            tf = tmpf[dch % 2]
            rtf = ("tmpf", dch % 2)
            op("dve", lambda e, dch=dch, b=b, tf=tf: e.scalar_tensor_tensor(
                out=tf[:, 0:n], in0=psb[b][:, 0:n], scalar=modT[:, si, 40 + dch:41 + dch], in1=x1a[:, dch, 0:n],
                op0=ALU.mult, op1=ALU.add), [pr_(b), "modT", *RA(dch)], [rtf])
            return tf, rtf

        def j_stage2(dch, tf, rtf):
            b2 = bank()
            for (j, off, m) in sl:
                op("pe", lambda e, j=j, off=off, m=m, b2=b2, tf=tf: e.transpose(
                    psb[b2][0:m, j * 128:(j + 1) * 128], tf[:, off:off + m], ident[:, :]),
                   [rtf, "ident"], [pr_(b2)])
            for (j, off, m) in sl:
                op("act", lambda e, j=j, m=m, dch=dch, b2=b2: e.copy(
                    xs[0:m, j, dch * 128:(dch + 1) * 128], psb[b2][0:m, j * 128:(j + 1) * 128]),
                   [pr_(b2)], [*RX(j)])

        st_ = j_stage1(0)
        for dch in range(8):
            nxt_ = j_stage1(dch + 1) if dch + 1 < 8 else None
            j_stage2(dch, *st_)
            st_ = nxt_
        if next_head is not None:
            next_par[0] = next_head()
        layer_norm_tile(sl, xs, RX)
        for (j, off, m) in sl:
            op("dve", lambda e, j=j, m=m: e.tensor_tensor(out=xs[0:m, j, :], in0=xs[0:m, j, :],
                                                          in1=ln2gb[0:m, :], op=ALU.mult),
               [*RX(j), "ln2gb"], [*RX(j)])
            op("dve", lambda e, j=j, m=m: e.tensor_tensor(out=xs[0:m, j, :], in0=xs[0:m, j, :],
                                                          in1=ln2bb[0:m, :], op=ALU.add),
               [*RX(j), "ln2bb"], [*RX(j)])
            Pg.dma("pool", y_o[t0 + off:t0 + off + m, :], xs[0:m, j, :], ("xs", par, j), reads=RX(j))
        return next_par[0]

    def init_seq(P_, is_sample):
        op("dve", lambda e: e.memset(carry[:, :], 0.0), [], ["carry"])
        if not is_sample:
            for c in range(4):
                op("pool", lambda e, c=c: e.memset(gluX[:, c, 0:CK - 1], 0.0), [], [("gluXh", c)])
            op("pool", lambda e: e.memset(hist[:, :, :], 0.0), [], [("hist", i) for i in range(NFF)])
            return
        Pg.dma("sp", cvst[0:CK - 1, :], st_conv, "cvst", writes=["cvst", ("attn_tm", 0)])
        b = bank()
        for c in range(4):
            op("pe", lambda e, c=c, b=b: e.transpose(psb[b][:, c * 32:c * 32 + CK - 1],
                                                     cvst[0:CK - 1, c * 128:(c + 1) * 128],
                                                     ident[0:CK - 1, 0:CK - 1]), ["cvst", ("attn_tm", 0), "ident"], [pr_(b)])
        for c in range(4):
            op("dve", lambda e, c=c, b=b: e.tensor_copy(gluX[:, c, 0:CK - 1], psb[b][:, c * 32:c * 32 + CK - 1]),
               [pr_(b)], [("gluXh", c)])
        Pg.dma("sp", st44[:, :], st_ffn.rearrange("r (i p) -> (r i) p", p=128), "st44", writes=["st44"])
        b = bank()
        op("pe", lambda e, b=b: e.transpose(psb[b][:, 0:44], st44[:, :], ident[0:44, 0:44]),
           ["st44", "ident"], [pr_(b)])
        op("dve", lambda e, b=b: e.tensor_copy(hist[:, :, :], psb[b][:, 0:44].rearrange("p (r i) -> p i r", i=NFF)),
           [pr_(b)], [("hist", i) for i in range(NFF)])
        nblk = P_ // 128
        for blk in range(nblk):
            kx = kst[blk % 2]
            vx = vst[blk % 2]
            Pg.dma("sp", kx[:, :], cache_k[blk * 128:(blk + 1) * 128, :], ("kstl", blk % 2),
                   writes=[("kst", blk % 2)])
            Pg.dma("sp", vx[:, :], cache_v[blk * 128:(blk + 1) * 128, :], ("vstl", blk % 2),
                   writes=[("vst", blk % 2)])
            for g in range(2):
                b = bank()
                for hl in range(4):
                    h = g * 4 + hl
                    op("pe", lambda e, h=h, hl=hl, b=b, kx=kx: e.transpose(
                        psb[b][0:64, hl * 128:(hl + 1) * 128], kx[:, h * 64:(h + 1) * 64], ident[:, :]),
                       [("kst", blk % 2), "ident"], [pr_(b)])
                op("act", lambda e, g=g, b=b, blk=blk: e.copy(
                    KA[0:64, g * 4:(g + 1) * 4, blk * 128:(blk + 1) * 128],
                    psb[b][0:64, :].rearrange("p (h k) -> p h k", k=128)),
                   [pr_(b)], [("KA", h, blk) for h in range(g * 4, g * 4 + 4)])
            op("dve", lambda e, blk=blk, vx=vx: e.tensor_copy(
                VA[:, blk, :, 0:64], vx[:, :].rearrange("p (h d) -> p h d", d=64)),
               [("vst", blk % 2)], [("VA", blk)])
        for ch in range(0, P_, TT):
            nn = min(TT, P_ - ch)
            b = bank()
            for (j, off, m) in subs_of(nn):
                Pg.dma("sp", lc[0:m, j % 2, :], cache_lf[ch + off:ch + off + m, :], ("lc", j % 2),
                       writes=[("lc", j % 2)])
                op("pe", lambda e, j=j, off=off, m=m, b=b: e.transpose(psb[b][0:8, off:off + m], lc[0:m, j % 2, :],
                                                                      ident[0:m, 0:m]),
                   [("lc", j % 2), "ident"], [pr_(b)])
            op("act", lambda e, b=b, nn=nn: e.mul(l_sb[:, 0:nn], psb[b][0:8, 0:nn], -1.0), [pr_(b)], ["l_sb"])
            append_G(nn, ch)

    tiles = []
    for si in range(NSEQ):
        is_sample = (si == NSEQ - 1)
        if is_sample:
            T_, P_ = TS, PAST
            x_src = xsm
            outs = (y_s, k_s, v_s, lf_s, cv_s, ff_s)
        else:
            T_, P_ = SEQ, 0
            x_src = xp[si]
            outs = (y_p[si], k_p[si], v_p[si], lf_p[si], cv_p[si], ff_p[si])
        t0 = 0
        while t0 < T_:
            n = min(TT, T_ - t0)
            tiles.append(dict(si=si, x_src=x_src, t0=t0, n=n, kpos=P_ + t0, outs=outs, last=(t0 + n >= T_),
                              first=(t0 == 0), P=P_, is_sample=is_sample))
            t0 += n
    par = None
    for ti, T in enumerate(tiles):
        if T["first"]:
            init_seq(T["P"], T["is_sample"])
        if par is None:
            par = tile_head(T["si"], T["x_src"], T["t0"], T["n"])
        nh = None
        if ti + 1 < len(tiles):
            N_ = tiles[ti + 1]
            nh = (lambda N_=N_: tile_head(N_["si"], N_["x_src"], N_["t0"], N_["n"]))
        par = tile(T["si"], T["x_src"], T["t0"], T["n"], T["kpos"], T["outs"], T["last"], par, nh)

    Pg.emit(stack)
    stack.close()
    return nc


_CONSTS = None


def _consts():
    ident = np.eye(128, dtype=np.float32)
    mask = np.triu(np.ones((128, 128), dtype=np.float32))
    return ident, mask


def make_in_maps(inputs, NPS, n_cores):
    ident, mask = _consts()
    f = lambda a: np.ascontiguousarray(np.asarray(a, dtype=np.float32))
    maps = []
    for c in range(n_cores):
        m = {}
        m["xp"] = f(inputs["x_prompt"][c * NPS:(c + 1) * NPS])
        m["xsm"] = f(inputs["x_sample"][c])
        m["c_all"] = f(np.concatenate([inputs["c_prompt"][c * NPS:(c + 1) * NPS],
                                       inputs["c_sample"][c:c + 1]], axis=0))
        P = inputs["cache_k"].shape[2]
        m["cache_k"] = f(inputs["cache_k"][0, c].reshape(P, 512))
        m["cache_v"] = f(inputs["cache_v"][0, c].reshape(P, 512))
        m["cache_logf"] = f(inputs["cache_logf"][0, c])
        m["state_conv"] = f(inputs["state_conv"][0, c])
        m["state_ffn"] = f(inputs["state_ffn_conv"][0, c])
        for nm in ("w_ada", "w_in", "w_attn_proj", "w_conv_proj", "w_out", "w_up", "w_down", "b_ada", "b_f",
                   "conv_w", "conv_b", "conv_ln_g", "conv_ln_b", "ln1_g", "ln1_b", "ffn_conv_w", "ffn_conv_b",
                   "ln2_g", "ln2_b"):
            m[nm] = f(inputs[nm][0])
        m["ident"] = ident
        m["mask"] = mask
        maps.append(m)
    return maps


def gather(results, NPS, SEQ, TS):
    cat = lambda k: np.concatenate([r[k] for r in results], axis=0)
    stk = lambda k: np.stack([r[k] for r in results], axis=0)
    nb = len(results) * NPS
    y_p = cat("y_p")
    y_s = stk("y_s")
    k_p = cat("k_p").reshape(1, nb, SEQ, H, DH)
    v_p = cat("v_p").reshape(1, nb, SEQ, H, DH)
    lf_p = cat("lf_p").reshape(1, nb, SEQ, H)
    cv_p = cat("cv_p").reshape(1, nb, CK - 1, 512)
    ff_p = cat("ff_p").reshape(1, nb, 2, DFF)
    ns = len(results)
    k_s = stk("k_s").reshape(1, ns, TS, H, DH)
    v_s = stk("v_s").reshape(1, ns, TS, H, DH)
    lf_s = stk("lf_s").reshape(1, ns, TS, H)
    cv_s = stk("cv_s").reshape(1, ns, CK - 1, 512)
    ff_s = stk("ff_s").reshape(1, ns, 2, DFF)
    return tuple(np.ascontiguousarray(a, dtype=np.float32) for a in
                 (y_p, y_s, k_p, v_p, lf_p, cv_p, ff_p, k_s, v_s, lf_s, cv_s, ff_s))


def kernel(**inputs):
    inputs = {k: np.asarray(v) for k, v in inputs.items()}
    B, SEQ, _ = inputs["x_prompt"].shape
    TS = inputs["x_sample"].shape[1]
    PAST = inputs["cache_k"].shape[2]
    NPS = B // N_CORES
    nc = build_nc(NPS, SEQ, TS, PAST, 256)
    in_maps = make_in_maps(inputs, NPS, N_CORES)
    res = run_bass_kernel_spmd(nc, in_maps, core_ids=list(range(N_CORES)))
    return gather(res.results, NPS, SEQ, TS)
```
